# Optimizing a Trainium2 kernel written in Bass

```python
import jax, jax.numpy as jnp
from jax import lax
import numpy as np

D_MODEL = 1024
BATCH = 8
SEQ = 2048
DEPTH = 2
DEC_BATCH = 128
DEC_SEQ = 4
PAST_LEN = 2048
PAGE_SIZE = 128

RET_HEADS = 4
RET_DK = 128
RET_DV = 128
RET_QK = RET_HEADS * RET_DK
RET_V = RET_HEADS * RET_DV
RET_CHUNK = 128
ROPE_BASE = 10000.0
SB_HEADS = 4
SB_DH = 128
SB_W = SB_HEADS * SB_DH
SB_BLOCK = 128
SB_BIAS_INIT = -8.0
LRU_WIDTH = 512
LRU_BLOCKS = 4
LRU_BW = LRU_WIDTH // LRU_BLOCKS
LRU_CONV = 4
LRU_C = 8.0
N_BRANCH = 3
D_FF = 2816
FFN_CONV = 3
EPS = 1e-6

IN_SIZES = (RET_QK, RET_QK, RET_V, RET_V, SB_W, SB_W, SB_W, LRU_WIDTH, N_BRANCH * D_MODEL)
D_IN = int(sum(IN_SIZES))
IN_OFFSETS = tuple(int(o) for o in np.cumsum(IN_SIZES)[:-1])

kernel_name = "hybrid_ret_sb_rglru_step"

F32 = jnp.float32


def rmsnorm(x, g):
    xf = x.astype(F32)
    y = xf * lax.rsqrt(jnp.mean(xf * xf, axis=-1, keepdims=True) + EPS)
    return (y * g.astype(F32)).astype(x.dtype)


def rotary(x, pos):
    half = x.shape[-1] // 2
    inv = ROPE_BASE ** (-jnp.arange(half, dtype=F32) / half)
    ang = pos[:, None] * inv[None, :]
    cos = jnp.cos(ang)[None, :, None, :]
    sin = jnp.sin(ang)[None, :, None, :]
    x1, x2 = x[..., :half], x[..., half:]
    return jnp.concatenate([x1 * cos - x2 * sin, x2 * cos + x1 * sin], axis=-1)


def retention_log_decay():
    return jnp.log(1.0 - 2.0 ** (-5.0 - jnp.arange(RET_HEADS, dtype=F32)))


def retention_chunk(q, k, v, S, lg):
    L = q.shape[1]
    idx = jnp.arange(L, dtype=F32)
    diff = idx[:, None] - idx[None, :]
    decay = jnp.where(diff >= 0, jnp.exp(jnp.maximum(diff, 0.0)[None] * lg[:, None, None]), 0.0)
    scores = jnp.einsum('bihd,bjhd->bhij', q, k) * decay[None]
    intra = jnp.einsum('bhij,bjhe->bihe', scores, v)
    q_dec = jnp.exp((idx + 1.0)[:, None] * lg[None, :])
    cross = jnp.einsum('bihd,bhde->bihe', q, S) * q_dec[None, :, :, None]
    k_dec = jnp.exp((L - 1.0 - idx)[:, None] * lg[None, :])
    S_new = jnp.exp(L * lg)[None, :, None, None] * S + jnp.einsum('bjhd,bjhe->bhde', k * k_dec[None, :, :, None], v)
    return intra + cross, S_new


def retention_prompt(q, k, v, lg):
    B, T, H, dk = q.shape
    dv = v.shape[-1]
    nc = T // RET_CHUNK

    def to_chunks(a):
        return a.reshape(B, nc, RET_CHUNK, H, a.shape[-1]).transpose(1, 0, 2, 3, 4)

    def step(S, qkv):
        qc, kc, vc = qkv
        o, S = retention_chunk(qc, kc, vc, S, lg)
        return S, o

    S0 = jnp.zeros((B, H, dk, dv), F32)
    S, o = lax.scan(step, S0, (to_chunks(q), to_chunks(k), to_chunks(v)))
    return o.transpose(1, 0, 2, 3, 4).reshape(B, T, H, dv), S


def stick_breaking(q, k, v, t_idx, bias):
    z = (jnp.einsum('bthd,bshd->bhts', q.astype(F32), k.astype(F32)) * (SB_DH ** -0.5)
         + bias.astype(F32)[None, :, None, None])
    s_idx = jnp.arange(k.shape[1])
    mask = (s_idx[None, :] < t_idx[:, None])[None, None]
    log_keep = jnp.where(mask, jax.nn.log_sigmoid(-z), 0.0)
    later = lax.cumsum(log_keep, axis=3, reverse=True) - log_keep
    A = jnp.where(mask, jnp.exp(jax.nn.log_sigmoid(z) + later), 0.0)
    return jnp.einsum('bhts,bshd->bthd', A, v.astype(F32))


def causal_dwconv(x, buf, w, b):
    W = w.shape[0]
    T = x.shape[1]
    xp = jnp.concatenate([buf.astype(x.dtype), x], axis=1)
    out = b.astype(x.dtype) + xp[:, 0:T] * w[0]
    for i in range(1, W):
        out = out + xp[:, i:i + T] * w[i]
    return out, xp[:, -(W - 1):]


def rg_lru(x, h0, w_a, b_a, w_x, b_x, lam):
    B, T, _ = x.shape
    xf = x.astype(F32)
    xb = xf.reshape(B, T, LRU_BLOCKS, LRU_BW)
    r = jax.nn.sigmoid(jnp.einsum('btni,nij->btnj', xb, w_a.astype(F32)).reshape(B, T, LRU_WIDTH) + b_a.astype(F32))
    i = jax.nn.sigmoid(jnp.einsum('btni,nij->btnj', xb, w_x.astype(F32)).reshape(B, T, LRU_WIDTH) + b_x.astype(F32))
    log_a = LRU_C * r * jax.nn.log_sigmoid(lam.astype(F32))
    a = jnp.exp(log_a)
    bterm = jnp.sqrt(-jnp.expm1(2.0 * log_a)) * (i * xf)
    bterm = bterm.at[:, 0].add(a[:, 0] * h0.astype(F32))

    def combine(e1, e2):
        a1, b1 = e1
        a2, b2 = e2
        return a1 * a2, a2 * b1 + b2

    _, h = lax.associative_scan(combine, (a, bterm), axis=1)
    return h, h[:, -1]


def hybrid_layer(x, pos, lw, ret_S, lru_h, lru_buf, ffn_buf, past_k, past_v):
    B, T, _ = x.shape
    dt = x.dtype
    xn = rmsnorm(x, lw["norm1"])
    proj = xn @ lw["w_in"]
    rq, rk, rv, rg, sq, sk, sv, lx, gl = jnp.split(proj, IN_OFFSETS, axis=-1)

    lg = retention_log_decay()
    q = rotary(rq.astype(F32).reshape(B, T, RET_HEADS, RET_DK), pos)
    k = rotary(rk.astype(F32).reshape(B, T, RET_HEADS, RET_DK), pos) * (RET_DK ** -0.5)
    v = rv.astype(F32).reshape(B, T, RET_HEADS, RET_DV)
    if ret_S is None:
        o, S_new = retention_prompt(q, k, v, lg)
    else:
        o, S_new = retention_chunk(q, k, v, ret_S.astype(F32), lg)
    mu = jnp.mean(o, axis=-1, keepdims=True)
    var = jnp.mean(jnp.square(o - mu), axis=-1, keepdims=True)
    o = (o - mu) * lax.rsqrt(var + EPS)
    o_ret = (o.reshape(B, T, RET_V) * lw["ret_gn"].astype(F32) * jax.nn.silu(rg.astype(F32))).astype(dt)

    q_sb = sq.reshape(B, T, SB_HEADS, SB_DH)
    k_sb = sk.reshape(B, T, SB_HEADS, SB_DH)
    v_sb = sv.reshape(B, T, SB_HEADS, SB_DH)
    if past_k is None:
        blocks = []
        for bi in range(T // SB_BLOCK):
            lo, hi = bi * SB_BLOCK, (bi + 1) * SB_BLOCK
            blocks.append(stick_breaking(q_sb[:, lo:hi], k_sb[:, :hi], v_sb[:, :hi], jnp.arange(lo, hi), lw["sb_bias"]))
        o_sb = jnp.concatenate(blocks, axis=1)
    else:
        P = past_k.shape[1]
        k_all = jnp.concatenate([past_k.astype(dt), k_sb], axis=1)
        v_all = jnp.concatenate([past_v.astype(dt), v_sb], axis=1)
        o_sb = stick_breaking(q_sb, k_all, v_all, P + jnp.arange(T), lw["sb_bias"])
    o_sb = o_sb.reshape(B, T, SB_W).astype(dt)

    if lru_buf is None:
        lru_buf = jnp.zeros((B, LRU_CONV - 1, LRU_WIDTH), dt)
        lru_h = jnp.zeros((B, LRU_WIDTH), F32)
    xc, lru_buf_new = causal_dwconv(lx, lru_buf, lw["lru_conv_w"], lw["lru_conv_b"])
    h, h_last = rg_lru(xc, lru_h, lw["lru_w_a"], lw["lru_b_a"], lw["lru_w_x"], lw["lru_b_x"], lw["lru_lambda"])
    o_lru = h.astype(dt)

    g_all = jax.nn.sigmoid(gl.astype(F32))
    g_ret, g_sb, g_lru = jnp.split(g_all, N_BRANCH, axis=-1)
    merged = (g_ret * (o_ret @ lw["w_br_ret"]).astype(F32)
              + g_sb * (o_sb @ lw["w_br_sb"]).astype(F32)
              + g_lru * (o_lru @ lw["w_br_lru"]).astype(F32))
    x = x + merged.astype(dt) @ lw["w_out"]

    xn2 = rmsnorm(x, lw["norm2"])
    g = xn2 @ lw["w_ffn_gate"]
    u = xn2 @ lw["w_ffn_up"]
    if ffn_buf is None:
        ffn_buf = jnp.zeros((B, FFN_CONV - 1, D_FF), dt)
    gc, ffn_buf_new = causal_dwconv(g, ffn_buf, lw["ffn_conv_w"], lw["ffn_conv_b"])
    x = x + (jax.nn.gelu(gc) * u) @ lw["w_ffn_down"]
    return x, (k_sb, v_sb, S_new, h_last, lru_buf_new, ffn_buf_new)


def setup_inputs(seed: int = 0) -> dict:
    key = jax.random.key(seed)
    ks = iter(jax.random.split(key, 40))
    n_pages = PAST_LEN // PAGE_SIZE
    n_used = DEC_BATCH * n_pages
    n_phys = (n_used * 5) // 4

    def nrm(shape, scale):
        return jax.random.normal(next(ks), shape, F32) * scale

    x_prompt = nrm((BATCH, SEQ, D_MODEL), 1.0)
    x_sample = nrm((DEC_BATCH, DEC_SEQ, D_MODEL), 1.0)
    cache_sb_k = nrm((DEPTH, n_phys, PAGE_SIZE, SB_HEADS, SB_DH), 1.0)
    cache_sb_v = nrm((DEPTH, n_phys, PAGE_SIZE, SB_HEADS, SB_DH), 1.0)
    state_ret = nrm((DEPTH, DEC_BATCH, RET_HEADS, RET_DK, RET_DV), 0.1)
    state_lru_h = nrm((DEPTH, DEC_BATCH, LRU_WIDTH), 0.5)
    state_lru_conv = nrm((DEPTH, DEC_BATCH, LRU_CONV - 1, LRU_WIDTH), 1.0)
    state_ffn_conv = nrm((DEPTH, DEC_BATCH, FFN_CONV - 1, D_FF), 1.0)
    page_table = jax.random.permutation(next(ks), n_phys)[:n_used].reshape(DEC_BATCH, n_pages).astype(jnp.int32)

    norm1 = 1.0 + nrm((DEPTH, D_MODEL), 0.01)
    w_in = nrm((DEPTH, D_MODEL, D_IN), D_MODEL ** -0.5)
    ret_gn = 1.0 + nrm((DEPTH, RET_V), 0.01)
    sb_bias = SB_BIAS_INIT + nrm((DEPTH, SB_HEADS), 0.1)
    lru_conv_w = nrm((DEPTH, LRU_CONV, LRU_WIDTH), LRU_CONV ** -0.5)
    lru_conv_b = nrm((DEPTH, LRU_WIDTH), 0.01)
    lru_w_a = nrm((DEPTH, LRU_BLOCKS, LRU_BW, LRU_BW), LRU_BW ** -0.5)
    lru_b_a = nrm((DEPTH, LRU_WIDTH), 0.01)
    lru_w_x = nrm((DEPTH, LRU_BLOCKS, LRU_BW, LRU_BW), LRU_BW ** -0.5)
    lru_b_x = nrm((DEPTH, LRU_WIDTH), 0.01)
    a_target = jax.random.uniform(next(ks), (DEPTH, LRU_WIDTH), F32, 0.9, 0.999) ** (1.0 / LRU_C)
    lru_lambda = jnp.log(a_target) - jnp.log1p(-a_target)
    w_br_ret = nrm((DEPTH, RET_V, D_MODEL), RET_V ** -0.5)
    w_br_sb = nrm((DEPTH, SB_W, D_MODEL), SB_W ** -0.5)
    w_br_lru = nrm((DEPTH, LRU_WIDTH, D_MODEL), LRU_WIDTH ** -0.5)
    w_out = nrm((DEPTH, D_MODEL, D_MODEL), D_MODEL ** -0.5)
    norm2 = 1.0 + nrm((DEPTH, D_MODEL), 0.01)
    w_ffn_gate = nrm((DEPTH, D_MODEL, D_FF), D_MODEL ** -0.5)
    w_ffn_up = nrm((DEPTH, D_MODEL, D_FF), D_MODEL ** -0.5)
    ffn_conv_w = nrm((DEPTH, FFN_CONV, D_FF), FFN_CONV ** -0.5)
    ffn_conv_b = nrm((DEPTH, D_FF), 0.01)
    w_ffn_down = nrm((DEPTH, D_FF, D_MODEL), D_FF ** -0.5)
    norm_f = 1.0 + nrm((D_MODEL,), 0.01)
    return {
        "x_prompt": x_prompt, "x_sample": x_sample,
        "cache_sb_k": cache_sb_k, "cache_sb_v": cache_sb_v,
        "state_ret": state_ret, "state_lru_h": state_lru_h,
        "state_lru_conv": state_lru_conv, "state_ffn_conv": state_ffn_conv,
        "page_table": page_table,
        "norm1": norm1, "w_in": w_in, "ret_gn": ret_gn, "sb_bias": sb_bias,
        "lru_conv_w": lru_conv_w, "lru_conv_b": lru_conv_b,
        "lru_w_a": lru_w_a, "lru_b_a": lru_b_a, "lru_w_x": lru_w_x, "lru_b_x": lru_b_x,
        "lru_lambda": lru_lambda,
        "w_br_ret": w_br_ret, "w_br_sb": w_br_sb, "w_br_lru": w_br_lru, "w_out": w_out,
        "norm2": norm2, "w_ffn_gate": w_ffn_gate, "w_ffn_up": w_ffn_up,
        "ffn_conv_w": ffn_conv_w, "ffn_conv_b": ffn_conv_b, "w_ffn_down": w_ffn_down,
        "norm_f": norm_f,
    }


def reference(x_prompt, x_sample, cache_sb_k, cache_sb_v, state_ret, state_lru_h, state_lru_conv,
              state_ffn_conv, page_table, norm1, w_in, ret_gn, sb_bias, lru_conv_w, lru_conv_b, lru_w_a, lru_b_a,
              lru_w_x, lru_b_x, lru_lambda, w_br_ret, w_br_sb, w_br_lru, w_out, norm2, w_ffn_gate,
              w_ffn_up, ffn_conv_w, ffn_conv_b, w_ffn_down, norm_f):
    def layer_weights(l):
        return {
            "norm1": norm1[l], "w_in": w_in[l], "ret_gn": ret_gn[l], "sb_bias": sb_bias[l],
            "lru_conv_w": lru_conv_w[l], "lru_conv_b": lru_conv_b[l],
            "lru_w_a": lru_w_a[l], "lru_b_a": lru_b_a[l], "lru_w_x": lru_w_x[l], "lru_b_x": lru_b_x[l],
            "lru_lambda": lru_lambda[l],
            "w_br_ret": w_br_ret[l], "w_br_sb": w_br_sb[l], "w_br_lru": w_br_lru[l], "w_out": w_out[l],
            "norm2": norm2[l], "w_ffn_gate": w_ffn_gate[l], "w_ffn_up": w_ffn_up[l],
            "ffn_conv_w": ffn_conv_w[l], "ffn_conv_b": ffn_conv_b[l], "w_ffn_down": w_ffn_down[l],
        }

    T_p = x_prompt.shape[1]
    pos_p = jnp.arange(T_p, dtype=F32)
    xp = x_prompt
    st_p = []
    for l in range(DEPTH):
        xp, st = hybrid_layer(xp, pos_p, layer_weights(l), None, None, None, None, None, None)
        st_p.append(st)
    y_prompt = rmsnorm(xp, norm_f)

    DB, T_s, _ = x_sample.shape
    past_len = page_table.shape[1] * cache_sb_k.shape[2]
    pos_s = past_len + jnp.arange(T_s, dtype=F32)
    xs = x_sample
    st_s = []
    for l in range(DEPTH):
        past_k = cache_sb_k[l][page_table].reshape(DB, past_len, SB_HEADS, SB_DH)
        past_v = cache_sb_v[l][page_table].reshape(DB, past_len, SB_HEADS, SB_DH)
        xs, st = hybrid_layer(xs, pos_s, layer_weights(l), state_ret[l], state_lru_h[l], state_lru_conv[l],
                              state_ffn_conv[l], past_k, past_v)
        st_s.append(st)
    y_sample = rmsnorm(xs, norm_f)

    def stack(sts, j):
        return jnp.stack([s[j] for s in sts], axis=0)

    return (y_prompt, y_sample,
            stack(st_p, 0), stack(st_p, 1), stack(st_s, 0), stack(st_s, 1),
            stack(st_p, 2), stack(st_s, 2),
            stack(st_p, 3), stack(st_s, 3),
            stack(st_p, 4), stack(st_s, 4),
            stack(st_p, 5), stack(st_s, 5))
```

```python
import numpy as np
from contextlib import ExitStack
import concourse.bass as bass
import concourse.mybir as mybir
from concourse.bass_utils import run_bass_kernel_spmd

F32 = mybir.dt.float32
BF16 = mybir.dt.bfloat16
I32 = mybir.dt.int32
AF = mybir.ActivationFunctionType
ALU = mybir.AluOpType
AX = mybir.AxisListType

NCORES = 8
L = 2
D = 1024
NT = 2176
NTT = 17
SS = 2048
BLKS = [(0, 512), (512, 512), (1024, 512), (1536, 512), (2048, 128)]
DFF = 2816
NG = 11
EPS = 1e-6
GAM = [1.0 - 2.0 ** (-5.0 - h) for h in range(4)]
import os as _os
SAME_ENGINE_SYNC = _os.environ.get("SES", "1") == "1"


class T:
    __slots__ = ("name", "w", "r", "excl")

    def __init__(self, name, excl=False):
        self.name = name
        self.w = None
        self.r = []
        self.excl = excl


class Sched:
    ENGS = ("sync", "scalar", "vector", "gpsimd", "tensor")

    def __init__(self, nc, stack, n_dma_sems=8):
        self.nc = nc
        self.recs = []
        self.sem = {e: stack.enter_context(nc.semaphore("s_" + e)) for e in self.ENGS}
        self.nds = n_dma_sems
        self.dsem = {e: [stack.enter_context(nc.semaphore("d_%s%d" % (e, i))) for i in range(n_dma_sems)]
                     for e in ("sync", "gpsimd")}
        self.ndma = {"sync": 0, "gpsimd": 0}
        self.dma_hist = {"sync": [], "gpsimd": []}
        self.last = {e: None for e in self.ENGS}

    def op(self, eng, fn, reads=(), writes=(), dma=False, extra=()):
        idx = len(self.recs)
        waits = set(extra)
        for t in reads:
            if t.w is not None:
                waits.add(t.w)
            if t.excl:
                waits.update(t.r)
        for t in writes:
            if t.w is not None:
                waits.add(t.w)
            waits.update(t.r)
        rec = dict(eng=eng, fn=fn, waits=waits, dma=dma, sig=False, dq=None)
        if dma:
            q = self.ndma[eng]
            self.ndma[eng] += 1
            rec["dq"] = q
            h = self.dma_hist[eng]
            if q >= self.nds:
                waits.add(h[q - self.nds])
            h.append(idx)
        self.recs.append(rec)
        for t in reads:
            t.r.append(idx)
        for t in writes:
            t.w = idx
            t.r = []
        if fn is not None:
            self.last[eng] = idx
        return idx

    def barrier(self):
        pend = set()
        for e in self.ENGS:
            if self.last[e] is not None:
                pend.add(self.last[e])
        for e in ("sync", "gpsimd"):
            pend.update(self.dma_hist[e][-self.nds:])
        for e in self.ENGS:
            self.op(e, None, extra=pend)

    def emit(self):
        nc = self.nc
        recs = self.recs
        for r in recs:
            keep = set()
            for w in r["waits"]:
                rw = recs[w]
                if rw["fn"] is None:
                    continue
                if rw["eng"] == r["eng"] and not rw["dma"]:
                    if r["eng"] in ("tensor", "sync"):
                        continue
                    if not SAME_ENGINE_SYNC:
                        continue
                keep.add(w)
            r["waits"] = keep
            for w in keep:
                recs[w]["sig"] = True
        cnt = {e: 0 for e in self.ENGS}
        for r in recs:
            if r["dma"]:
                q = r["dq"]
                r["sem"] = self.dsem[r["eng"]][q % self.nds]
                r["val"] = 16 * (q // self.nds + 1)
            elif r["sig"]:
                cnt[r["eng"]] += 1
                r["sem"] = self.sem[r["eng"]]
                r["val"] = cnt[r["eng"]]
        per = {e: [r for r in recs if r["eng"] == e] for e in self.ENGS}
        print('SEMCOUNTS', cnt, dict(self.ndma), {e: len(per[e]) for e in self.ENGS})

        def run(e, h):
            seen = {}
            for r in per[e]:
                for w in sorted(r["waits"]):
                    rw = recs[w]
                    key = id(rw["sem"])
                    if seen.get(key, 0) >= rw["val"]:
                        continue
                    seen[key] = rw["val"]
                    h.wait_ge(rw["sem"], rw["val"])
                if r["fn"] is None:
                    continue
                ins = r["fn"](h)
                if r["dma"] or r["sig"]:
                    ins.then_inc(r["sem"], 16 if r["dma"] else 1)
            if e in self.ndma:
                n = self.ndma[e]
                for s in range(min(n, self.nds)):
                    last_q = ((n - 1 - s) // self.nds) * self.nds + s
                    h.wait_ge(self.dsem[e][s], 16 * (last_q // self.nds + 1))

        with nc.Block() as block:
            @block.sync
            def _(h):
                run("sync", h)

            @block.scalar
            def _(h):
                run("scalar", h)

            @block.vector
            def _(h):
                run("vector", h)

            @block.gpsimd
            def _(h):
                run("gpsimd", h)

            @block.tensor
            def _(h):
                run("tensor", h)


def build_nc(n_layers=L, do_sample_sb=True, stop=None, big_cache=True):
    nc = bass.Bass("TRN2", target_bir_lowering=False)
    gstack = ExitStack()
    with gstack:
        S = Sched(nc, gstack)

        def din(name, shape, dt=F32):
            return nc.dram_tensor(name, list(shape), dt, kind="ExternalInput").ap()

        def dout(name, shape):
            return nc.dram_tensor(name, list(shape), F32, kind="ExternalOutput").ap()

        xin = din("xin", [NT, D])
        ck = din("ck", [L * 2560 * 128 if big_cache else 128, 512])
        cv = din("cv", [L * 2560 * 128 if big_cache else 128, 512])
        ptd = din("pt", [1, 256], I32)
        sret = din("sret", [L, 16, 4, 128, 128])
        slh = din("slh", [L, 16, 512])
        slc = din("slc", [L, 48, 512])
        sfc = din("sfc", [L, 32, DFF])
        w_in = din("w_in", [L, 16, 128, 8, 512])
        w_br = din("w_br", [L, 3, 128, 4, 1024])
        w_out = din("w_out", [L, 2, 128, 8, 512])
        w_gate = din("w_gate", [L, NG, 128, 8, 256])
        w_up = din("w_up", [L, NG, 128, 8, 256])
        w_down = din("w_down", [L, NG, 128, 2, 1024])
        w_lru = din("w_lru", [L, 128, 8, 128])
        v_norm = din("v_norm", [2 * L + 1, 1, D])
        v_gn = din("v_gn", [L, 1, 512])
        v_sbb = din("v_sbb", [L, 1, 16])
        v_sbb16 = din("v_sbb16", [L, 16, 1])
        v_lru = din("v_lru", [L, 128, 4, 8])
        v_ffn = din("v_ffn", [L, 128, 22, 4])
        c_cos = din("c_cos", [128, NT])
        c_sin = din("c_sin", [128, NT])
        c_dec = din("c_dec", [128, 8, 128])
        c_qdec = din("c_qdec", [128, 8, 128])
        c_kdec = din("c_kdec", [128, 8])
        c_mneg = din("c_mneg", [128, 128])
        c_bmask = din("c_bmask", [128, 16, 128])
        c_rmask = din("c_rmask", [128, 16])
        c_iota = din("c_iota", [128, 1])

        y = dout("y", [NT, D])
        sbk = dout("sbk", [L, NT, 512])
        sbv = dout("sbv", [L, NT, 512])
        rets = dout("rets", [L, 17, 4, 128, 128])
        lruh = dout("lruh", [L, 17, 512])
        lruc = dout("lruc", [L, 17, 3, 512])
        ffnc = dout("ffnc", [L, 17, 2, DFF])
        xA = [nc.dram_tensor("xA%d" % l, [NT, D], F32, kind="Internal").ap() for l in range(L)]
        xB = [nc.dram_tensor("xB%d" % l, [NT, D], F32, kind="Internal").ap() for l in range(L)]
        xAT = [[T("xA") for _ in range(NTT)] for l in range(L)]
        xBT = [[T("xB") for _ in range(NTT)] for l in range(L)]

        uid = [0]

        def sb(st, name, shape, dt):
            uid[0] += 1
            return st.enter_context(nc.sbuf_tensor("%s_%d" % (name, uid[0]), list(shape), dt))

        def OP(eng, name, *args, r=(), w=(), **kw):
            S.op(eng, lambda h: getattr(h, name)(*args, **kw), reads=r, writes=w)

        def DMA(eng, out, in_, r=(), w=(), **kw):
            S.op(eng, lambda h: h.dma_start(out=out, in_=in_, **kw), reads=r, writes=w, dma=True)

        def MM(out, lhsT, rhs, start, stop, r=(), w=()):
            S.op("tensor", lambda h: h.matmul(out, lhsT=lhsT, rhs=rhs, start=start, stop=stop), reads=r, writes=w)

        def TR(out, in_, ident, r=(), w=()):
            S.op("tensor", lambda h: h.transpose(out, in_, ident), reads=r, writes=w)

        def ACT(out, in_, func, r=(), w=(), **kw):
            S.op("scalar", lambda h: h.activation(out=out, in_=in_, func=func, **kw), reads=r, writes=w)

        pz = gstack.enter_context(nc.psum_tensor("pz", [128, 2560], F32))
        pbs = [gstack.enter_context(nc.psum_tensor("pb%d" % i, [128, 512], F32)) for i in range(3)]
        PB = [pz[:, k * 512:(k + 1) * 512] for k in range(5)] + [p[:, :] for p in pbs]
        PT = [T("pbank%d" % i, excl=True) for i in range(8)]
        pzT = T("pz")

        NRING = 3
        ring = [sb(gstack, "wr%d" % i, [128, 4096], BF16) for i in range(NRING)]
        ringT = [T("wr%d" % i) for i in range(NRING)]
        rstate = {"i": 0}

        def wload(src, kc, ncol):
            i = rstate["i"] % NRING
            rstate["i"] += 1
            view = ring[i][:, 0:kc * ncol].rearrange("p (k n) -> p k n", n=ncol)
            DMA("gpsimd", view, src, w=[ringT[i]])
            return view, ringT[i]

        FB = sb(gstack, "FB", [128, 8, NT], BF16)
        FBT = T("FB")
        xio = [sb(gstack, "xio%d" % i, [128, D], F32) for i in range(2)]
        xioT = [T("xio%d" % i) for i in range(2)]
        xnb = [sb(gstack, "xnb%d" % i, [128, D], BF16) for i in range(2)]
        xnbT = [T("xnb%d" % i) for i in range(2)]
        gB = sb(gstack, "gB", [128, D], F32)
        gBT = T("gB")
        identb = sb(gstack, "identb", [128, 128], BF16)
        identf = sb(gstack, "identf", [128, 128], F32)
        idT = T("ident")
        mneg = sb(gstack, "mneg", [128, 128], BF16)
        mnegf = sb(gstack, "mnegf", [128, 128], F32)
        ones = sb(gstack, "ones", [128, 1], F32)
        cT = T("consts")
        st4 = sb(gstack, "st4", [128, 16], F32)
        st4T = T("st4")

        OP("gpsimd", "memset", identf[:], 1.0, w=[idT])
        OP("gpsimd", "affine_select", out=identf[:], in_=identf[:], pattern=[[-1, 128]], compare_op=ALU.is_equal,
           fill=0.0, base=0, channel_multiplier=1, r=[idT], w=[idT])
        OP("vector", "tensor_copy", identb[:], identf[:], r=[idT], w=[idT])
        DMA("sync", mnegf[:], c_mneg, w=[cT])
        OP("vector", "tensor_copy", mneg[:], mnegf[:], r=[cT], w=[cT])
        OP("vector", "memset", ones[:], 1.0, w=[cT])

        def rmsnorm_stats(xt, xtT, ncols):
            junk = xnb[0]
            ACT(sqs[:, 0:ncols], xt, AF.Square, accum_out=st4[:, 0:1], r=[xtT], w=[sqsT, st4T])
            ACT(st4[:, 1:2], st4[:, 0:1], AF.Sqrt, bias=eps_t[:, 0:1], scale=1.0 / ncols, r=[st4T, cT], w=[st4T])
            OP("vector", "reciprocal", st4[:, 2:3], st4[:, 1:2], r=[st4T], w=[st4T])

        sqs = sb(gstack, "sqs", [128, D], BF16)
        sqsT = T("sqs")
        eps_t = sb(gstack, "eps_t", [128, 1], F32)
        OP("vector", "memset", eps_t[:], EPS, w=[cT])

        def norm_to_FB(src_tile_fn, l_norm_idx):
            DMA("sync", gB[:], v_norm[l_norm_idx].partition_broadcast(128), w=[gBT])
            for tt in range(NTT):
                xt, xtT = src_tile_fn(tt)
                rmsnorm_stats(xt, xtT, D)
                nb, nbT = xnb[tt % 2], xnbT[tt % 2]
                OP("vector", "scalar_tensor_tensor", out=nb[:], in0=xt, scalar=st4[:, 2:3], in1=gB[:],
                   op0=ALU.mult, op1=ALU.mult, r=[xtT, st4T, gBT], w=[nbT])
                bank = 5 + (tt % 2)
                pv = PB[bank].bitcast(BF16)
                for kc in range(8):
                    TR(pv[:, kc * 128:(kc + 1) * 128], nb[:, kc * 128:(kc + 1) * 128], identb[:], r=[nbT, idT], w=[PT[bank]])
                ACT(FB[:, :, tt * 128:(tt + 1) * 128], pv.rearrange("p (k n) -> p k n", n=128), AF.Copy,
                    r=[PT[bank]], w=[FBT])

        ptf = sb(gstack, "ptf", [128, 256], F32)
        idx16 = [sb(gstack, "idx16_%d" % i, [128, 16], I32) for i in range(2)]
        idx16T = [T("idx16") for i in range(2)]
        ptT = T("pt")
        with ExitStack() as tst:
            ptb = sb(tst, "ptb", [128, 256], I32)
            iof = sb(tst, "iof", [128, 1], F32)
            DMA("sync", ptb[:], ptd.partition_broadcast(128), w=[ptT])
            DMA("sync", iof[:], c_iota, w=[ptT])
            OP("vector", "tensor_copy", ptf[:], ptb[:], r=[ptT], w=[ptT])
            OP("vector", "scalar_tensor_tensor", out=ptf[:], in0=ptf[:], scalar=128.0, in1=iof[:].to_broadcast([128, 256]),
               op0=ALU.mult, op1=ALU.add, r=[ptT], w=[ptT])
            S.barrier()

        for l in range(n_layers):
            mst = ExitStack()
            with mst:
                xsrc = xin if l == 0 else xB[l - 1]
                xsrcT = None if l == 0 else xBT[l - 1]
                FA = sb(mst, "FA", [128, 8, NT], BF16)
                FAT = T("FA")
                OT = sb(mst, "OT", [128, 4, NT], BF16)
                OTT = T("OT")

                def src1(tt, xsrc=xsrc, xsrcT=xsrcT):
                    xt, xtT = xio[tt % 2], xioT[tt % 2]
                    DMA("sync", xt[:], xsrc[tt * 128:(tt + 1) * 128, :], r=([xsrcT[tt]] if xsrcT else []), w=[xtT])
                    return xt[:], xtT
                DMA("sync", gB[:], v_norm[2 * l].partition_broadcast(128), w=[gBT])
                for tt in range(NTT):
                    xt, xtT = src1(tt)
                    rmsnorm_stats(xt, xtT, D)
                    nb, nbT = xnb[tt % 2], xnbT[tt % 2]
                    OP("vector", "scalar_tensor_tensor", out=nb[:], in0=xt, scalar=st4[:, 2:3], in1=gB[:],
                       op0=ALU.mult, op1=ALU.mult, r=[xtT, st4T, gBT], w=[nbT])
                    bank = 5 + (tt % 2)
                    pv = PB[bank].bitcast(BF16)
                    for kc in range(8):
                        TR(pv[:, kc * 128:(kc + 1) * 128], nb[:, kc * 128:(kc + 1) * 128], identb[:], r=[nbT, idT], w=[PT[bank]])
                    ACT(FA[:, :, tt * 128:(tt + 1) * 128], pv.rearrange("p (k n) -> p k n", n=128), AF.Copy,
                        r=[PT[bank]], w=[FAT])

                def proj_fm(wv, wT, c0, blk, bank, src=FA, srcT=FAT, kcs=8):
                    b0, n = blk
                    for kc in range(kcs):
                        MM(PB[bank][:, 0:n], wv[:, kc, c0:c0 + 128], src[:, kc, b0:b0 + n], kc == 0, kc == kcs - 1,
                           r=[wT, srcT], w=[PT[bank]])

                def proj_tm(wv, wT, ncol, tt, bank, src=FA, srcT=FAT, kcs=8, c0=0):
                    for kc in range(kcs):
                        MM(PB[bank][:, 0:ncol], src[:, kc, tt * 128:(tt + 1) * 128], wv[:, kc, c0:c0 + ncol], kc == 0, kc == kcs - 1,
                           r=[wT, srcT], w=[PT[bank]])

                first_merge = [True]

                def merge(br):
                    wb, wbT = wload(w_br[l, br], 4, 1024)
                    for half in range(2):
                        wg, wgT = wload(w_in[l, 8 + 2 * br + half], 8, 512)
                        for jj in range(4):
                            j = half * 4 + jj
                            for bi, blk in enumerate(BLKS):
                                b0, n = blk
                                b1, b2 = (0, 1) if (bi % 2 == 0) else (2, 3)
                                proj_fm(wb, wbT, j * 128, blk, b1, src=OT, srcT=OTT, kcs=4)
                                proj_fm(wg, wgT, jj * 128, blk, b2)
                                gt, gtT = mtmp[bi % 2], mtmpT[bi % 2]
                                ACT(gt[:, 0:n], PB[b2][:, 0:n], AF.Sigmoid, r=[PT[b2]], w=[gtT])
                                if first_merge[0]:
                                    OP("vector", "tensor_tensor", FB[:, j, b0:b0 + n], gt[:, 0:n], PB[b1][:, 0:n], op=ALU.mult,
                                       r=[gtT, PT[b1]], w=[FBT])
                                else:
                                    OP("vector", "tensor_tensor", gt[:, 0:n], gt[:, 0:n], PB[b1][:, 0:n], op=ALU.mult,
                                       r=[gtT, PT[b1]], w=[gtT])
                                    OP("vector", "tensor_tensor", FB[:, j, b0:b0 + n], FB[:, j, b0:b0 + n], gt[:, 0:n], op=ALU.add,
                                       r=[gtT, FBT], w=[FBT])
                    first_merge[0] = False

                mtmp = [sb(mst, "mtmp%d" % i, [128, 512], F32) for i in range(2)]
                mtmpT = [T("mtmp%d" % i) for i in range(2)]

                if stop == "norm1":
                    S.barrier(); S.emit(); return nc
                bst = ExitStack()
                with bst:
                    EXT = sb(bst, "EXT", [128, 3 + 2048], F32); EXTT = T("EXT")
                    EXS = sb(bst, "EXS", [128, 16, 7], F32); EXST = T("EXS")
                    XC = sb(bst, "XC", [128, NT], F32); XCT = T("XC")
                    XCb = sb(bst, "XCb", [128, NT], BF16); XCbT = T("XCb")
                    RA = sb(bst, "RA", [128, NT], F32); RAT = T("RA")
                    IB = sb(bst, "IB", [128, NT], F32); IBT = T("IB")
                    A2 = sb(bst, "A2", [128, NT], F32); A2T = T("A2")
                    slc_tm = sb(bst, "slc_tm", [48, 512], F32); slcT = T("slc")
                    slh_tm = sb(bst, "slh_tm", [16, 512], F32); slhT = T("slh")
                    H0 = sb(bst, "H0", [128, 16], F32); H0T = T("H0")
                    HL = sb(bst, "HL", [128, 4, 17], F32); HLT = T("HL")
                    hl_tm = sb(bst, "hl_tm", [17, 512], F32); hltT = T("hl_tm")
                    lv = sb(bst, "lv", [128, 4, 8], F32); lvT = T("lv")
                    lc = sb(bst, "lc", [128, 4, 4], F32); lcT = T("lc")
                    lxtm = sb(bst, "lxtm", [128, 512], F32); lxtmT = T("lxtm")
                    wlr = sb(bst, "wlr", [128, 8, 128], BF16); wlrT = T("wlr")

                    DMA("sync", slc_tm[:], slc[l], w=[slcT])
                    DMA("sync", slh_tm[:], slh[l], w=[slhT])
                    DMA("sync", lv[:], v_lru[l], w=[lvT])
                    DMA("gpsimd", wlr[:], w_lru[l], w=[wlrT])
                    OP("vector", "memset", EXT[:, 0:3], 0.0, w=[EXTT])
                    OP("vector", "memset", XC[:, 2112:NT], 0.0, w=[XCT])
                    ACT(lc[:, :, 0], lv[:, :, 7], AF.Exp, scale=-1.0, r=[lvT], w=[lcT])
                    ACT(lc[:, :, 0], lc[:, :, 0], AF.Ln, bias=1.0, r=[lcT], w=[lcT])
                    OP("vector", "tensor_scalar", lc[:, :, 1], lc[:, :, 0], -8.0, None, op0=ALU.mult, r=[lcT], w=[lcT])
                    OP("vector", "tensor_scalar", lc[:, :, 2], lc[:, :, 0], -16.0, None, op0=ALU.mult, r=[lcT], w=[lcT])

                    wlx, wlxT = wload(w_in[l, 7], 8, 512)
                    for (tt, kind) in ((15, "p"), (16, "s")):
                        proj_tm(wlx, wlxT, 512, tt, 4)
                        OP("vector", "tensor_copy", lxtm[:], PB[4], r=[PT[4]], w=[lxtmT])
                        if kind == "p":
                            DMA("sync", lruc[l, 0], lxtm[125:128, :], r=[lxtmT])
                        else:
                            for i in range(3):
                                DMA("sync", lruc[l, 1:17, i, :], lxtm[1 + i:64:4, :], r=[lxtmT])
                    for n in range(4):
                        EXSv = EXS
                        TR(PB[4][:, 0:48], slc_tm[0:48, n * 128:(n + 1) * 128], identf[0:48, 0:48], r=[slcT, idT], w=[PT[4]])
                        OP("vector", "tensor_copy", EXS[:, :, 0:3], PB[4][:, 0:48].rearrange("p (b i) -> p b i", i=3), r=[PT[4]], w=[EXST])
                        TR(PB[4][:, 64:80], slh_tm[0:16, n * 128:(n + 1) * 128], identf[0:16, 0:16], r=[slhT, idT], w=[PT[4]])
                        OP("vector", "tensor_copy", H0[:], PB[4][:, 64:80], r=[PT[4]], w=[H0T])
                        for bi, blk in enumerate(BLKS):
                            b0, nn = blk
                            bank = bi % 4
                            proj_fm(wlx, wlxT, n * 128, blk, bank)
                            if bi < 4:
                                ACT(EXT[:, 3 + b0:3 + b0 + nn], PB[bank][:, 0:nn], AF.Copy, r=[PT[bank]], w=[EXTT])
                            else:
                                ACT(EXS[:, :, 3:7], PB[bank][:, 0:64].rearrange("p (b t) -> p b t", t=4), AF.Copy, r=[PT[bank]], w=[EXST])
                        XCs = XC[:, SS:SS + 64].rearrange("p (b t) -> p b t", t=4)
                        OP("vector", "tensor_scalar", XC[:, 0:2048], EXT[:, 0:2048], lv[:, n, 0:1], lv[:, n, 4:5], op0=ALU.mult, op1=ALU.add,
                           r=[EXTT, lvT], w=[XCT])
                        OP("vector", "tensor_scalar", XCs, EXS[:, :, 0:4], lv[:, n, 0:1], lv[:, n, 4:5], op0=ALU.mult, op1=ALU.add,
                           r=[EXST, lvT], w=[XCT])
                        for i in range(1, 4):
                            OP("vector", "scalar_tensor_tensor", out=XC[:, 0:2048], in0=EXT[:, i:i + 2048], scalar=lv[:, n, i:i + 1], in1=XC[:, 0:2048],
                               op0=ALU.mult, op1=ALU.add, r=[EXTT, lvT, XCT], w=[XCT])
                            OP("vector", "scalar_tensor_tensor", out=XCs, in0=EXS[:, :, i:i + 4], scalar=lv[:, n, i:i + 1], in1=XCs,
                               op0=ALU.mult, op1=ALU.add, r=[EXST, lvT, XCT], w=[XCT])
                        OP("vector", "tensor_copy", XCb[:], XC[:], r=[XCT], w=[XCbT])
                        for bi, blk in enumerate(BLKS):
                            b0, nn = blk
                            b1, b2 = (0, 1) if bi % 2 == 0 else (2, 3)
                            MM(PB[b1][:, 0:nn], wlr[:, n, :], XCb[:, b0:b0 + nn], True, True, r=[wlrT, XCbT], w=[PT[b1]])
                            MM(PB[b2][:, 0:nn], wlr[:, 4 + n, :], XCb[:, b0:b0 + nn], True, True, r=[wlrT, XCbT], w=[PT[b2]])
                            ACT(RA[:, b0:b0 + nn], PB[b1][:, 0:nn], AF.Sigmoid, bias=lv[:, n, 5:6], r=[PT[b1], lvT], w=[RAT])
                            ACT(IB[:, b0:b0 + nn], PB[b2][:, 0:nn], AF.Sigmoid, bias=lv[:, n, 6:7], r=[PT[b2], lvT], w=[IBT])
                        ACT(A2[:], RA[:], AF.Exp, scale=lc[:, n, 2:3], r=[RAT, lcT], w=[A2T])
                        ACT(RA[:], RA[:], AF.Exp, scale=lc[:, n, 1:2], r=[RAT, lcT], w=[RAT])
                        ACT(A2[:], A2[:], AF.Sqrt, scale=-1.0, bias=1.0, r=[A2T], w=[A2T])
                        OP("vector", "tensor_tensor", IB[:], IB[:], A2[:], op=ALU.mult, r=[IBT, A2T], w=[IBT])
                        OP("vector", "tensor_tensor", IB[:], IB[:], XC[:], op=ALU.mult, r=[IBT, XCT], w=[IBT])
                        RAs = RA[:, SS:SS + 64].rearrange("p (b t) -> p b t", t=4)
                        IBs = IB[:, SS:SS + 64].rearrange("p (b t) -> p b t", t=4)
                        OP("vector", "tensor_tensor", H0[:], H0[:], RAs[:, :, 0], op=ALU.mult, r=[H0T, RAT], w=[H0T])
                        OP("vector", "tensor_tensor", IBs[:, :, 0], IBs[:, :, 0], H0[:], op=ALU.add, r=[H0T, IBT], w=[IBT])
                        OP("vector", "memset", RAs[:, :, 0], 0.0, r=[H0T], w=[RAT])
                        OP("vector", "tensor_tensor_scan", out=XC[:, 0:2048], data0=RA[:, 0:2048], data1=IB[:, 0:2048], initial=0.0,
                           op0=ALU.mult, op1=ALU.add, r=[RAT, IBT], w=[XCT])
                        OP("vector", "tensor_tensor_scan", out=XC[:, SS:SS + 64], data0=RA[:, SS:SS + 64], data1=IB[:, SS:SS + 64], initial=0.0,
                           op0=ALU.mult, op1=ALU.add, r=[RAT, IBT], w=[XCT])
                        ACT(OT[:, n, :], XC[:], AF.Copy, r=[XCT], w=[OTT])
                        OP("vector", "tensor_copy", HL[:, n, 0:1], XC[:, 2047:2048], r=[XCT], w=[HLT])
                        OP("vector", "tensor_copy", HL[:, n, 1:17], XCs[:, :, 3], r=[XCT], w=[HLT])
                    for n in range(4):
                        TR(PB[4][0:17, n * 128:(n + 1) * 128], HL[:, n, :], identf[:], r=[HLT, idT], w=[PT[4]])
                    OP("vector", "tensor_copy", hl_tm[:], PB[4][0:17, :], r=[PT[4]], w=[hltT])
                    DMA("sync", lruh[l], hl_tm[:], r=[hltT])
                    if stop == "lru0":
                        S.barrier(); S.emit(); return nc
                    merge(2)
                    S.barrier()
                    if stop == "lru":
                        S.emit(); return nc

                bst = ExitStack()
                with bst:
                    cosT = sb(bst, "cosT", [128, NT], F32)
                    sinT = sb(bst, "sinT", [128, NT], F32)
                    tabT = T("tab")
                    DMA("sync", cosT[:], c_cos, w=[tabT])
                    DMA("sync", sinT[:], c_sin, w=[tabT])
                    dec = sb(bst, "dec", [128, 8, 128], F32)
                    qdec = sb(bst, "qdec", [128, 8, 128], F32)
                    kdec = sb(bst, "kdec", [128, 8], F32)
                    bmask = sb(bst, "bmask", [128, 16, 128], BF16)
                    rmask = sb(bst, "rmask", [128, 16], F32)
                    gnB = sb(bst, "gnB", [128, 512], F32)
                    DMA("sync", dec[:], c_dec, w=[tabT])
                    DMA("sync", qdec[:], c_qdec, w=[tabT])
                    DMA("sync", kdec[:], c_kdec, w=[tabT])
                    DMA("gpsimd", bmask[:], c_bmask, w=[tabT])
                    DMA("sync", rmask[:], c_rmask, w=[tabT])
                    DMA("sync", gnB[:], v_gn[l].partition_broadcast(128), w=[tabT])
                    qrT = sb(bst, "qrT", [128, NT], BF16); qrTT = T("qrT")
                    q2T = sb(bst, "q2T", [128, NT], BF16); q2TT = T("q2T")
                    krT = sb(bst, "krT", [128, NT], BF16); krTT = T("krT")
                    rt = [sb(bst, "rt%d" % i, [128, 512], F32) for i in range(2)]
                    rtT = [T("rt%d" % i) for i in range(2)]
                    Sf = sb(bst, "Sf", [128, 128], F32); SfT = T("Sf")
                    Sb2 = [sb(bst, "Sb%d" % i, [128, 128], BF16) for i in range(2)]; Sb2T = [T("Sb") for i in range(2)]
                    vb2 = [sb(bst, "vb%d" % i, [128, 128], BF16) for i in range(2)]; vb2T = [T("vb") for i in range(2)]
                    sg2 = [sb(bst, "sg%d" % i, [128, 128], F32) for i in range(2)]; sg2T = [T("sg") for i in range(2)]
                    scb2 = [sb(bst, "scb%d" % i, [128, 128], BF16) for i in range(2)]; scb2T = [T("scb") for i in range(2)]
                    kdb2 = [sb(bst, "kdb%d" % i, [128, 128], BF16) for i in range(2)]; kdb2T = [T("kdb") for i in range(2)]
                    on2 = [sb(bst, "on%d" % i, [128, 128], F32) for i in range(2)]; on2T = [T("on") for i in range(2)]
                    orow2 = [sb(bst, "orow%d" % i, [128, 128], BF16) for i in range(2)]; orow2T = [T("orow") for i in range(2)]
                    gs2 = [sb(bst, "gs%d" % i, [128, 8], F32) for i in range(2)]; gs2T = [T("gs") for i in range(2)]
                    TXsc = [T("Xsc") for i in range(2)]; TXv = [T("Xv") for i in range(2)]; TXo = [T("Xo") for i in range(2)]
                    TYd = [T("Yd") for i in range(2)]; TYk = [T("Yk") for i in range(2)]; TYo = [T("Yo") for i in range(2)]
                    Q2p = sb(bst, "Q2p", [128, 16, 128], BF16); Q2pT = T("Q2p")
                    kdp = Q2p; kdpT = Q2pT
                    SSf = sb(bst, "SSf", [128, 16, 128], F32); SSfT = T("SSf")
                    SSb = sb(bst, "SSb", [128, 16, 128], BF16); SSbT = T("SSb")

                    for h in range(4):
                        hs = slice(h * 128, (h + 1) * 128)
                        wq, wqT = wload(w_in[l, 0, :, :, hs], 8, 128)
                        wqs, wqsT = wload(w_in[l, 14, :, :, hs], 8, 128)
                        for (wa, waT, wsw, wswT, dst, dstT, scl) in ((wq, wqT, wqs, wqsT, qrT, qrTT, 1.0),):
                            for bi, blk in enumerate(BLKS):
                                b0, nn = blk
                                b1, b2 = (0, 1) if bi % 2 == 0 else (2, 3)
                                proj_fm(wa, waT, 0, blk, b1)
                                proj_fm(wsw, wswT, 0, blk, b2)
                                t1, t1T = rt[0], rtT[0]
                                t2, t2T = rt[1], rtT[1]
                                OP("vector", "tensor_tensor", t1[:, 0:nn], PB[b1][:, 0:nn], cosT[:, b0:b0 + nn], op=ALU.mult, r=[PT[b1], tabT], w=[t1T])
                                OP("vector", "tensor_tensor", t2[:, 0:nn], PB[b2][:, 0:nn], sinT[:, b0:b0 + nn], op=ALU.mult, r=[PT[b2], tabT], w=[t2T])
                                OP("vector", "tensor_tensor", dst[:, b0:b0 + nn], t1[:, 0:nn], t2[:, 0:nn], op=ALU.add, r=[t1T, t2T], w=[dstT])
                        wk, wkT = wload(w_in[l, 1, :, :, hs], 8, 128)
                        wks, wksT = wload(w_in[l, 15, :, :, hs], 8, 128)
                        for bi, blk in enumerate(BLKS):
                            b0, nn = blk
                            b1, b2 = (0, 1) if bi % 2 == 0 else (2, 3)
                            proj_fm(wk, wkT, 0, blk, b1)
                            proj_fm(wks, wksT, 0, blk, b2)
                            t1, t1T = rt[0], rtT[0]
                            t2, t2T = rt[1], rtT[1]
                            OP("vector", "scalar_tensor_tensor", out=t1[:, 0:nn], in0=PB[b1][:, 0:nn], scalar=128.0 ** -0.5, in1=cosT[:, b0:b0 + nn],
                               op0=ALU.mult, op1=ALU.mult, r=[PT[b1], tabT], w=[t1T])
                            OP("vector", "scalar_tensor_tensor", out=t2[:, 0:nn], in0=PB[b2][:, 0:nn], scalar=128.0 ** -0.5, in1=sinT[:, b0:b0 + nn],
                               op0=ALU.mult, op1=ALU.mult, r=[PT[b2], tabT], w=[t2T])
                            OP("vector", "tensor_tensor", krT[:, b0:b0 + nn], t1[:, 0:nn], t2[:, 0:nn], op=ALU.add, r=[t1T, t2T], w=[krTT])
                        OP("vector", "tensor_tensor", q2T[:, 0:2048].rearrange("p (c i) -> p c i", i=128),
                           qrT[:, 0:2048].rearrange("p (c i) -> p c i", i=128),
                           qdec[:, h, :].unsqueeze(1).to_broadcast([128, 16, 128]), op=ALU.mult, r=[qrTT, tabT], w=[q2TT])
                        OP("vector", "tensor_tensor", q2T[:, SS:NT], qrT[:, SS:NT], qdec[:, 4 + h, :], op=ALU.mult, r=[qrTT, tabT], w=[q2TT])
                        wv, wvT = wload(w_in[l, 2, :, :, hs], 8, 128)
                        wgt, wgtT = wload(w_in[l, 3, :, :, hs], 8, 128)
                        OP("vector", "memset", Sf[:], 0.0, w=[SfT])
                        OP("vector", "memset", Sb2[1][:], 0.0, w=[Sb2T[1]])
                        DMA("sync", SSf[:], sret[l, :, h].rearrange("b d e -> d b e"), w=[SSfT])
                        DMA("gpsimd", SSb[:], sret[l, :, h].rearrange("b d e -> d b e"), w=[SSbT])
                        for c in range(NTT):
                            p = c % 2
                            cs = slice(c * 128, (c + 1) * 128)
                            samp = (c == 16)
                            dsel = 4 + h if samp else h
                            Xsc, Xv, Xg, Xo = PB[p][:, 0:128], PB[2 + p][:, 0:128], PB[2 + p][:, 128:256], PB[4 + p][:, 0:128]
                            yv = PB[6].bitcast(BF16)
                            Yd = PB[7][:, 0:128]
                            TXsc = [PT[0], PT[1]]; TXv = [PT[2], PT[3]]; TXo = [PT[4], PT[5]]
                            TYk = [PT[6], PT[6]]; TYo = [PT[6], PT[6]]; TYd = [PT[7], PT[7]]
                            vb, vbT, sg, sgT, scb, scbT = vb2[p], vb2T[p], sg2[p], sg2T[p], scb2[p], scb2T[p]
                            kdb, kdbT, on, onT, orow, orowT, gs, gsT = kdb2[p], kdb2T[p], on2[p], on2T[p], orow2[p], orow2T[p], gs2[p], gs2T[p]
                            Sbp, SbpT = Sb2[1 - p], Sb2T[1 - p]
                            Sbn, SbnT = Sb2[p], Sb2T[p]
                            MM(Xsc, krT[:, cs], qrT[:, cs], True, True, r=[krTT, qrTT], w=[TXsc[p]])
                            OP("vector", "tensor_tensor", scb[:], Xsc, dec[:, dsel, :], op=ALU.mult, r=[TXsc[p], tabT], w=[scbT])
                            for kc in range(8):
                                MM(Xv, FA[:, kc, cs], wv[:, kc, :], kc == 0, kc == 7, r=[FAT, wvT], w=[TXv[p]])
                            for kc in range(8):
                                MM(Xg, FA[:, kc, cs], wgt[:, kc, :], kc == 0, kc == 7, r=[FAT, wgtT], w=[TXv[p]])
                            ACT(vb[:], Xv, AF.Copy, r=[TXv[p]], w=[vbT])
                            ACT(sg[:], Xg, AF.Silu, r=[TXv[p]], w=[sgT])
                            TR(yv[:, 0:128], krT[:, cs], identb[:], r=[krTT, idT], w=[TYk[p]])
                            OP("vector", "tensor_scalar", kdb[:], yv[:, 0:128], kdec[:, dsel:dsel + 1], None, op0=ALU.mult, r=[TYk[p], tabT], w=[kdbT])
                            if not samp:
                                MM(Yd, kdb[:], vb[:], True, True, r=[kdbT, vbT], w=[TYd[p]])
                                MM(Xo, scb[:], vb[:], True, False, r=[scbT, vbT], w=[TXo[p]])
                                MM(Xo, q2T[:, cs], Sbp[:], False, True, r=[q2TT, SbpT], w=[TXo[p]])
                                OP("vector", "scalar_tensor_tensor", out=Sbn[:], in0=Sf[:], scalar=GAM[h] ** 128, in1=Yd,
                                   op0=ALU.mult, op1=ALU.add, r=[SfT, TYd[p]], w=[SbnT])
                                OP("vector", "scalar_tensor_tensor", out=Sf[:], in0=Sf[:], scalar=GAM[h] ** 128, in1=Yd,
                                   op0=ALU.mult, op1=ALU.add, r=[SfT, TYd[p]], w=[SfT])
                                if c == 15:
                                    DMA("sync", rets[l, 0, h], Sf[:], r=[SfT])
                            else:
                                OP("vector", "tensor_tensor", Q2p[:], q2T[:, cs].unsqueeze(1).to_broadcast([128, 16, 128]), bmask[:],
                                   op=ALU.mult, r=[q2TT, tabT], w=[Q2pT])
                                MM(Xo, scb[:], vb[:], True, False, r=[scbT, vbT], w=[TXo[p]])
                                for b in range(16):
                                    MM(Xo, Q2p[:, b, :], SSb[:, b, :], False, b == 15, r=[Q2pT, SSbT], w=[TXo[p]])
                                OP("vector", "tensor_tensor", kdp[:], kdb[:].unsqueeze(1).to_broadcast([128, 16, 128]),
                                   rmask[:].unsqueeze(2).to_broadcast([128, 16, 128]), op=ALU.mult, r=[kdbT, tabT], w=[kdpT])
                                for g4 in range(4):
                                    bank = 7
                                    for bb_ in range(4):
                                        b = g4 * 4 + bb_
                                        MM(PB[bank][:, bb_ * 128:(bb_ + 1) * 128], kdp[:, b, :], vb[:], True, True, r=[kdpT, vbT], w=[PT[bank]])
                                    OP("vector", "scalar_tensor_tensor", out=SSf[:, g4 * 4:(g4 + 1) * 4, :], in0=SSf[:, g4 * 4:(g4 + 1) * 4, :],
                                       scalar=GAM[h] ** 4, in1=PB[bank].rearrange("p (b e) -> p b e", e=128),
                                       op0=ALU.mult, op1=ALU.add, r=[SSfT, PT[bank]], w=[SSfT])
                                DMA("sync", rets[l, 1:17, h].rearrange("b d e -> d b e"), SSf[:], r=[SSfT])
                            ACT(on[:], Xo, AF.Square, accum_out=gs[:, 1:2], r=[TXo[p]], w=[onT, gsT])
                            OP("vector", "reduce_sum", gs[:, 0:1], Xo, axis=AX.X, r=[TXo[p], onT], w=[gsT])
                            OP("vector", "tensor_scalar", gs[:, 2:3], gs[:, 0:1], 1.0 / 128, None, op0=ALU.mult, r=[gsT], w=[gsT])
                            OP("vector", "tensor_tensor", gs[:, 3:4], gs[:, 2:3], gs[:, 2:3], op=ALU.mult, r=[gsT], w=[gsT])
                            OP("vector", "scalar_tensor_tensor", out=gs[:, 4:5], in0=gs[:, 1:2], scalar=1.0 / 128, in1=gs[:, 3:4],
                               op0=ALU.mult, op1=ALU.subtract, r=[gsT], w=[gsT])
                            ACT(gs[:, 5:6], gs[:, 4:5], AF.Sqrt, bias=eps_t[:, 0:1], r=[gsT, cT], w=[gsT])
                            OP("vector", "reciprocal", gs[:, 6:7], gs[:, 5:6], r=[gsT], w=[gsT])
                            OP("vector", "scalar_tensor_tensor", out=gs[:, 7:8], in0=gs[:, 2:3], scalar=-1.0, in1=gs[:, 6:7],
                               op0=ALU.mult, op1=ALU.mult, r=[gsT], w=[gsT])
                            ACT(on[:], Xo, AF.Identity, scale=gs[:, 6:7], bias=gs[:, 7:8], r=[TXo[p], gsT], w=[onT])
                            OP("vector", "tensor_tensor", on[:], on[:], gnB[:, hs], op=ALU.mult, r=[onT, tabT], w=[onT])
                            OP("vector", "tensor_tensor", orow[:], on[:], sg[:], op=ALU.mult, r=[onT, sgT], w=[orowT])
                            TR(yv[:, 128:256], orow[:], identb[:], r=[orowT, idT], w=[TYo[p]])
                            ACT(OT[:, h, cs], yv[:, 128:256], AF.Copy, r=[TYo[p]], w=[OTT])
                    merge(0)
                    S.barrier()
                    if stop == "ret":
                        S.emit(); return nc

                bst = ExitStack()
                with bst:
                    B1 = sb(bst, "B1", [128, 2052], F32); B1T = T("B1")
                    B2 = sb(bst, "B2", [128, 2052], F32); B2T = T("B2")
                    Ab = sb(bst, "Ab", [128, 2048], BF16); AbT = T("Ab")
                    Apad = sb(bst, "Apad", [16, 16, 64], BF16); ApadT = T("Apad")
                    AT_ = sb(bst, "AT", [128, 17, 128], BF16); ATT = T("AT")
                    bb = sb(bst, "bb", [128, 32], F32); bbT = T("bb")
                    bb16 = sb(bst, "bb16", [16, 4], F32); bb16T = T("bb16")
                    sqS = sb(bst, "sqS", [128, 4, 128], BF16)
                    skS = sb(bst, "skS", [128, 4, 128], BF16)
                    svS = sb(bst, "svS", [128, 512], BF16)
                    sqST = T("sqS")
                    DMA("sync", bb[:, 0:16], v_sbb[l].partition_broadcast(128), w=[bbT])
                    DMA("sync", bb16[:, 0:1], v_sbb16[l], w=[bb16T])

                    def sb_core(h, qap, qT_, nk_past, keyT, keyTT, vfn, out_ap):
                        rows = 128
                        nkb = nk_past // 128
                        for kb in range(0, nk_past, 512):
                            n = min(512, nk_past - kb)
                            last_blk = (kb + n == nk_past)
                            MM(pz[0:rows, kb:kb + n], qap, keyT[:, kb:kb + n], True, not last_blk, r=[qT_, keyTT], w=PT[0:5])
                            if last_blk:
                                MM(pz[0:rows, nk_past - 128:nk_past], identb[:], mneg[:], False, True, r=[idT, cT], w=PT[0:5])
                        nk = nk_past
                        ACT(B1[0:rows, 0:nk], pz[0:rows, 0:nk], AF.Exp, bias=bb[0:rows, h:h + 1], r=PT[0:5] + [bbT], w=[B1T])
                        ACT(B1[0:rows, 0:nk], B1[0:rows, 0:nk], AF.Ln, bias=1.0, r=[B1T], w=[B1T])
                        OP("vector", "tensor_tensor_scan", out=B2[0:rows, 0:nk], data0=ones[0:rows, 0:1].to_broadcast([rows, nk]), data1=B1[0:rows, 0:nk], initial=0.0,
                           op0=ALU.mult, op1=ALU.add, r=[cT, B1T], w=[B2T])
                        OP("vector", "tensor_tensor", bb[0:rows, 16:17], bb[0:rows, h:h + 1], B2[0:rows, nk - 1:nk], op=ALU.subtract, r=[bbT, B2T], w=[bbT])
                        OP("vector", "tensor_tensor", B1[0:rows, 0:nk], B2[0:rows, 0:nk], B1[0:rows, 0:nk], op=ALU.subtract, r=[B1T, B2T], w=[B1T])
                        OP("vector", "tensor_tensor", B1[0:rows, 0:nk], B1[0:rows, 0:nk], pz[0:rows, 0:nk], op=ALU.add, r=[B1T] + PT[0:5], w=[B1T])
                        ACT(Ab[0:rows, 0:nk_past], B1[0:rows, 0:nk_past], AF.Exp, bias=bb[0:rows, 16:17], r=[B1T, bbT], w=[AbT])
                        for g in range(0, nkb, 8):
                            bank = 5 + ((g // 8) % 2)
                            pv = PB[bank].bitcast(BF16)
                            ng = min(8, nkb - g)
                            for k in range(ng):
                                TR(pv[:, k * 128:k * 128 + rows], Ab[0:rows, (g + k) * 128:(g + k + 1) * 128], identb[0:rows, 0:rows],
                                   r=[AbT, idT], w=[PT[bank]])
                            OP("vector", "tensor_copy", AT_[:, g:g + ng, 0:rows], pv[:, 0:ng * 128].rearrange("p (k n) -> p k n", n=128)[:, :, 0:rows],
                               r=[PT[bank]], w=[ATT])
                        for kt in range(nkb):
                            vap, vT = vfn(kt)
                            MM(PB[7][:, 0:rows], vap, AT_[:, kt, 0:rows], kt == 0, kt == nkb - 1, r=[vT, ATT], w=[PT[7]])
                        ACT(out_ap, PB[7][:, 0:rows], AF.Copy, r=[PT[7]], w=[OTT])

                    pst = ExitStack()
                    with pst:
                        sqT = sb(pst, "sqT", [128, NT], BF16); sqTT = T("sqT")
                        skT = sb(pst, "skT", [128, NT], BF16); skTT = T("skT")
                        svtm = sb(pst, "svtm", [128, NTT, 512], BF16); svtmT = T("svtm")
                        kvo = [sb(pst, "kvo%d" % i, [128, 512], F32) for i in range(2)]
                        kvoT = [T("kvo%d" % i) for i in range(2)]
                        wkf, wkfT = wload(w_in[l, 5], 8, 512)
                        wvf, wvfT = wload(w_in[l, 6], 8, 512)
                        for tt in range(NTT):
                            ts_ = slice(tt * 128, (tt + 1) * 128)
                            proj_tm(wkf, wkfT, 512, tt, 0)
                            OP("vector", "tensor_copy", kvo[0][:], PB[0], r=[PT[0]], w=[kvoT[0]])
                            DMA("sync", sbk[l, ts_, :], kvo[0][:], r=[kvoT[0]])
                            proj_tm(wvf, wvfT, 512, tt, 1)
                            OP("vector", "tensor_copy", kvo[1][:], PB[1], r=[PT[1]], w=[kvoT[1]])
                            OP("vector", "tensor_copy", svtm[:, tt, :], PB[1], r=[PT[1]], w=[svtmT])
                            DMA("sync", sbv[l, ts_, :], kvo[1][:], r=[kvoT[1]])
                        OP("vector", "tensor_copy", svS[:], svtm[:, 16, :], r=[svtmT], w=[sqST])
                        for h in range(4):
                            hs = slice(h * 128, (h + 1) * 128)
                            wq, wqT = wload(w_in[l, 4, :, :, hs], 8, 128)
                            wk, wkT = wload(w_in[l, 5, :, :, hs], 8, 128)
                            for bi, blk in enumerate(BLKS):
                                b0, nn = blk
                                b1, b2 = (0, 1) if bi % 2 == 0 else (2, 3)
                                proj_fm(wq, wqT, 0, blk, b1)
                                proj_fm(wk, wkT, 0, blk, b2)
                                ACT(sqT[:, b0:b0 + nn], PB[b1][:, 0:nn], AF.Copy, scale=128.0 ** -0.5, r=[PT[b1]], w=[sqTT])
                                OP("vector", "tensor_copy", skT[:, b0:b0 + nn], PB[b2][:, 0:nn], r=[PT[b2]], w=[skTT])
                            OP("vector", "tensor_copy", sqS[:, h, :], sqT[:, SS:NT], r=[sqTT], w=[sqST])
                            OP("vector", "tensor_copy", skS[:, h, :], skT[:, SS:NT], r=[skTT], w=[sqST])
                            for qb in range(16):
                                qc = slice(qb * 128, (qb + 1) * 128)
                                sb_core(h, sqT[:, qc], sqTT, (qb + 1) * 128, skT, skTT,
                                        lambda kt, h=h: (svtm[:, kt, h * 128:(h + 1) * 128], svtmT), OT[:, h, qc])
                        OP("vector", "memset", OT[:, :, SS:NT], 0.0, w=[OTT])
                        S.barrier()
                    if do_sample_sb:
                        sst = ExitStack()
                        with sst:
                            Kb = sb(sst, "Kb", [128, 16, 512], BF16); KbT = [T("Kb%d" % i) for i in range(16)]
                            Vb = sb(sst, "Vb", [128, 16, 512], BF16); VbT = [T("Vb%d" % i) for i in range(16)]
                            KT = [sb(sst, "KT%d" % i, [128, 2048], BF16) for i in range(2)]
                            KTT = [T("KT%d" % i) for i in range(2)]
                            qpad = [sb(sst, "qpad%d" % i, [128, 96], BF16) for i in range(2)]
                            qpadT = [T("qpad%d" % i) for i in range(2)]
                            E16 = sb(sst, "E16", [4, 16], BF16); E16T = T("E16")
                            OP("vector", "memset", Apad[:], 0.0, w=[ApadT])
                            for i in range(2):
                                OP("vector", "memset", qpad[i][:], 0.0, w=[qpadT[i]])
                            for hh_ in range(4):
                                OP("vector", "tensor_copy", E16[:, hh_ * 4:(hh_ + 1) * 4], identb[0:4, 0:4], r=[idT], w=[E16T])
                            for b in range(16):
                                ix, ixT = idx16[b % 2], idx16T[b % 2]
                                OP("vector", "tensor_scalar", ix[:], ptf[:, b * 16:(b + 1) * 16], float(l * 2560 * 128), None, op0=ALU.add,
                                   r=[ptT], w=[ixT])
                                for pg in range(16):
                                    S.op("gpsimd", (lambda pg, ix: lambda hh: hh.indirect_dma_start(
                                        out=Kb[:, pg, :], out_offset=None, in_=ck,
                                        in_offset=bass.IndirectOffsetOnAxis(ap=ix[:, pg:pg + 1], axis=0)))(pg, ix),
                                        reads=[ixT], writes=[KbT[pg]], dma=True)
                                for pg in range(16):
                                    S.op("gpsimd", (lambda pg, ix: lambda hh: hh.indirect_dma_start(
                                        out=Vb[:, pg, :], out_offset=None, in_=cv,
                                        in_offset=bass.IndirectOffsetOnAxis(ap=ix[:, pg:pg + 1], axis=0)))(pg, ix),
                                        reads=[ixT], writes=[VbT[pg]], dma=True)
                                qp, qpT = qpad[b % 2], qpadT[b % 2]
                                OP("vector", "tensor_copy", qp[:, 0:80].rearrange("p (h x) -> p h x", x=20)[:, :, 0:4], sqS[:, :, b * 4:b * 4 + 4],
                                   r=[sqST], w=[qpT])
                                for h in range(4):
                                    kt_, ktT = KT[h % 2], KTT[h % 2]
                                    for g in range(0, 16, 8):
                                        bank = 5 + ((g // 8) % 2)
                                        pv = PB[bank].bitcast(BF16)
                                        for k in range(8):
                                            TR(pv[:, k * 128:(k + 1) * 128], Kb[:, g + k, h * 128:(h + 1) * 128], identb[:], r=[KbT[g + k], idT], w=[PT[bank]])
                                        OP("vector", "tensor_copy", kt_[:, g * 128:(g + 8) * 128], pv, r=[PT[bank]], w=[ktT])
                                    for kb in range(0, 2048, 512):
                                        MM(pz[0:16, kb:kb + 512], qp[:, h * 16:(h + 1) * 16], kt_[:, kb:kb + 512], h == 0, h == 3, r=[qpT, ktT], w=PT[0:5])
                                    MM(pz[0:16, 2048:2052], qp[:, h * 16:(h + 1) * 16], skS[:, h, b * 4:b * 4 + 4], h == 0, False, r=[qpT, sqST], w=PT[0:5])
                                MM(pz[0:16, 2048:2052], E16[:], mneg[0:4, 0:4], False, True, r=[E16T, cT], w=PT[0:5])
                                nk = 2052
                                ACT(B1[0:16, 0:nk], pz[0:16, 0:nk], AF.Exp, bias=bb16[:, 0:1], r=PT[0:5] + [bb16T], w=[B1T])
                                ACT(B1[0:16, 0:nk], B1[0:16, 0:nk], AF.Ln, bias=1.0, r=[B1T], w=[B1T])
                                OP("vector", "tensor_tensor_scan", out=B2[0:16, 0:nk], data0=ones[0:16, 0:1].to_broadcast([16, nk]), data1=B1[0:16, 0:nk], initial=0.0,
                                   op0=ALU.mult, op1=ALU.add, r=[cT, B1T], w=[B2T])
                                OP("vector", "tensor_tensor", bb16[:, 1:2], bb16[:, 0:1], B2[0:16, nk - 1:nk], op=ALU.subtract, r=[bb16T, B2T], w=[bb16T])
                                OP("vector", "tensor_tensor", B1[0:16, 0:nk], B2[0:16, 0:nk], B1[0:16, 0:nk], op=ALU.subtract, r=[B1T, B2T], w=[B1T])
                                OP("vector", "tensor_tensor", B1[0:16, 0:nk], B1[0:16, 0:nk], pz[0:16, 0:nk], op=ALU.add, r=[B1T] + PT[0:5], w=[B1T])
                                ACT(Ab[0:16, 0:2048], B1[0:16, 0:2048], AF.Exp, bias=bb16[:, 1:2], r=[B1T, bb16T], w=[AbT])
                                ACT(Apad[0:16, b, b * 4:b * 4 + 4], B1[0:16, 2048:2052], AF.Exp, bias=bb16[:, 1:2], r=[B1T, bb16T], w=[ApadT])
                                for g in range(0, 16, 8):
                                    bank = 5 + ((g // 8) % 2)
                                    pv = PB[bank].bitcast(BF16)
                                    for k in range(8):
                                        TR(pv[:, k * 128:k * 128 + 16], Ab[0:16, (g + k) * 128:(g + k + 1) * 128], identb[0:16, 0:16], r=[AbT, idT], w=[PT[bank]])
                                    OP("vector", "tensor_copy", AT_[:, g:g + 8, 0:16], pv.rearrange("p (k n) -> p k n", n=128)[:, :, 0:16], r=[PT[bank]], w=[ATT])
                                pv = PB[5].bitcast(BF16)
                                TR(pv[0:64, 0:16], Apad[0:16, b, :], identb[0:16, 0:16], r=[ApadT, idT], w=[PT[5]])
                                OP("vector", "tensor_copy", AT_[0:64, 16, 0:16], pv[0:64, 0:16], r=[PT[5]], w=[ATT])
                                for h in range(4):
                                    for kt in range(16):
                                        MM(PB[7][:, h * 4:(h + 1) * 4], Vb[:, kt, h * 128:(h + 1) * 128], AT_[:, kt, h * 4:(h + 1) * 4], kt == 0, False,
                                           r=[VbT[kt], ATT], w=[PT[7]])
                                    MM(PB[7][:, h * 4:(h + 1) * 4], svS[0:64, h * 128:(h + 1) * 128], AT_[0:64, 16, h * 4:(h + 1) * 4], False, True,
                                       r=[sqST, ATT], w=[PT[7]])
                                ACT(OT[:, :, SS + b * 4:SS + b * 4 + 4], PB[7][:, 0:16].rearrange("p (h t) -> p h t", t=4), AF.Copy, r=[PT[7]], w=[OTT])
                            S.barrier()
                    merge(1)
                    S.barrier()
                    if stop == "sb":
                        S.emit(); return nc

                for ch in range(2):
                    wo, woT = wload(w_out[l, ch], 8, 512)
                    for tt in range(NTT):
                        ts_ = slice(tt * 128, (tt + 1) * 128)
                        bank = tt % 4
                        xt, xtT = xio[tt % 2], xioT[tt % 2]
                        DMA("sync", xt[:, 0:512], xsrc[ts_, ch * 512:(ch + 1) * 512], r=([xsrcT[tt]] if xsrcT else []), w=[xtT])
                        proj_tm(wo, woT, 512, tt, bank, src=FB, srcT=FBT)
                        OP("vector", "tensor_tensor", xt[:, 0:512], xt[:, 0:512], PB[bank], op=ALU.add, r=[xtT, PT[bank]], w=[xtT])
                        DMA("sync", xA[l][ts_, ch * 512:(ch + 1) * 512], xt[:, 0:512], r=[xtT], w=[xAT[l][tt]])
                S.barrier()
                if stop == "wout":
                    S.emit(); return nc

            fst = ExitStack()
            with fst:
                X = sb(fst, "X", [128, NTT, D], F32)
                XT = [T("X%d" % i) for i in range(NTT)]
                GE2 = [sb(fst, "GE%d" % i, [128, 2 + 2048], F32) for i in range(2)]; GE2T = [T("GE") for i in range(2)]
                GS2 = [sb(fst, "GS%d" % i, [128, 16, 6], F32) for i in range(2)]; GS2T = [T("GS") for i in range(2)]
                GC2 = [sb(fst, "GC%d" % i, [128, NT], F32) for i in range(2)]; GC2T = [T("GC") for i in range(2)]
                HT = sb(fst, "HT", [128, 2, NT], BF16); HTT = T("HT")
                sfc_tm = sb(fst, "sfc_tm", [32, DFF], F32); sfcT = T("sfc")
                fv = sb(fst, "fv", [128, 22, 4], F32); fvT = T("fv")
                gtm = sb(fst, "gtm", [128, 256], F32); gtmT = T("gtm")
                DMA("sync", sfc_tm[:], sfc[l], w=[sfcT])
                DMA("sync", fv[:], v_ffn[l], w=[fvT])
                for i_ in range(2):
                    OP("vector", "memset", GE2[i_][:, 0:2], 0.0, w=[GE2T[i_]])
                    OP("vector", "memset", GC2[i_][:, 2112:NT], 0.0, w=[GC2T[i_]])
                DMA("sync", gB[:], v_norm[2 * l + 1].partition_broadcast(128), w=[gBT])
                for tt in range(NTT):
                    DMA("sync", X[:, tt, :], xA[l][tt * 128:(tt + 1) * 128, :], r=[xAT[l][tt]], w=[XT[tt]])
                    rmsnorm_stats(X[:, tt, :], XT[tt], D)
                    nb, nbT = xnb[tt % 2], xnbT[tt % 2]
                    OP("vector", "scalar_tensor_tensor", out=nb[:], in0=X[:, tt, :], scalar=st4[:, 2:3], in1=gB[:],
                       op0=ALU.mult, op1=ALU.mult, r=[XT[tt], st4T, gBT], w=[nbT])
                    bank = 5 + (tt % 2)
                    pv = PB[bank].bitcast(BF16)
                    for kc in range(8):
                        TR(pv[:, kc * 128:(kc + 1) * 128], nb[:, kc * 128:(kc + 1) * 128], identb[:], r=[nbT, idT], w=[PT[bank]])
                    ACT(FB[:, :, tt * 128:(tt + 1) * 128], pv.rearrange("p (k n) -> p k n", n=128), AF.Copy, r=[PT[bank]], w=[FBT])
                for gi in range(NG):
                    wg, wgT = wload(w_gate[l, gi], 8, 256)
                    wu, wuT = wload(w_up[l, gi], 8, 256)
                    wd, wdT = wload(w_down[l, gi], 2, 1024)
                    for (tt, kind) in ((15, "p"), (16, "s")):
                        proj_tm(wg, wgT, 256, tt, 4, src=FB, srcT=FBT)
                        OP("vector", "tensor_copy", gtm[:], PB[4][:, 0:256], r=[PT[4]], w=[gtmT])
                        if kind == "p":
                            DMA("sync", ffnc[l, 0, :, gi * 256:(gi + 1) * 256], gtm[126:128, :], r=[gtmT])
                        else:
                            for i in range(2):
                                DMA("sync", ffnc[l, 1:17, i, gi * 256:(gi + 1) * 256], gtm[2 + i:64:4, :], r=[gtmT])
                    for c2 in range(2):
                        gc = gi * 2 + c2
                        GE, GET, GS, GST, GC, GCT = GE2[c2], GE2T[c2], GS2[c2], GS2T[c2], GC2[c2], GC2T[c2]
                        TR(PB[4][:, 256:288], sfc_tm[0:32, gc * 128:(gc + 1) * 128], identf[0:32, 0:32], r=[sfcT, idT], w=[PT[4]])
                        OP("vector", "tensor_copy", GS[:, :, 0:2], PB[4][:, 256:288].rearrange("p (b i) -> p b i", i=2), r=[PT[4]], w=[GST])
                        for bi, blk in enumerate(BLKS):
                            b0, nn = blk
                            bank = bi % 4
                            proj_fm(wg, wgT, c2 * 128, blk, bank, src=FB, srcT=FBT)
                            if bi < 4:
                                ACT(GE[:, 2 + b0:2 + b0 + nn], PB[bank][:, 0:nn], AF.Copy, r=[PT[bank]], w=[GET])
                            else:
                                ACT(GS[:, :, 2:6], PB[bank][:, 0:64].rearrange("p (b t) -> p b t", t=4), AF.Copy, r=[PT[bank]], w=[GST])
                        GCs = GC[:, SS:SS + 64].rearrange("p (b t) -> p b t", t=4)
                        OP("vector", "tensor_scalar", GC[:, 0:2048], GE[:, 0:2048], fv[:, gc, 0:1], fv[:, gc, 3:4], op0=ALU.mult, op1=ALU.add,
                           r=[GET, fvT], w=[GCT])
                        OP("vector", "tensor_scalar", GCs, GS[:, :, 0:4], fv[:, gc, 0:1], fv[:, gc, 3:4], op0=ALU.mult, op1=ALU.add,
                           r=[GST, fvT], w=[GCT])
                        for i in range(1, 3):
                            OP("vector", "scalar_tensor_tensor", out=GC[:, 0:2048], in0=GE[:, i:i + 2048], scalar=fv[:, gc, i:i + 1], in1=GC[:, 0:2048],
                               op0=ALU.mult, op1=ALU.add, r=[GET, fvT, GCT], w=[GCT])
                            OP("vector", "scalar_tensor_tensor", out=GCs, in0=GS[:, :, i:i + 4], scalar=fv[:, gc, i:i + 1], in1=GCs,
                               op0=ALU.mult, op1=ALU.add, r=[GST, fvT, GCT], w=[GCT])
                        ACT(GC[:], GC[:], AF.Gelu_apprx_tanh, r=[GCT], w=[GCT])
                        for bi, blk in enumerate(BLKS):
                            b0, nn = blk
                            bank = bi % 4
                            proj_fm(wu, wuT, c2 * 128, blk, bank, src=FB, srcT=FBT)
                            OP("vector", "tensor_tensor", HT[:, c2, b0:b0 + nn], GC[:, b0:b0 + nn], PB[bank][:, 0:nn], op=ALU.mult,
                               r=[GCT, PT[bank]], w=[HTT])
                    for tt in range(NTT):
                        for ch in range(2):
                            bank = (tt * 2 + ch) % 4
                            for c2 in range(2):
                                MM(PB[bank], HT[:, c2, tt * 128:(tt + 1) * 128], wd[:, c2, ch * 512:(ch + 1) * 512], c2 == 0, c2 == 1,
                                   r=[HTT, wdT], w=[PT[bank]])
                            OP("vector", "tensor_tensor", X[:, tt, ch * 512:(ch + 1) * 512], X[:, tt, ch * 512:(ch + 1) * 512], PB[bank], op=ALU.add,
                               r=[XT[tt], PT[bank]], w=[XT[tt]])
                if l < L - 1:
                    for tt in range(NTT):
                        DMA("sync", xB[l][tt * 128:(tt + 1) * 128, :], X[:, tt, :], r=[XT[tt]], w=[xBT[l][tt]])
                else:
                    DMA("sync", gB[:], v_norm[2 * L].partition_broadcast(128), w=[gBT])
                    for tt in range(NTT):
                        rmsnorm_stats(X[:, tt, :], XT[tt], D)
                        xt, xtT = xio[tt % 2], xioT[tt % 2]
                        OP("vector", "scalar_tensor_tensor", out=xt[:], in0=X[:, tt, :], scalar=st4[:, 2:3], in1=gB[:],
                           op0=ALU.mult, op1=ALU.mult, r=[XT[tt], st4T, gBT], w=[xtT])
                        DMA("sync", y[tt * 128:(tt + 1) * 128, :], xt[:], r=[xtT])
                S.barrier()
        S.emit()
    return nc


def _consts():
    pos = np.concatenate([np.arange(2048), np.tile(2048 + np.arange(4), 16), np.zeros(64)]).astype(np.float32)
    half = 64
    inv = (np.float32(10000.0) ** (-np.arange(half, dtype=np.float32) / np.float32(half))).astype(np.float32)
    ang = (pos[None, :] * inv[:, None]).astype(np.float32)
    cos = np.cos(ang).astype(np.float32)
    sin = np.sin(ang).astype(np.float32)
    c_cos = np.concatenate([cos, cos], 0)
    c_sin = np.concatenate([-sin, sin], 0)
    lg = np.log(np.array(GAM, dtype=np.float64))
    idx = np.arange(128)
    dec = np.zeros((128, 8, 128), np.float64)
    qdec = np.zeros((128, 8, 128), np.float64)
    kdec = np.zeros((128, 8), np.float64)
    for h in range(4):
        diff = idx[None, :] - idx[:, None]
        dec[:, h, :] = np.where(diff >= 0, np.exp(np.maximum(diff, 0) * lg[h]), 0.0)
        qdec[:, h, :] = np.exp((idx + 1.0) * lg[h])[None, :]
        kdec[:, h] = np.exp((127.0 - idx) * lg[h])
        for j in range(64):
            for i in range(64):
                if j // 4 == i // 4 and i >= j:
                    dec[j, 4 + h, i] = np.exp((i - j) * lg[h])
        t = idx % 4
        qd = np.exp((t + 1.0) * lg[h]); qd[64:] = 0.0
        qdec[:, 4 + h, :] = qd[None, :]
        kd = np.exp((3.0 - t) * lg[h]); kd[64:] = 0.0
        kdec[:, 4 + h] = kd
    mneg = np.where(idx[None, :] < idx[:, None], 0.0, -30000.0)
    bmask = np.zeros((128, 16, 128), np.float32)
    rmask = np.zeros((128, 16), np.float32)
    for b in range(16):
        bmask[:, b, b * 4:b * 4 + 4] = 1.0
        rmask[b * 4:b * 4 + 4, b] = 1.0
    return dict(c_cos=c_cos, c_sin=c_sin, c_dec=dec.astype(np.float32), c_qdec=qdec.astype(np.float32),
                c_kdec=kdec.astype(np.float32), c_mneg=mneg.astype(np.float32), c_bmask=bmask, c_rmask=rmask,
                c_iota=np.arange(128, dtype=np.float32).reshape(128, 1))


def _blk(w, kc, ncol):
    K, N = w.shape
    return np.ascontiguousarray(w.reshape(kc, 128, N // ncol, ncol).transpose(2, 1, 0, 3))


def _prep(x_prompt, x_sample, cache_sb_k, cache_sb_v, state_ret, state_lru_h, state_lru_conv,
           state_ffn_conv, page_table, norm1, w_in, ret_gn, sb_bias, lru_conv_w, lru_conv_b, lru_w_a, lru_b_a,
           lru_w_x, lru_b_x, lru_lambda, w_br_ret, w_br_sb, w_br_lru, w_out, norm2, w_ffn_gate,
           w_ffn_up, ffn_conv_w, ffn_conv_b, w_ffn_down, norm_f, big_cache=True):
    f = lambda a: np.asarray(a, dtype=np.float32)
    x_prompt, x_sample = f(x_prompt), f(x_sample)
    w_in = f(w_in)
    shared = _consts()
    perm = np.concatenate([np.arange(h * 128, (h + 1) * 128).reshape(2, 64)[::-1].reshape(-1) for h in range(4)])
    w_in_ext = np.concatenate([w_in, w_in[:, :, 0:512][:, :, perm], w_in[:, :, 512:1024][:, :, perm]], axis=2)
    shared["w_in"] = np.stack([_blk(w_in_ext[l], 8, 512) for l in range(L)])
    wbr = np.stack([np.stack([f(w)[l].reshape(4, 128, 1024).transpose(1, 0, 2) for w in (w_br_ret, w_br_sb, w_br_lru)]) for l in range(L)])
    shared["w_br"] = np.ascontiguousarray(wbr)
    shared["w_out"] = np.stack([_blk(f(w_out)[l], 8, 512) for l in range(L)])
    shared["w_gate"] = np.stack([_blk(f(w_ffn_gate)[l], 8, 256) for l in range(L)])
    shared["w_up"] = np.stack([_blk(f(w_ffn_up)[l], 8, 256) for l in range(L)])
    shared["w_down"] = np.ascontiguousarray(f(w_ffn_down).reshape(L, NG, 2, 128, 1024).transpose(0, 1, 3, 2, 4))
    wl = np.concatenate([f(lru_w_a), f(lru_w_x)], axis=1)
    shared["w_lru"] = np.ascontiguousarray(wl.transpose(0, 2, 1, 3))
    shared["v_norm"] = np.concatenate([np.stack([f(norm1)[l], f(norm2)[l]]) for l in range(L)] + [f(norm_f)[None]], 0).reshape(2 * L + 1, 1, D)
    shared["v_gn"] = f(ret_gn).reshape(L, 1, 512)
    shared["v_sbb"] = np.ascontiguousarray(np.concatenate([f(sb_bias), np.zeros((L, 12), np.float32)], axis=1).reshape(L, 1, 16))
    vl = np.concatenate([f(lru_conv_w), f(lru_conv_b)[:, None], f(lru_b_a)[:, None], f(lru_b_x)[:, None], f(lru_lambda)[:, None]], axis=1)
    shared["v_lru"] = np.ascontiguousarray(vl.reshape(L, 8, 4, 128).transpose(0, 3, 2, 1))
    vf = np.concatenate([f(ffn_conv_w), f(ffn_conv_b)[:, None]], axis=1)
    shared["v_ffn"] = np.ascontiguousarray(vf.reshape(L, 4, 22, 128).transpose(0, 3, 2, 1))
    if not big_cache:
        shared["ck"] = np.zeros((128, 512), np.float32)
        shared["cv"] = np.zeros((128, 512), np.float32)
    else:
      shared["ck"] = f(cache_sb_k).reshape(L * 2560 * 128, 512)
      shared["cv"] = f(cache_sb_v).reshape(L * 2560 * 128, 512)
    shared["v_sbb16"] = np.ascontiguousarray(np.repeat(f(sb_bias), 4, axis=1).reshape(L, 16, 1))
    in_maps = []
    for c in range(NCORES):
        bs = slice(16 * c, 16 * c + 16)
        xin = np.zeros((NT, D), np.float32)
        xin[0:2048] = x_prompt[c]
        xin[2048:2112] = x_sample[bs].reshape(64, D)
        m = dict(shared)
        m["xin"] = xin
        m["pt"] = np.ascontiguousarray(np.asarray(page_table)[bs].astype(np.int32).reshape(1, 256))
        m["sret"] = np.ascontiguousarray(f(state_ret)[:, bs])
        m["slh"] = np.ascontiguousarray(f(state_lru_h)[:, bs])
        m["slc"] = np.ascontiguousarray(f(state_lru_conv)[:, bs].reshape(L, 48, 512))
        m["sfc"] = np.ascontiguousarray(f(state_ffn_conv)[:, bs].reshape(L, 32, DFF))
        in_maps.append(m)
    return in_maps


def _assemble(R):
    cat = lambda fn, ax: np.concatenate([fn(r) for r in R], axis=ax)
    y_prompt = np.stack([r["y"][0:2048] for r in R])
    y_sample = cat(lambda r: r["y"][2048:2112].reshape(16, 4, D), 0)
    sbk_p = np.stack([r["sbk"][:, 0:2048].reshape(L, 2048, 4, 128) for r in R], axis=1)
    sbv_p = np.stack([r["sbv"][:, 0:2048].reshape(L, 2048, 4, 128) for r in R], axis=1)
    sbk_s = cat(lambda r: r["sbk"][:, 2048:2112].reshape(L, 16, 4, 4, 128), 1)
    sbv_s = cat(lambda r: r["sbv"][:, 2048:2112].reshape(L, 16, 4, 4, 128), 1)
    ret_p = np.stack([r["rets"][:, 0] for r in R], axis=1)
    ret_s = cat(lambda r: r["rets"][:, 1:17], 1)
    lh_p = np.stack([r["lruh"][:, 0] for r in R], axis=1)
    lh_s = cat(lambda r: r["lruh"][:, 1:17], 1)
    lc_p = np.stack([r["lruc"][:, 0] for r in R], axis=1)
    lc_s = cat(lambda r: r["lruc"][:, 1:17], 1)
    fc_p = np.stack([r["ffnc"][:, 0] for r in R], axis=1)
    fc_s = cat(lambda r: r["ffnc"][:, 1:17], 1)
    outs = (y_prompt, y_sample, sbk_p, sbv_p, sbk_s, sbv_s, ret_p, ret_s, lh_p, lh_s, lc_p, lc_s, fc_p, fc_s)
    return tuple(np.ascontiguousarray(o, dtype=np.float32) for o in outs)


def kernel(**inputs):
    in_maps = _prep(**inputs)
    nc = build_nc()
    res = run_bass_kernel_spmd(nc, in_maps, core_ids=list(range(NCORES)))
    return _assemble(res.results)
```

```python
import numpy as np
from contextlib import ExitStack
import concourse.bass as bass
import concourse.mybir as mybir
from concourse.bass_utils import run_bass_kernel_spmd

F32 = mybir.dt.float32
BF16 = mybir.dt.bfloat16
I32 = mybir.dt.int32
AF = mybir.ActivationFunctionType
ALU = mybir.AluOpType
AX = mybir.AxisListType

NCORES = 8
L = 2
D = 1024
NT = 2176
NTT = 17
SS = 2048
BLKS = [(0, 512), (512, 512), (1024, 512), (1536, 512), (2048, 128)]
DFF = 2816
NG = 11
EPS = 1e-6
GAM = [1.0 - 2.0 ** (-5.0 - h) for h in range(4)]
import os as _os
SAME_ENGINE_SYNC = _os.environ.get("SES", "1") == "1"


class T:
    __slots__ = ("name", "w", "r", "excl")

    def __init__(self, name, excl=False):
        self.name = name
        self.w = None
        self.r = []
        self.excl = excl


class Sched:
    ENGS = ("sync", "scalar", "vector", "gpsimd", "tensor")

    def __init__(self, nc, stack, n_dma_sems=8):
        self.nc = nc
        self.recs = []
        self.sem = {e: stack.enter_context(nc.semaphore("s_" + e)) for e in self.ENGS}
        self.nds = n_dma_sems
        self.dsem = {e: [stack.enter_context(nc.semaphore("d_%s%d" % (e, i))) for i in range(n_dma_sems)]
                     for e in ("sync", "gpsimd")}
        self.ndma = {"sync": 0, "gpsimd": 0}
        self.dma_hist = {"sync": [], "gpsimd": []}
        self.last = {e: None for e in self.ENGS}

    def op(self, eng, fn, reads=(), writes=(), dma=False, extra=()):
        idx = len(self.recs)
        waits = set(extra)
        for t in reads:
            if t.w is not None:
                waits.add(t.w)
            if t.excl:
                waits.update(t.r)
        for t in writes:
            if t.w is not None:
                waits.add(t.w)
            waits.update(t.r)
        rec = dict(eng=eng, fn=fn, waits=waits, dma=dma, sig=False, dq=None)
        if dma:
            q = self.ndma[eng]
            self.ndma[eng] += 1
            rec["dq"] = q
            h = self.dma_hist[eng]
            if q >= self.nds:
                waits.add(h[q - self.nds])
            h.append(idx)
        self.recs.append(rec)
        for t in reads:
            t.r.append(idx)
        for t in writes:
            t.w = idx
            t.r = []
        if fn is not None:
            self.last[eng] = idx
        return idx

    def barrier(self):
        pend = set()
        for e in self.ENGS:
            if self.last[e] is not None:
                pend.add(self.last[e])
        for e in ("sync", "gpsimd"):
            pend.update(self.dma_hist[e][-self.nds:])
        for e in self.ENGS:
            self.op(e, None, extra=pend)

    def emit(self):
        nc = self.nc
        recs = self.recs
        for r in recs:
            keep = set()
            for w in r["waits"]:
                rw = recs[w]
                if rw["fn"] is None:
                    continue
                if rw["eng"] == r["eng"] and not rw["dma"]:
                    if r["eng"] in ("tensor", "sync"):
                        continue
                    if not SAME_ENGINE_SYNC:
                        continue
                keep.add(w)
            r["waits"] = keep
            for w in keep:
                recs[w]["sig"] = True
        cnt = {e: 0 for e in self.ENGS}
        for r in recs:
            if r["dma"]:
                q = r["dq"]
                r["sem"] = self.dsem[r["eng"]][q % self.nds]
                r["val"] = 16 * (q // self.nds + 1)
            elif r["sig"]:
                cnt[r["eng"]] += 1
                r["sem"] = self.sem[r["eng"]]
                r["val"] = cnt[r["eng"]]
        per = {e: [r for r in recs if r["eng"] == e] for e in self.ENGS}
        print('SEMCOUNTS', cnt, dict(self.ndma), {e: len(per[e]) for e in self.ENGS})

        def run(e, h):
            seen = {}
            for r in per[e]:
                for w in sorted(r["waits"]):
                    rw = recs[w]
                    key = id(rw["sem"])
                    if seen.get(key, 0) >= rw["val"]:
                        continue
                    seen[key] = rw["val"]
                    h.wait_ge(rw["sem"], rw["val"])
                if r["fn"] is None:
                    continue
                ins = r["fn"](h)
                if r["dma"] or r["sig"]:
                    ins.then_inc(r["sem"], 16 if r["dma"] else 1)
            if e in self.ndma:
                n = self.ndma[e]
                for s in range(min(n, self.nds)):
                    last_q = ((n - 1 - s) // self.nds) * self.nds + s
                    h.wait_ge(self.dsem[e][s], 16 * (last_q // self.nds + 1))

        with nc.Block() as block:
            @block.sync
            def _(h):
                run("sync", h)

            @block.scalar
            def _(h):
                run("scalar", h)

            @block.vector
            def _(h):
                run("vector", h)

            @block.gpsimd
            def _(h):
                run("gpsimd", h)

            @block.tensor
            def _(h):
                run("tensor", h)


def build_nc(n_layers=L, do_sample_sb=True, stop=None, big_cache=True):
    nc = bass.Bass("TRN2", target_bir_lowering=False)
    gstack = ExitStack()
    with gstack:
        S = Sched(nc, gstack)

        def din(name, shape, dt=F32):
            return nc.dram_tensor(name, list(shape), dt, kind="ExternalInput").ap()

        def dout(name, shape):
            return nc.dram_tensor(name, list(shape), F32, kind="ExternalOutput").ap()

        xin = din("xin", [NT, D])
        ck = din("ck", [L * 2560 * 128 if big_cache else 128, 512])
        cv = din("cv", [L * 2560 * 128 if big_cache else 128, 512])
        ptd = din("pt", [1, 256], I32)
        sret = din("sret", [L, 16, 4, 128, 128])
        slh = din("slh", [L, 16, 512])
        slc = din("slc", [L, 48, 512])
        sfc = din("sfc", [L, 32, DFF])
        w_in = din("w_in", [L, 16, 128, 8, 512])
        w_br = din("w_br", [L, 3, 128, 4, 1024])
        w_out = din("w_out", [L, 2, 128, 8, 512])
        w_gate = din("w_gate", [L, NG, 128, 8, 256])
        w_up = din("w_up", [L, NG, 128, 8, 256])
        w_down = din("w_down", [L, NG, 128, 2, 1024])
        w_lru = din("w_lru", [L, 128, 8, 128])
        v_norm = din("v_norm", [2 * L + 1, 1, D])
        v_gn = din("v_gn", [L, 1, 512])
        v_sbb = din("v_sbb", [L, 1, 16])
        v_sbb16 = din("v_sbb16", [L, 16, 1])
        v_lru = din("v_lru", [L, 128, 4, 8])
        v_ffn = din("v_ffn", [L, 128, 22, 4])
        c_cos = din("c_cos", [128, NT])
        c_sin = din("c_sin", [128, NT])
        c_dec = din("c_dec", [128, 8, 128])
        c_qdec = din("c_qdec", [128, 8, 128])
        c_kdec = din("c_kdec", [128, 8])
        c_mneg = din("c_mneg", [128, 128])
        c_bmask = din("c_bmask", [128, 16, 128])
        c_rmask = din("c_rmask", [128, 16])
        c_iota = din("c_iota", [128, 1])

        y = dout("y", [NT, D])
        sbk = dout("sbk", [L, NT, 512])
        sbv = dout("sbv", [L, NT, 512])
        rets = dout("rets", [L, 17, 4, 128, 128])
        lruh = dout("lruh", [L, 17, 512])
        lruc = dout("lruc", [L, 17, 3, 512])
        ffnc = dout("ffnc", [L, 17, 2, DFF])
        xA = [nc.dram_tensor("xA%d" % l, [NT, D], F32, kind="Internal").ap() for l in range(L)]
        xB = [nc.dram_tensor("xB%d" % l, [NT, D], F32, kind="Internal").ap() for l in range(L)]
        xAT = [[T("xA") for _ in range(NTT)] for l in range(L)]
        xBT = [[T("xB") for _ in range(NTT)] for l in range(L)]

        uid = [0]

        def sb(st, name, shape, dt):
            uid[0] += 1
            return st.enter_context(nc.sbuf_tensor("%s_%d" % (name, uid[0]), list(shape), dt))

        def OP(eng, name, *args, r=(), w=(), **kw):
            S.op(eng, lambda h: getattr(h, name)(*args, **kw), reads=r, writes=w)

        def DMA(eng, out, in_, r=(), w=(), **kw):
            S.op(eng, lambda h: h.dma_start(out=out, in_=in_, **kw), reads=r, writes=w, dma=True)

        def MM(out, lhsT, rhs, start, stop, r=(), w=()):
            S.op("tensor", lambda h: h.matmul(out, lhsT=lhsT, rhs=rhs, start=start, stop=stop), reads=r, writes=w)

        def TR(out, in_, ident, r=(), w=()):
            S.op("tensor", lambda h: h.transpose(out, in_, ident), reads=r, writes=w)

        def ACT(out, in_, func, r=(), w=(), **kw):
            S.op("scalar", lambda h: h.activation(out=out, in_=in_, func=func, **kw), reads=r, writes=w)

        pz = gstack.enter_context(nc.psum_tensor("pz", [128, 2560], F32))
        pbs = [gstack.enter_context(nc.psum_tensor("pb%d" % i, [128, 512], F32)) for i in range(3)]
        PB = [pz[:, k * 512:(k + 1) * 512] for k in range(5)] + [p[:, :] for p in pbs]
        PT = [T("pbank%d" % i, excl=True) for i in range(8)]
        pzT = T("pz")

        NRING = 3
        ring = [sb(gstack, "wr%d" % i, [128, 4096], BF16) for i in range(NRING)]
        ringT = [T("wr%d" % i) for i in range(NRING)]
        rstate = {"i": 0}

        def wload(src, kc, ncol):
            i = rstate["i"] % NRING
            rstate["i"] += 1
            view = ring[i][:, 0:kc * ncol].rearrange("p (k n) -> p k n", n=ncol)
            DMA("gpsimd", view, src, w=[ringT[i]])
            return view, ringT[i]

        FB = sb(gstack, "FB", [128, 8, NT], BF16)
        FBT = T("FB")
        xio = [sb(gstack, "xio%d" % i, [128, D], F32) for i in range(2)]
        xioT = [T("xio%d" % i) for i in range(2)]
        xnb = [sb(gstack, "xnb%d" % i, [128, D], BF16) for i in range(2)]
        xnbT = [T("xnb%d" % i) for i in range(2)]
        gB = sb(gstack, "gB", [128, D], F32)
        gBT = T("gB")
        identb = sb(gstack, "identb", [128, 128], BF16)
        identf = sb(gstack, "identf", [128, 128], F32)
        idT = T("ident")
        mneg = sb(gstack, "mneg", [128, 128], BF16)
        mnegf = sb(gstack, "mnegf", [128, 128], F32)
        ones = sb(gstack, "ones", [128, 1], F32)
        cT = T("consts")
        st4 = sb(gstack, "st4", [128, 16], F32)
        st4T = T("st4")

        OP("gpsimd", "memset", identf[:], 1.0, w=[idT])
        OP("gpsimd", "affine_select", out=identf[:], in_=identf[:], pattern=[[-1, 128]], compare_op=ALU.is_equal,
           fill=0.0, base=0, channel_multiplier=1, r=[idT], w=[idT])
        OP("vector", "tensor_copy", identb[:], identf[:], r=[idT], w=[idT])
        DMA("sync", mnegf[:], c_mneg, w=[cT])
        OP("vector", "tensor_copy", mneg[:], mnegf[:], r=[cT], w=[cT])
        OP("vector", "memset", ones[:], 1.0, w=[cT])

        def rmsnorm_stats(xt, xtT, ncols):
            junk = xnb[0]
            ACT(sqs[:, 0:ncols], xt, AF.Square, accum_out=st4[:, 0:1], r=[xtT], w=[sqsT, st4T])
            ACT(st4[:, 1:2], st4[:, 0:1], AF.Sqrt, bias=eps_t[:, 0:1], scale=1.0 / ncols, r=[st4T, cT], w=[st4T])
            OP("vector", "reciprocal", st4[:, 2:3], st4[:, 1:2], r=[st4T], w=[st4T])

        sqs = sb(gstack, "sqs", [128, D], BF16)
        sqsT = T("sqs")
        eps_t = sb(gstack, "eps_t", [128, 1], F32)
        OP("vector", "memset", eps_t[:], EPS, w=[cT])

        def norm_to_FB(src_tile_fn, l_norm_idx):
            DMA("sync", gB[:], v_norm[l_norm_idx].partition_broadcast(128), w=[gBT])
            for tt in range(NTT):
                xt, xtT = src_tile_fn(tt)
                rmsnorm_stats(xt, xtT, D)
                nb, nbT = xnb[tt % 2], xnbT[tt % 2]
                OP("vector", "scalar_tensor_tensor", out=nb[:], in0=xt, scalar=st4[:, 2:3], in1=gB[:],
                   op0=ALU.mult, op1=ALU.mult, r=[xtT, st4T, gBT], w=[nbT])
                bank = 5 + (tt % 2)
                pv = PB[bank].bitcast(BF16)
                for kc in range(8):
                    TR(pv[:, kc * 128:(kc + 1) * 128], nb[:, kc * 128:(kc + 1) * 128], identb[:], r=[nbT, idT], w=[PT[bank]])
                ACT(FB[:, :, tt * 128:(tt + 1) * 128], pv.rearrange("p (k n) -> p k n", n=128), AF.Copy,
                    r=[PT[bank]], w=[FBT])

        ptf = sb(gstack, "ptf", [128, 256], F32)
        idx16 = [sb(gstack, "idx16_%d" % i, [128, 16], I32) for i in range(2)]
        idx16T = [T("idx16") for i in range(2)]
        ptT = T("pt")
        with ExitStack() as tst:
            ptb = sb(tst, "ptb", [128, 256], I32)
            iof = sb(tst, "iof", [128, 1], F32)
            DMA("sync", ptb[:], ptd.partition_broadcast(128), w=[ptT])
            DMA("sync", iof[:], c_iota, w=[ptT])
            OP("vector", "tensor_copy", ptf[:], ptb[:], r=[ptT], w=[ptT])
            OP("vector", "scalar_tensor_tensor", out=ptf[:], in0=ptf[:], scalar=128.0, in1=iof[:].to_broadcast([128, 256]),
               op0=ALU.mult, op1=ALU.add, r=[ptT], w=[ptT])
            S.barrier()

        for l in range(n_layers):
            mst = ExitStack()
            with mst:
                xsrc = xin if l == 0 else xB[l - 1]
                xsrcT = None if l == 0 else xBT[l - 1]
                FA = sb(mst, "FA", [128, 8, NT], BF16)
                FAT = T("FA")
                OT = sb(mst, "OT", [128, 4, NT], BF16)
                OTT = T("OT")

                def src1(tt, xsrc=xsrc, xsrcT=xsrcT):
                    xt, xtT = xio[tt % 2], xioT[tt % 2]
                    DMA("sync", xt[:], xsrc[tt * 128:(tt + 1) * 128, :], r=([xsrcT[tt]] if xsrcT else []), w=[xtT])
                    return xt[:], xtT
                DMA("sync", gB[:], v_norm[2 * l].partition_broadcast(128), w=[gBT])
                for tt in range(NTT):
                    xt, xtT = src1(tt)
                    rmsnorm_stats(xt, xtT, D)
                    nb, nbT = xnb[tt % 2], xnbT[tt % 2]
                    OP("vector", "scalar_tensor_tensor", out=nb[:], in0=xt, scalar=st4[:, 2:3], in1=gB[:],
                       op0=ALU.mult, op1=ALU.mult, r=[xtT, st4T, gBT], w=[nbT])
                    bank = 5 + (tt % 2)
                    pv = PB[bank].bitcast(BF16)
                    for kc in range(8):
                        TR(pv[:, kc * 128:(kc + 1) * 128], nb[:, kc * 128:(kc + 1) * 128], identb[:], r=[nbT, idT], w=[PT[bank]])
                    ACT(FA[:, :, tt * 128:(tt + 1) * 128], pv.rearrange("p (k n) -> p k n", n=128), AF.Copy,
                        r=[PT[bank]], w=[FAT])

                def proj_fm(wv, wT, c0, blk, bank, src=FA, srcT=FAT, kcs=8):
                    b0, n = blk
                    for kc in range(kcs):
                        MM(PB[bank][:, 0:n], wv[:, kc, c0:c0 + 128], src[:, kc, b0:b0 + n], kc == 0, kc == kcs - 1,
                           r=[wT, srcT], w=[PT[bank]])

                def proj_tm(wv, wT, ncol, tt, bank, src=FA, srcT=FAT, kcs=8, c0=0):
                    for kc in range(kcs):
                        MM(PB[bank][:, 0:ncol], src[:, kc, tt * 128:(tt + 1) * 128], wv[:, kc, c0:c0 + ncol], kc == 0, kc == kcs - 1,
                           r=[wT, srcT], w=[PT[bank]])

                first_merge = [True]

                def merge(br):
                    wb, wbT = wload(w_br[l, br], 4, 1024)
                    for half in range(2):
                        wg, wgT = wload(w_in[l, 8 + 2 * br + half], 8, 512)
                        for jj in range(4):
                            j = half * 4 + jj
                            for bi, blk in enumerate(BLKS):
                                b0, n = blk
                                b1, b2 = (0, 1) if (bi % 2 == 0) else (2, 3)
                                proj_fm(wb, wbT, j * 128, blk, b1, src=OT, srcT=OTT, kcs=4)
                                proj_fm(wg, wgT, jj * 128, blk, b2)
                                gt, gtT = mtmp[bi % 2], mtmpT[bi % 2]
                                ACT(gt[:, 0:n], PB[b2][:, 0:n], AF.Sigmoid, r=[PT[b2]], w=[gtT])
                                if first_merge[0]:
                                    OP("vector", "tensor_tensor", FB[:, j, b0:b0 + n], gt[:, 0:n], PB[b1][:, 0:n], op=ALU.mult,
                                       r=[gtT, PT[b1]], w=[FBT])
                                else:
                                    OP("vector", "tensor_tensor", gt[:, 0:n], gt[:, 0:n], PB[b1][:, 0:n], op=ALU.mult,
                                       r=[gtT, PT[b1]], w=[gtT])
                                    OP("vector", "tensor_tensor", FB[:, j, b0:b0 + n], FB[:, j, b0:b0 + n], gt[:, 0:n], op=ALU.add,
                                       r=[gtT, FBT], w=[FBT])
                    first_merge[0] = False

                mtmp = [sb(mst, "mtmp%d" % i, [128, 512], F32) for i in range(2)]
                mtmpT = [T("mtmp%d" % i) for i in range(2)]

                if stop == "norm1":
                    S.barrier(); S.emit(); return nc
                bst = ExitStack()
                with bst:
                    EXT = sb(bst, "EXT", [128, 3 + 2048], F32); EXTT = T("EXT")
                    EXS = sb(bst, "EXS", [128, 16, 7], F32); EXST = T("EXS")
                    XC = sb(bst, "XC", [128, NT], F32); XCT = T("XC")
                    XCb = sb(bst, "XCb", [128, NT], BF16); XCbT = T("XCb")
                    RA = sb(bst, "RA", [128, NT], F32); RAT = T("RA")
                    IB = sb(bst, "IB", [128, NT], F32); IBT = T("IB")
                    A2 = sb(bst, "A2", [128, NT], F32); A2T = T("A2")
                    slc_tm = sb(bst, "slc_tm", [48, 512], F32); slcT = T("slc")
                    slh_tm = sb(bst, "slh_tm", [16, 512], F32); slhT = T("slh")
                    H0 = sb(bst, "H0", [128, 16], F32); H0T = T("H0")
                    HL = sb(bst, "HL", [128, 4, 17], F32); HLT = T("HL")
                    hl_tm = sb(bst, "hl_tm", [17, 512], F32); hltT = T("hl_tm")
                    lv = sb(bst, "lv", [128, 4, 8], F32); lvT = T("lv")
                    lc = sb(bst, "lc", [128, 4, 4], F32); lcT = T("lc")
                    lxtm = sb(bst, "lxtm", [128, 512], F32); lxtmT = T("lxtm")
                    wlr = sb(bst, "wlr", [128, 8, 128], BF16); wlrT = T("wlr")

                    DMA("sync", slc_tm[:], slc[l], w=[slcT])
                    DMA("sync", slh_tm[:], slh[l], w=[slhT])
                    DMA("sync", lv[:], v_lru[l], w=[lvT])
                    DMA("gpsimd", wlr[:], w_lru[l], w=[wlrT])
                    OP("vector", "memset", EXT[:, 0:3], 0.0, w=[EXTT])
                    OP("vector", "memset", XC[:, 2112:NT], 0.0, w=[XCT])
                    ACT(lc[:, :, 0], lv[:, :, 7], AF.Exp, scale=-1.0, r=[lvT], w=[lcT])
                    ACT(lc[:, :, 0], lc[:, :, 0], AF.Ln, bias=1.0, r=[lcT], w=[lcT])
                    OP("vector", "tensor_scalar", lc[:, :, 1], lc[:, :, 0], -8.0, None, op0=ALU.mult, r=[lcT], w=[lcT])
                    OP("vector", "tensor_scalar", lc[:, :, 2], lc[:, :, 0], -16.0, None, op0=ALU.mult, r=[lcT], w=[lcT])

                    wlx, wlxT = wload(w_in[l, 7], 8, 512)
                    for (tt, kind) in ((15, "p"), (16, "s")):
                        proj_tm(wlx, wlxT, 512, tt, 4)
                        OP("vector", "tensor_copy", lxtm[:], PB[4], r=[PT[4]], w=[lxtmT])
                        if kind == "p":
                            DMA("sync", lruc[l, 0], lxtm[125:128, :], r=[lxtmT])
                        else:
                            for i in range(3):
                                DMA("sync", lruc[l, 1:17, i, :], lxtm[1 + i:64:4, :], r=[lxtmT])
                    for n in range(4):
                        EXSv = EXS
                        TR(PB[4][:, 0:48], slc_tm[0:48, n * 128:(n + 1) * 128], identf[0:48, 0:48], r=[slcT, idT], w=[PT[4]])
                        OP("vector", "tensor_copy", EXS[:, :, 0:3], PB[4][:, 0:48].rearrange("p (b i) -> p b i", i=3), r=[PT[4]], w=[EXST])
                        TR(PB[4][:, 64:80], slh_tm[0:16, n * 128:(n + 1) * 128], identf[0:16, 0:16], r=[slhT, idT], w=[PT[4]])
                        OP("vector", "tensor_copy", H0[:], PB[4][:, 64:80], r=[PT[4]], w=[H0T])
                        for bi, blk in enumerate(BLKS):
                            b0, nn = blk
                            bank = bi % 4
                            proj_fm(wlx, wlxT, n * 128, blk, bank)
                            if bi < 4:
                                ACT(EXT[:, 3 + b0:3 + b0 + nn], PB[bank][:, 0:nn], AF.Copy, r=[PT[bank]], w=[EXTT])
                            else:
                                ACT(EXS[:, :, 3:7], PB[bank][:, 0:64].rearrange("p (b t) -> p b t", t=4), AF.Copy, r=[PT[bank]], w=[EXST])
                        XCs = XC[:, SS:SS + 64].rearrange("p (b t) -> p b t", t=4)
                        OP("vector", "tensor_scalar", XC[:, 0:2048], EXT[:, 0:2048], lv[:, n, 0:1], lv[:, n, 4:5], op0=ALU.mult, op1=ALU.add,
                           r=[EXTT, lvT], w=[XCT])
                        OP("vector", "tensor_scalar", XCs, EXS[:, :, 0:4], lv[:, n, 0:1], lv[:, n, 4:5], op0=ALU.mult, op1=ALU.add,
                           r=[EXST, lvT], w=[XCT])
                        for i in range(1, 4):
                            OP("vector", "scalar_tensor_tensor", out=XC[:, 0:2048], in0=EXT[:, i:i + 2048], scalar=lv[:, n, i:i + 1], in1=XC[:, 0:2048],
                               op0=ALU.mult, op1=ALU.add, r=[EXTT, lvT, XCT], w=[XCT])
                            OP("vector", "scalar_tensor_tensor", out=XCs, in0=EXS[:, :, i:i + 4], scalar=lv[:, n, i:i + 1], in1=XCs,
                               op0=ALU.mult, op1=ALU.add, r=[EXST, lvT, XCT], w=[XCT])
                        OP("vector", "tensor_copy", XCb[:], XC[:], r=[XCT], w=[XCbT])
                        for bi, blk in enumerate(BLKS):
                            b0, nn = blk
                            b1, b2 = (0, 1) if bi % 2 == 0 else (2, 3)
                            MM(PB[b1][:, 0:nn], wlr[:, n, :], XCb[:, b0:b0 + nn], True, True, r=[wlrT, XCbT], w=[PT[b1]])
                            MM(PB[b2][:, 0:nn], wlr[:, 4 + n, :], XCb[:, b0:b0 + nn], True, True, r=[wlrT, XCbT], w=[PT[b2]])
                            ACT(RA[:, b0:b0 + nn], PB[b1][:, 0:nn], AF.Sigmoid, bias=lv[:, n, 5:6], r=[PT[b1], lvT], w=[RAT])
                            ACT(IB[:, b0:b0 + nn], PB[b2][:, 0:nn], AF.Sigmoid, bias=lv[:, n, 6:7], r=[PT[b2], lvT], w=[IBT])
                        ACT(A2[:], RA[:], AF.Exp, scale=lc[:, n, 2:3], r=[RAT, lcT], w=[A2T])
                        ACT(RA[:], RA[:], AF.Exp, scale=lc[:, n, 1:2], r=[RAT, lcT], w=[RAT])
                        ACT(A2[:], A2[:], AF.Sqrt, scale=-1.0, bias=1.0, r=[A2T], w=[A2T])
                        OP("vector", "tensor_tensor", IB[:], IB[:], A2[:], op=ALU.mult, r=[IBT, A2T], w=[IBT])
                        OP("vector", "tensor_tensor", IB[:], IB[:], XC[:], op=ALU.mult, r=[IBT, XCT], w=[IBT])
                        RAs = RA[:, SS:SS + 64].rearrange("p (b t) -> p b t", t=4)
                        IBs = IB[:, SS:SS + 64].rearrange("p (b t) -> p b t", t=4)
                        OP("vector", "tensor_tensor", H0[:], H0[:], RAs[:, :, 0], op=ALU.mult, r=[H0T, RAT], w=[H0T])
                        OP("vector", "tensor_tensor", IBs[:, :, 0], IBs[:, :, 0], H0[:], op=ALU.add, r=[H0T, IBT], w=[IBT])
                        OP("vector", "memset", RAs[:, :, 0], 0.0, r=[H0T], w=[RAT])
                        OP("vector", "tensor_tensor_scan", out=XC[:, 0:2048], data0=RA[:, 0:2048], data1=IB[:, 0:2048], initial=0.0,
                           op0=ALU.mult, op1=ALU.add, r=[RAT, IBT], w=[XCT])
                        OP("vector", "tensor_tensor_scan", out=XC[:, SS:SS + 64], data0=RA[:, SS:SS + 64], data1=IB[:, SS:SS + 64], initial=0.0,
                           op0=ALU.mult, op1=ALU.add, r=[RAT, IBT], w=[XCT])
                        ACT(OT[:, n, :], XC[:], AF.Copy, r=[XCT], w=[OTT])
                        OP("vector", "tensor_copy", HL[:, n, 0:1], XC[:, 2047:2048], r=[XCT], w=[HLT])
                        OP("vector", "tensor_copy", HL[:, n, 1:17], XCs[:, :, 3], r=[XCT], w=[HLT])
                    for n in range(4):
                        TR(PB[4][0:17, n * 128:(n + 1) * 128], HL[:, n, :], identf[:], r=[HLT, idT], w=[PT[4]])
                    OP("vector", "tensor_copy", hl_tm[:], PB[4][0:17, :], r=[PT[4]], w=[hltT])
                    DMA("sync", lruh[l], hl_tm[:], r=[hltT])
                    if stop == "lru0":
                        S.barrier(); S.emit(); return nc
                    merge(2)
                    S.barrier()
                    if stop == "lru":
                        S.emit(); return nc

                bst = ExitStack()
                with bst:
                    cosT = sb(bst, "cosT", [128, NT], F32)
                    sinT = sb(bst, "sinT", [128, NT], F32)
                    tabT = T("tab")
                    DMA("sync", cosT[:], c_cos, w=[tabT])
                    DMA("sync", sinT[:], c_sin, w=[tabT])
                    dec = sb(bst, "dec", [128, 8, 128], F32)
                    qdec = sb(bst, "qdec", [128, 8, 128], F32)
                    kdec = sb(bst, "kdec", [128, 8], F32)
                    bmask = sb(bst, "bmask", [128, 16, 128], BF16)
                    rmask = sb(bst, "rmask", [128, 16], F32)
                    gnB = sb(bst, "gnB", [128, 512], F32)
                    DMA("sync", dec[:], c_dec, w=[tabT])
                    DMA("sync", qdec[:], c_qdec, w=[tabT])
                    DMA("sync", kdec[:], c_kdec, w=[tabT])
                    DMA("gpsimd", bmask[:], c_bmask, w=[tabT])
                    DMA("sync", rmask[:], c_rmask, w=[tabT])
                    DMA("sync", gnB[:], v_gn[l].partition_broadcast(128), w=[tabT])
                    qrT = sb(bst, "qrT", [128, NT], BF16); qrTT = T("qrT")
                    q2T = sb(bst, "q2T", [128, NT], BF16); q2TT = T("q2T")
                    krT = sb(bst, "krT", [128, NT], BF16); krTT = T("krT")
                    rt = [sb(bst, "rt%d" % i, [128, 512], F32) for i in range(2)]
                    rtT = [T("rt%d" % i) for i in range(2)]
                    Sf = sb(bst, "Sf", [128, 128], F32); SfT = T("Sf")
                    Sb2 = [sb(bst, "Sb%d" % i, [128, 128], BF16) for i in range(2)]; Sb2T = [T("Sb") for i in range(2)]
                    vb2 = [sb(bst, "vb%d" % i, [128, 128], BF16) for i in range(2)]; vb2T = [T("vb") for i in range(2)]
                    sg2 = [sb(bst, "sg%d" % i, [128, 128], F32) for i in range(2)]; sg2T = [T("sg") for i in range(2)]
                    scb2 = [sb(bst, "scb%d" % i, [128, 128], BF16) for i in range(2)]; scb2T = [T("scb") for i in range(2)]
                    kdb2 = [sb(bst, "kdb%d" % i, [128, 128], BF16) for i in range(2)]; kdb2T = [T("kdb") for i in range(2)]
                    on2 = [sb(bst, "on%d" % i, [128, 128], F32) for i in range(2)]; on2T = [T("on") for i in range(2)]
                    orow2 = [sb(bst, "orow%d" % i, [128, 128], BF16) for i in range(2)]; orow2T = [T("orow") for i in range(2)]
                    gs2 = [sb(bst, "gs%d" % i, [128, 8], F32) for i in range(2)]; gs2T = [T("gs") for i in range(2)]
                    TXsc = [T("Xsc") for i in range(2)]; TXv = [T("Xv") for i in range(2)]; TXo = [T("Xo") for i in range(2)]
                    TYd = [T("Yd") for i in range(2)]; TYk = [T("Yk") for i in range(2)]; TYo = [T("Yo") for i in range(2)]
                    Q2p = sb(bst, "Q2p", [128, 16, 128], BF16); Q2pT = T("Q2p")
                    kdp = Q2p; kdpT = Q2pT
                    SSf = sb(bst, "SSf", [128, 16, 128], F32); SSfT = T("SSf")
                    SSb = sb(bst, "SSb", [128, 16, 128], BF16); SSbT = T("SSb")

                    for h in range(4):
                        hs = slice(h * 128, (h + 1) * 128)
                        wq, wqT = wload(w_in[l, 0, :, :, hs], 8, 128)
                        wqs, wqsT = wload(w_in[l, 14, :, :, hs], 8, 128)
                        for (wa, waT, wsw, wswT, dst, dstT, scl) in ((wq, wqT, wqs, wqsT, qrT, qrTT, 1.0),):
                            for bi, blk in enumerate(BLKS):
                                b0, nn = blk
                                b1, b2 = (0, 1) if bi % 2 == 0 else (2, 3)
                                proj_fm(wa, waT, 0, blk, b1)
                                proj_fm(wsw, wswT, 0, blk, b2)
                                t1, t1T = rt[0], rtT[0]
                                t2, t2T = rt[1], rtT[1]
                                OP("vector", "tensor_tensor", t1[:, 0:nn], PB[b1][:, 0:nn], cosT[:, b0:b0 + nn], op=ALU.mult, r=[PT[b1], tabT], w=[t1T])
                                OP("vector", "tensor_tensor", t2[:, 0:nn], PB[b2][:, 0:nn], sinT[:, b0:b0 + nn], op=ALU.mult, r=[PT[b2], tabT], w=[t2T])
                                OP("vector", "tensor_tensor", dst[:, b0:b0 + nn], t1[:, 0:nn], t2[:, 0:nn], op=ALU.add, r=[t1T, t2T], w=[dstT])
                        wk, wkT = wload(w_in[l, 1, :, :, hs], 8, 128)
                        wks, wksT = wload(w_in[l, 15, :, :, hs], 8, 128)
                        for bi, blk in enumerate(BLKS):
                            b0, nn = blk
                            b1, b2 = (0, 1) if bi % 2 == 0 else (2, 3)
                            proj_fm(wk, wkT, 0, blk, b1)
                            proj_fm(wks, wksT, 0, blk, b2)
                            t1, t1T = rt[0], rtT[0]
                            t2, t2T = rt[1], rtT[1]
                            OP("vector", "scalar_tensor_tensor", out=t1[:, 0:nn], in0=PB[b1][:, 0:nn], scalar=128.0 ** -0.5, in1=cosT[:, b0:b0 + nn],
                               op0=ALU.mult, op1=ALU.mult, r=[PT[b1], tabT], w=[t1T])
                            OP("vector", "scalar_tensor_tensor", out=t2[:, 0:nn], in0=PB[b2][:, 0:nn], scalar=128.0 ** -0.5, in1=sinT[:, b0:b0 + nn],
                               op0=ALU.mult, op1=ALU.mult, r=[PT[b2], tabT], w=[t2T])
                            OP("vector", "tensor_tensor", krT[:, b0:b0 + nn], t1[:, 0:nn], t2[:, 0:nn], op=ALU.add, r=[t1T, t2T], w=[krTT])
                        OP("vector", "tensor_tensor", q2T[:, 0:2048].rearrange("p (c i) -> p c i", i=128),
                           qrT[:, 0:2048].rearrange("p (c i) -> p c i", i=128),
                           qdec[:, h, :].unsqueeze(1).to_broadcast([128, 16, 128]), op=ALU.mult, r=[qrTT, tabT], w=[q2TT])
                        OP("vector", "tensor_tensor", q2T[:, SS:NT], qrT[:, SS:NT], qdec[:, 4 + h, :], op=ALU.mult, r=[qrTT, tabT], w=[q2TT])
                        wv, wvT = wload(w_in[l, 2, :, :, hs], 8, 128)
                        wgt, wgtT = wload(w_in[l, 3, :, :, hs], 8, 128)
                        OP("vector", "memset", Sf[:], 0.0, w=[SfT])
                        OP("vector", "memset", Sb2[1][:], 0.0, w=[Sb2T[1]])
                        DMA("sync", SSf[:], sret[l, :, h].rearrange("b d e -> d b e"), w=[SSfT])
                        DMA("gpsimd", SSb[:], sret[l, :, h].rearrange("b d e -> d b e"), w=[SSbT])
                        for c in range(NTT):
                            p = c % 2
                            cs = slice(c * 128, (c + 1) * 128)
                            samp = (c == 16)
                            dsel = 4 + h if samp else h
                            Xsc, Xv, Xg, Xo = PB[p][:, 0:128], PB[2 + p][:, 0:128], PB[2 + p][:, 128:256], PB[4 + p][:, 0:128]
                            yv = PB[6].bitcast(BF16)
                            Yd = PB[7][:, 0:128]
                            TXsc = [PT[0], PT[1]]; TXv = [PT[2], PT[3]]; TXo = [PT[4], PT[5]]
                            TYk = [PT[6], PT[6]]; TYo = [PT[6], PT[6]]; TYd = [PT[7], PT[7]]
                            vb, vbT, sg, sgT, scb, scbT = vb2[p], vb2T[p], sg2[p], sg2T[p], scb2[p], scb2T[p]
                            kdb, kdbT, on, onT, orow, orowT, gs, gsT = kdb2[p], kdb2T[p], on2[p], on2T[p], orow2[p], orow2T[p], gs2[p], gs2T[p]
                            Sbp, SbpT = Sb2[1 - p], Sb2T[1 - p]
                            Sbn, SbnT = Sb2[p], Sb2T[p]
                            MM(Xsc, krT[:, cs], qrT[:, cs], True, True, r=[krTT, qrTT], w=[TXsc[p]])
                            OP("vector", "tensor_tensor", scb[:], Xsc, dec[:, dsel, :], op=ALU.mult, r=[TXsc[p], tabT], w=[scbT])
                            for kc in range(8):
                                MM(Xv, FA[:, kc, cs], wv[:, kc, :], kc == 0, kc == 7, r=[FAT, wvT], w=[TXv[p]])
                            for kc in range(8):
                                MM(Xg, FA[:, kc, cs], wgt[:, kc, :], kc == 0, kc == 7, r=[FAT, wgtT], w=[TXv[p]])
                            ACT(vb[:], Xv, AF.Copy, r=[TXv[p]], w=[vbT])
                            ACT(sg[:], Xg, AF.Silu, r=[TXv[p]], w=[sgT])
                            TR(yv[:, 0:128], krT[:, cs], identb[:], r=[krTT, idT], w=[TYk[p]])
                            OP("vector", "tensor_scalar", kdb[:], yv[:, 0:128], kdec[:, dsel:dsel + 1], None, op0=ALU.mult, r=[TYk[p], tabT], w=[kdbT])
                            if not samp:
                                MM(Yd, kdb[:], vb[:], True, True, r=[kdbT, vbT], w=[TYd[p]])
                                MM(Xo, scb[:], vb[:], True, False, r=[scbT, vbT], w=[TXo[p]])
                                MM(Xo, q2T[:, cs], Sbp[:], False, True, r=[q2TT, SbpT], w=[TXo[p]])
                                OP("vector", "scalar_tensor_tensor", out=Sbn[:], in0=Sf[:], scalar=GAM[h] ** 128, in1=Yd,
                                   op0=ALU.mult, op1=ALU.add, r=[SfT, TYd[p]], w=[SbnT])
                                OP("vector", "scalar_tensor_tensor", out=Sf[:], in0=Sf[:], scalar=GAM[h] ** 128, in1=Yd,
                                   op0=ALU.mult, op1=ALU.add, r=[SfT, TYd[p]], w=[SfT])
                                if c == 15:
                                    DMA("sync", rets[l, 0, h], Sf[:], r=[SfT])
                            else:
                                OP("vector", "tensor_tensor", Q2p[:], q2T[:, cs].unsqueeze(1).to_broadcast([128, 16, 128]), bmask[:],
                                   op=ALU.mult, r=[q2TT, tabT], w=[Q2pT])
                                MM(Xo, scb[:], vb[:], True, False, r=[scbT, vbT], w=[TXo[p]])
                                for b in range(16):
                                    MM(Xo, Q2p[:, b, :], SSb[:, b, :], False, b == 15, r=[Q2pT, SSbT], w=[TXo[p]])
                                OP("vector", "tensor_tensor", kdp[:], kdb[:].unsqueeze(1).to_broadcast([128, 16, 128]),
                                   rmask[:].unsqueeze(2).to_broadcast([128, 16, 128]), op=ALU.mult, r=[kdbT, tabT], w=[kdpT])
                                for g4 in range(4):
                                    bank = 7
                                    for bb_ in range(4):
                                        b = g4 * 4 + bb_
                                        MM(PB[bank][:, bb_ * 128:(bb_ + 1) * 128], kdp[:, b, :], vb[:], True, True, r=[kdpT, vbT], w=[PT[bank]])
                                    OP("vector", "scalar_tensor_tensor", out=SSf[:, g4 * 4:(g4 + 1) * 4, :], in0=SSf[:, g4 * 4:(g4 + 1) * 4, :],
                                       scalar=GAM[h] ** 4, in1=PB[bank].rearrange("p (b e) -> p b e", e=128),
                                       op0=ALU.mult, op1=ALU.add, r=[SSfT, PT[bank]], w=[SSfT])
                                DMA("sync", rets[l, 1:17, h].rearrange("b d e -> d b e"), SSf[:], r=[SSfT])
                            ACT(on[:], Xo, AF.Square, accum_out=gs[:, 1:2], r=[TXo[p]], w=[onT, gsT])
                            OP("vector", "reduce_sum", gs[:, 0:1], Xo, axis=AX.X, r=[TXo[p], onT], w=[gsT])
                            OP("vector", "tensor_scalar", gs[:, 2:3], gs[:, 0:1], 1.0 / 128, None, op0=ALU.mult, r=[gsT], w=[gsT])
                            OP("vector", "tensor_tensor", gs[:, 3:4], gs[:, 2:3], gs[:, 2:3], op=ALU.mult, r=[gsT], w=[gsT])
                            OP("vector", "scalar_tensor_tensor", out=gs[:, 4:5], in0=gs[:, 1:2], scalar=1.0 / 128, in1=gs[:, 3:4],
                               op0=ALU.mult, op1=ALU.subtract, r=[gsT], w=[gsT])
                            ACT(gs[:, 5:6], gs[:, 4:5], AF.Sqrt, bias=eps_t[:, 0:1], r=[gsT, cT], w=[gsT])
                            OP("vector", "reciprocal", gs[:, 6:7], gs[:, 5:6], r=[gsT], w=[gsT])
                            OP("vector", "scalar_tensor_tensor", out=gs[:, 7:8], in0=gs[:, 2:3], scalar=-1.0, in1=gs[:, 6:7],
                               op0=ALU.mult, op1=ALU.mult, r=[gsT], w=[gsT])
                            ACT(on[:], Xo, AF.Identity, scale=gs[:, 6:7], bias=gs[:, 7:8], r=[TXo[p], gsT], w=[onT])
                            OP("vector", "tensor_tensor", on[:], on[:], gnB[:, hs], op=ALU.mult, r=[onT, tabT], w=[onT])
                            OP("vector", "tensor_tensor", orow[:], on[:], sg[:], op=ALU.mult, r=[onT, sgT], w=[orowT])
                            TR(yv[:, 128:256], orow[:], identb[:], r=[orowT, idT], w=[TYo[p]])
                            ACT(OT[:, h, cs], yv[:, 128:256], AF.Copy, r=[TYo[p]], w=[OTT])
                    merge(0)
                    S.barrier()
                    if stop == "ret":
                        S.emit(); return nc

                bst = ExitStack()
                with bst:
                    B1 = sb(bst, "B1", [128, 2052], F32); B1T = T("B1")
                    B2 = sb(bst, "B2", [128, 2052], F32); B2T = T("B2")
                    Ab = sb(bst, "Ab", [128, 2048], BF16); AbT = T("Ab")
                    Apad = sb(bst, "Apad", [16, 16, 64], BF16); ApadT = T("Apad")
                    AT_ = sb(bst, "AT", [128, 17, 128], BF16); ATT = T("AT")
                    bb = sb(bst, "bb", [128, 32], F32); bbT = T("bb")
                    bb16 = sb(bst, "bb16", [16, 4], F32); bb16T = T("bb16")
                    sqS = sb(bst, "sqS", [128, 4, 128], BF16)
                    skS = sb(bst, "skS", [128, 4, 128], BF16)
                    svS = sb(bst, "svS", [128, 512], BF16)
                    sqST = T("sqS")
                    DMA("sync", bb[:, 0:16], v_sbb[l].partition_broadcast(128), w=[bbT])
                    DMA("sync", bb16[:, 0:1], v_sbb16[l], w=[bb16T])

                    def sb_core(h, qap, qT_, nk_past, keyT, keyTT, vfn, out_ap):
                        rows = 128
                        nkb = nk_past // 128
                        for kb in range(0, nk_past, 512):
                            n = min(512, nk_past - kb)
                            last_blk = (kb + n == nk_past)
                            MM(pz[0:rows, kb:kb + n], qap, keyT[:, kb:kb + n], True, not last_blk, r=[qT_, keyTT], w=PT[0:5])
                            if last_blk:
                                MM(pz[0:rows, nk_past - 128:nk_past], identb[:], mneg[:], False, True, r=[idT, cT], w=PT[0:5])
                        nk = nk_past
                        ACT(B1[0:rows, 0:nk], pz[0:rows, 0:nk], AF.Exp, bias=bb[0:rows, h:h + 1], r=PT[0:5] + [bbT], w=[B1T])
                        ACT(B1[0:rows, 0:nk], B1[0:rows, 0:nk], AF.Ln, bias=1.0, r=[B1T], w=[B1T])
                        OP("vector", "tensor_tensor_scan", out=B2[0:rows, 0:nk], data0=ones[0:rows, 0:1].to_broadcast([rows, nk]), data1=B1[0:rows, 0:nk], initial=0.0,
                           op0=ALU.mult, op1=ALU.add, r=[cT, B1T], w=[B2T])
                        OP("vector", "tensor_tensor", bb[0:rows, 16:17], bb[0:rows, h:h + 1], B2[0:rows, nk - 1:nk], op=ALU.subtract, r=[bbT, B2T], w=[bbT])
                        OP("vector", "tensor_tensor", B1[0:rows, 0:nk], B2[0:rows, 0:nk], B1[0:rows, 0:nk], op=ALU.subtract, r=[B1T, B2T], w=[B1T])
                        OP("vector", "tensor_tensor", B1[0:rows, 0:nk], B1[0:rows, 0:nk], pz[0:rows, 0:nk], op=ALU.add, r=[B1T] + PT[0:5], w=[B1T])
                        ACT(Ab[0:rows, 0:nk_past], B1[0:rows, 0:nk_past], AF.Exp, bias=bb[0:rows, 16:17], r=[B1T, bbT], w=[AbT])
                        for g in range(0, nkb, 8):
                            bank = 5 + ((g // 8) % 2)
                            pv = PB[bank].bitcast(BF16)
                            ng = min(8, nkb - g)
                            for k in range(ng):
                                TR(pv[:, k * 128:k * 128 + rows], Ab[0:rows, (g + k) * 128:(g + k + 1) * 128], identb[0:rows, 0:rows],
                                   r=[AbT, idT], w=[PT[bank]])
                            OP("vector", "tensor_copy", AT_[:, g:g + ng, 0:rows], pv[:, 0:ng * 128].rearrange("p (k n) -> p k n", n=128)[:, :, 0:rows],
                               r=[PT[bank]], w=[ATT])
                        for kt in range(nkb):
                            vap, vT = vfn(kt)
                            MM(PB[7][:, 0:rows], vap, AT_[:, kt, 0:rows], kt == 0, kt == nkb - 1, r=[vT, ATT], w=[PT[7]])
                        ACT(out_ap, PB[7][:, 0:rows], AF.Copy, r=[PT[7]], w=[OTT])

                    pst = ExitStack()
                    with pst:
                        sqT = sb(pst, "sqT", [128, NT], BF16); sqTT = T("sqT")
                        skT = sb(pst, "skT", [128, NT], BF16); skTT = T("skT")
                        svtm = sb(pst, "svtm", [128, NTT, 512], BF16); svtmT = T("svtm")
                        kvo = [sb(pst, "kvo%d" % i, [128, 512], F32) for i in range(2)]
                        kvoT = [T("kvo%d" % i) for i in range(2)]
                        wkf, wkfT = wload(w_in[l, 5], 8, 512)
                        wvf, wvfT = wload(w_in[l, 6], 8, 512)
                        for tt in range(NTT):
                            ts_ = slice(tt * 128, (tt + 1) * 128)
                            proj_tm(wkf, wkfT, 512, tt, 0)
                            OP("vector", "tensor_copy", kvo[0][:], PB[0], r=[PT[0]], w=[kvoT[0]])
                            DMA("sync", sbk[l, ts_, :], kvo[0][:], r=[kvoT[0]])
                            proj_tm(wvf, wvfT, 512, tt, 1)
                            OP("vector", "tensor_copy", kvo[1][:], PB[1], r=[PT[1]], w=[kvoT[1]])
                            OP("vector", "tensor_copy", svtm[:, tt, :], PB[1], r=[PT[1]], w=[svtmT])
                            DMA("sync", sbv[l, ts_, :], kvo[1][:], r=[kvoT[1]])
                        OP("vector", "tensor_copy", svS[:], svtm[:, 16, :], r=[svtmT], w=[sqST])
                        for h in range(4):
                            hs = slice(h * 128, (h + 1) * 128)
                            wq, wqT = wload(w_in[l, 4, :, :, hs], 8, 128)
                            wk, wkT = wload(w_in[l, 5, :, :, hs], 8, 128)
                            for bi, blk in enumerate(BLKS):
                                b0, nn = blk
                                b1, b2 = (0, 1) if bi % 2 == 0 else (2, 3)
                                proj_fm(wq, wqT, 0, blk, b1)
                                proj_fm(wk, wkT, 0, blk, b2)
                                ACT(sqT[:, b0:b0 + nn], PB[b1][:, 0:nn], AF.Copy, scale=128.0 ** -0.5, r=[PT[b1]], w=[sqTT])
                                OP("vector", "tensor_copy", skT[:, b0:b0 + nn], PB[b2][:, 0:nn], r=[PT[b2]], w=[skTT])
                            OP("vector", "tensor_copy", sqS[:, h, :], sqT[:, SS:NT], r=[sqTT], w=[sqST])
                            OP("vector", "tensor_copy", skS[:, h, :], skT[:, SS:NT], r=[skTT], w=[sqST])
                            for qb in range(16):
                                qc = slice(qb * 128, (qb + 1) * 128)
                                sb_core(h, sqT[:, qc], sqTT, (qb + 1) * 128, skT, skTT,
                                        lambda kt, h=h: (svtm[:, kt, h * 128:(h + 1) * 128], svtmT), OT[:, h, qc])
                        OP("vector", "memset", OT[:, :, SS:NT], 0.0, w=[OTT])
                        S.barrier()
                    if do_sample_sb:
                        sst = ExitStack()
                        with sst:
                            Kb = sb(sst, "Kb", [128, 16, 512], BF16); KbT = [T("Kb%d" % i) for i in range(16)]
                            Vb = sb(sst, "Vb", [128, 16, 512], BF16); VbT = [T("Vb%d" % i) for i in range(16)]
                            KT = [sb(sst, "KT%d" % i, [128, 2048], BF16) for i in range(2)]
                            KTT = [T("KT%d" % i) for i in range(2)]
                            qpad = [sb(sst, "qpad%d" % i, [128, 96], BF16) for i in range(2)]
                            qpadT = [T("qpad%d" % i) for i in range(2)]
                            E16 = sb(sst, "E16", [4, 16], BF16); E16T = T("E16")
                            OP("vector", "memset", Apad[:], 0.0, w=[ApadT])
                            for i in range(2):
                                OP("vector", "memset", qpad[i][:], 0.0, w=[qpadT[i]])
                            for hh_ in range(4):
                                OP("vector", "tensor_copy", E16[:, hh_ * 4:(hh_ + 1) * 4], identb[0:4, 0:4], r=[idT], w=[E16T])
                            for b in range(16):
                                ix, ixT = idx16[b % 2], idx16T[b % 2]
                                OP("vector", "tensor_scalar", ix[:], ptf[:, b * 16:(b + 1) * 16], float(l * 2560 * 128), None, op0=ALU.add,
                                   r=[ptT], w=[ixT])
                                for pg in range(16):
                                    S.op("gpsimd", (lambda pg, ix: lambda hh: hh.indirect_dma_start(
                                        out=Kb[:, pg, :], out_offset=None, in_=ck,
                                        in_offset=bass.IndirectOffsetOnAxis(ap=ix[:, pg:pg + 1], axis=0)))(pg, ix),
                                        reads=[ixT], writes=[KbT[pg]], dma=True)
                                for pg in range(16):
                                    S.op("gpsimd", (lambda pg, ix: lambda hh: hh.indirect_dma_start(
                                        out=Vb[:, pg, :], out_offset=None, in_=cv,
                                        in_offset=bass.IndirectOffsetOnAxis(ap=ix[:, pg:pg + 1], axis=0)))(pg, ix),
                                        reads=[ixT], writes=[VbT[pg]], dma=True)
                                qp, qpT = qpad[b % 2], qpadT[b % 2]
                                OP("vector", "tensor_copy", qp[:, 0:80].rearrange("p (h x) -> p h x", x=20)[:, :, 0:4], sqS[:, :, b * 4:b * 4 + 4],
                                   r=[sqST], w=[qpT])
                                for h in range(4):
                                    kt_, ktT = KT[h % 2], KTT[h % 2]
                                    for g in range(0, 16, 8):
                                        bank = 5 + ((g // 8) % 2)
                                        pv = PB[bank].bitcast(BF16)
                                        for k in range(8):
                                            TR(pv[:, k * 128:(k + 1) * 128], Kb[:, g + k, h * 128:(h + 1) * 128], identb[:], r=[KbT[g + k], idT], w=[PT[bank]])
                                        OP("vector", "tensor_copy", kt_[:, g * 128:(g + 8) * 128], pv, r=[PT[bank]], w=[ktT])
                                    for kb in range(0, 2048, 512):
                                        MM(pz[0:16, kb:kb + 512], qp[:, h * 16:(h + 1) * 16], kt_[:, kb:kb + 512], h == 0, h == 3, r=[qpT, ktT], w=PT[0:5])
                                    MM(pz[0:16, 2048:2052], qp[:, h * 16:(h + 1) * 16], skS[:, h, b * 4:b * 4 + 4], h == 0, False, r=[qpT, sqST], w=PT[0:5])
                                MM(pz[0:16, 2048:2052], E16[:], mneg[0:4, 0:4], False, True, r=[E16T, cT], w=PT[0:5])
                                nk = 2052
                                ACT(B1[0:16, 0:nk], pz[0:16, 0:nk], AF.Exp, bias=bb16[:, 0:1], r=PT[0:5] + [bb16T], w=[B1T])
                                ACT(B1[0:16, 0:nk], B1[0:16, 0:nk], AF.Ln, bias=1.0, r=[B1T], w=[B1T])
                                OP("vector", "tensor_tensor_scan", out=B2[0:16, 0:nk], data0=ones[0:16, 0:1].to_broadcast([16, nk]), data1=B1[0:16, 0:nk], initial=0.0,
                                   op0=ALU.mult, op1=ALU.add, r=[cT, B1T], w=[B2T])
                                OP("vector", "tensor_tensor", bb16[:, 1:2], bb16[:, 0:1], B2[0:16, nk - 1:nk], op=ALU.subtract, r=[bb16T, B2T], w=[bb16T])
                                OP("vector", "tensor_tensor", B1[0:16, 0:nk], B2[0:16, 0:nk], B1[0:16, 0:nk], op=ALU.subtract, r=[B1T, B2T], w=[B1T])
                                OP("vector", "tensor_tensor", B1[0:16, 0:nk], B1[0:16, 0:nk], pz[0:16, 0:nk], op=ALU.add, r=[B1T] + PT[0:5], w=[B1T])
                                ACT(Ab[0:16, 0:2048], B1[0:16, 0:2048], AF.Exp, bias=bb16[:, 1:2], r=[B1T, bb16T], w=[AbT])
                                ACT(Apad[0:16, b, b * 4:b * 4 + 4], B1[0:16, 2048:2052], AF.Exp, bias=bb16[:, 1:2], r=[B1T, bb16T], w=[ApadT])
                                for g in range(0, 16, 8):
                                    bank = 5 + ((g // 8) % 2)
                                    pv = PB[bank].bitcast(BF16)
                                    for k in range(8):
                                        TR(pv[:, k * 128:k * 128 + 16], Ab[0:16, (g + k) * 128:(g + k + 1) * 128], identb[0:16, 0:16], r=[AbT, idT], w=[PT[bank]])
                                    OP("vector", "tensor_copy", AT_[:, g:g + 8, 0:16], pv.rearrange("p (k n) -> p k n", n=128)[:, :, 0:16], r=[PT[bank]], w=[ATT])
                                pv = PB[5].bitcast(BF16)
                                TR(pv[0:64, 0:16], Apad[0:16, b, :], identb[0:16, 0:16], r=[ApadT, idT], w=[PT[5]])
                                OP("vector", "tensor_copy", AT_[0:64, 16, 0:16], pv[0:64, 0:16], r=[PT[5]], w=[ATT])
                                for h in range(4):
                                    for kt in range(16):
                                        MM(PB[7][:, h * 4:(h + 1) * 4], Vb[:, kt, h * 128:(h + 1) * 128], AT_[:, kt, h * 4:(h + 1) * 4], kt == 0, False,
                                           r=[VbT[kt], ATT], w=[PT[7]])
                                    MM(PB[7][:, h * 4:(h + 1) * 4], svS[0:64, h * 128:(h + 1) * 128], AT_[0:64, 16, h * 4:(h + 1) * 4], False, True,
                                       r=[sqST, ATT], w=[PT[7]])
                                ACT(OT[:, :, SS + b * 4:SS + b * 4 + 4], PB[7][:, 0:16].rearrange("p (h t) -> p h t", t=4), AF.Copy, r=[PT[7]], w=[OTT])
                            S.barrier()
                    merge(1)
                    S.barrier()
                    if stop == "sb":
                        S.emit(); return nc

                for ch in range(2):
                    wo, woT = wload(w_out[l, ch], 8, 512)
                    for tt in range(NTT):
                        ts_ = slice(tt * 128, (tt + 1) * 128)
                        bank = tt % 4
                        xt, xtT = xio[tt % 2], xioT[tt % 2]
                        DMA("sync", xt[:, 0:512], xsrc[ts_, ch * 512:(ch + 1) * 512], r=([xsrcT[tt]] if xsrcT else []), w=[xtT])
                        proj_tm(wo, woT, 512, tt, bank, src=FB, srcT=FBT)
                        OP("vector", "tensor_tensor", xt[:, 0:512], xt[:, 0:512], PB[bank], op=ALU.add, r=[xtT, PT[bank]], w=[xtT])
                        DMA("sync", xA[l][ts_, ch * 512:(ch + 1) * 512], xt[:, 0:512], r=[xtT], w=[xAT[l][tt]])
                S.barrier()
                if stop == "wout":
                    S.emit(); return nc

            fst = ExitStack()
            with fst:
                X = sb(fst, "X", [128, NTT, D], F32)
                XT = [T("X%d" % i) for i in range(NTT)]
                GE2 = [sb(fst, "GE%d" % i, [128, 2 + 2048], F32) for i in range(2)]; GE2T = [T("GE") for i in range(2)]
                GS2 = [sb(fst, "GS%d" % i, [128, 16, 6], F32) for i in range(2)]; GS2T = [T("GS") for i in range(2)]
                GC2 = [sb(fst, "GC%d" % i, [128, NT], F32) for i in range(2)]; GC2T = [T("GC") for i in range(2)]
                HT = sb(fst, "HT", [128, 2, NT], BF16); HTT = T("HT")
                sfc_tm = sb(fst, "sfc_tm", [32, DFF], F32); sfcT = T("sfc")
                fv = sb(fst, "fv", [128, 22, 4], F32); fvT = T("fv")
                gtm = sb(fst, "gtm", [128, 256], F32); gtmT = T("gtm")
                DMA("sync", sfc_tm[:], sfc[l], w=[sfcT])
                DMA("sync", fv[:], v_ffn[l], w=[fvT])
                for i_ in range(2):
                    OP("vector", "memset", GE2[i_][:, 0:2], 0.0, w=[GE2T[i_]])
                    OP("vector", "memset", GC2[i_][:, 2112:NT], 0.0, w=[GC2T[i_]])
                DMA("sync", gB[:], v_norm[2 * l + 1].partition_broadcast(128), w=[gBT])
                for tt in range(NTT):
                    DMA("sync", X[:, tt, :], xA[l][tt * 128:(tt + 1) * 128, :], r=[xAT[l][tt]], w=[XT[tt]])
                    rmsnorm_stats(X[:, tt, :], XT[tt], D)
                    nb, nbT = xnb[tt % 2], xnbT[tt % 2]
                    OP("vector", "scalar_tensor_tensor", out=nb[:], in0=X[:, tt, :], scalar=st4[:, 2:3], in1=gB[:],
                       op0=ALU.mult, op1=ALU.mult, r=[XT[tt], st4T, gBT], w=[nbT])
                    bank = 5 + (tt % 2)
                    pv = PB[bank].bitcast(BF16)
                    for kc in range(8):
                        TR(pv[:, kc * 128:(kc + 1) * 128], nb[:, kc * 128:(kc + 1) * 128], identb[:], r=[nbT, idT], w=[PT[bank]])
                    ACT(FB[:, :, tt * 128:(tt + 1) * 128], pv.rearrange("p (k n) -> p k n", n=128), AF.Copy, r=[PT[bank]], w=[FBT])
                for gi in range(NG):
                    wg, wgT = wload(w_gate[l, gi], 8, 256)
                    wu, wuT = wload(w_up[l, gi], 8, 256)
                    wd, wdT = wload(w_down[l, gi], 2, 1024)
                    for (tt, kind) in ((15, "p"), (16, "s")):
                        proj_tm(wg, wgT, 256, tt, 4, src=FB, srcT=FBT)
                        OP("vector", "tensor_copy", gtm[:], PB[4][:, 0:256], r=[PT[4]], w=[gtmT])
                        if kind == "p":
                            DMA("sync", ffnc[l, 0, :, gi * 256:(gi + 1) * 256], gtm[126:128, :], r=[gtmT])
                        else:
                            for i in range(2):
                                DMA("sync", ffnc[l, 1:17, i, gi * 256:(gi + 1) * 256], gtm[2 + i:64:4, :], r=[gtmT])
                    for c2 in range(2):
                        gc = gi * 2 + c2
                        GE, GET, GS, GST, GC, GCT = GE2[c2], GE2T[c2], GS2[c2], GS2T[c2], GC2[c2], GC2T[c2]
                        TR(PB[4][:, 256:288], sfc_tm[0:32, gc * 128:(gc + 1) * 128], identf[0:32, 0:32], r=[sfcT, idT], w=[PT[4]])
                        OP("vector", "tensor_copy", GS[:, :, 0:2], PB[4][:, 256:288].rearrange("p (b i) -> p b i", i=2), r=[PT[4]], w=[GST])
                        for bi, blk in enumerate(BLKS):
                            b0, nn = blk
                            bank = bi % 2
                            proj_fm(wg, wgT, c2 * 128, blk, bank, src=FB, srcT=FBT)
                            if bi < 4:
                                ACT(GE[:, 2 + b0:2 + b0 + nn], PB[bank][:, 0:nn], AF.Copy, r=[PT[bank]], w=[GET])
                            else:
                                ACT(GS[:, :, 2:6], PB[bank][:, 0:64].rearrange("p (b t) -> p b t", t=4), AF.Copy, r=[PT[bank]], w=[GST])
                    for c2 in range(2):
                        gc = gi * 2 + c2
                        GE, GET, GS, GST, GC, GCT = GE2[c2], GE2T[c2], GS2[c2], GS2T[c2], GC2[c2], GC2T[c2]
                        GCs = GC[:, SS:SS + 64].rearrange("p (b t) -> p b t", t=4)
                        OP("vector", "tensor_scalar", GC[:, 0:2048], GE[:, 0:2048], fv[:, gc, 0:1], fv[:, gc, 3:4], op0=ALU.mult, op1=ALU.add,
                           r=[GET, fvT], w=[GCT])
                        OP("vector", "tensor_scalar", GCs, GS[:, :, 0:4], fv[:, gc, 0:1], fv[:, gc, 3:4], op0=ALU.mult, op1=ALU.add,
                           r=[GST, fvT], w=[GCT])
                        for i in range(1, 3):
                            OP("vector", "scalar_tensor_tensor", out=GC[:, 0:2048], in0=GE[:, i:i + 2048], scalar=fv[:, gc, i:i + 1], in1=GC[:, 0:2048],
                               op0=ALU.mult, op1=ALU.add, r=[GET, fvT, GCT], w=[GCT])
                            OP("vector", "scalar_tensor_tensor", out=GCs, in0=GS[:, :, i:i + 4], scalar=fv[:, gc, i:i + 1], in1=GCs,
                               op0=ALU.mult, op1=ALU.add, r=[GST, fvT, GCT], w=[GCT])
                        ACT(GC[:], GC[:], AF.Gelu_apprx_tanh, r=[GCT], w=[GCT])
                    for c2 in range(2):
                        GC, GCT = GC2[c2], GC2T[c2]
                        for bi, blk in enumerate(BLKS):
                            b0, nn = blk
                            bank = 2 + (c2 * 5 + bi) % 4
                            proj_fm(wu, wuT, c2 * 128, blk, bank, src=FB, srcT=FBT)
                            OP("vector", "tensor_tensor", HT[:, c2, b0:b0 + nn], GC[:, b0:b0 + nn], PB[bank][:, 0:nn], op=ALU.mult,
                               r=[GCT, PT[bank]], w=[HTT])
                    for tt in range(NTT):
                        for ch in range(2):
                            bank = 6 + (tt * 2 + ch) % 2
                            for c2 in range(2):
                                MM(PB[bank], HT[:, c2, tt * 128:(tt + 1) * 128], wd[:, c2, ch * 512:(ch + 1) * 512], c2 == 0, c2 == 1,
                                   r=[HTT, wdT], w=[PT[bank]])
                            OP("vector", "tensor_tensor", X[:, tt, ch * 512:(ch + 1) * 512], X[:, tt, ch * 512:(ch + 1) * 512], PB[bank], op=ALU.add,
                               r=[XT[tt], PT[bank]], w=[XT[tt]])
                if l < L - 1:
                    for tt in range(NTT):
                        DMA("sync", xB[l][tt * 128:(tt + 1) * 128, :], X[:, tt, :], r=[XT[tt]], w=[xBT[l][tt]])
                else:
                    DMA("sync", gB[:], v_norm[2 * L].partition_broadcast(128), w=[gBT])
                    for tt in range(NTT):
                        rmsnorm_stats(X[:, tt, :], XT[tt], D)
                        xt, xtT = xio[tt % 2], xioT[tt % 2]
                        OP("vector", "scalar_tensor_tensor", out=xt[:], in0=X[:, tt, :], scalar=st4[:, 2:3], in1=gB[:],
                           op0=ALU.mult, op1=ALU.mult, r=[XT[tt], st4T, gBT], w=[xtT])
                        DMA("sync", y[tt * 128:(tt + 1) * 128, :], xt[:], r=[xtT])
                S.barrier()
        S.emit()
    return nc


def _consts():
    pos = np.concatenate([np.arange(2048), np.tile(2048 + np.arange(4), 16), np.zeros(64)]).astype(np.float32)
    half = 64
    inv = (np.float32(10000.0) ** (-np.arange(half, dtype=np.float32) / np.float32(half))).astype(np.float32)
    ang = (pos[None, :] * inv[:, None]).astype(np.float32)
    cos = np.cos(ang).astype(np.float32)
    sin = np.sin(ang).astype(np.float32)
    c_cos = np.concatenate([cos, cos], 0)
    c_sin = np.concatenate([-sin, sin], 0)
    lg = np.log(np.array(GAM, dtype=np.float64))
    idx = np.arange(128)
    dec = np.zeros((128, 8, 128), np.float64)
    qdec = np.zeros((128, 8, 128), np.float64)
    kdec = np.zeros((128, 8), np.float64)
    for h in range(4):
        diff = idx[None, :] - idx[:, None]
        dec[:, h, :] = np.where(diff >= 0, np.exp(np.maximum(diff, 0) * lg[h]), 0.0)
        qdec[:, h, :] = np.exp((idx + 1.0) * lg[h])[None, :]
        kdec[:, h] = np.exp((127.0 - idx) * lg[h])
        for j in range(64):
            for i in range(64):
                if j // 4 == i // 4 and i >= j:
                    dec[j, 4 + h, i] = np.exp((i - j) * lg[h])
        t = idx % 4
        qd = np.exp((t + 1.0) * lg[h]); qd[64:] = 0.0
        qdec[:, 4 + h, :] = qd[None, :]
        kd = np.exp((3.0 - t) * lg[h]); kd[64:] = 0.0
        kdec[:, 4 + h] = kd
    mneg = np.where(idx[None, :] < idx[:, None], 0.0, -30000.0)
    bmask = np.zeros((128, 16, 128), np.float32)
    rmask = np.zeros((128, 16), np.float32)
    for b in range(16):
        bmask[:, b, b * 4:b * 4 + 4] = 1.0
        rmask[b * 4:b * 4 + 4, b] = 1.0
    return dict(c_cos=c_cos, c_sin=c_sin, c_dec=dec.astype(np.float32), c_qdec=qdec.astype(np.float32),
                c_kdec=kdec.astype(np.float32), c_mneg=mneg.astype(np.float32), c_bmask=bmask, c_rmask=rmask,
                c_iota=np.arange(128, dtype=np.float32).reshape(128, 1))


def _blk(w, kc, ncol):
    K, N = w.shape
    return np.ascontiguousarray(w.reshape(kc, 128, N // ncol, ncol).transpose(2, 1, 0, 3))


def _prep(x_prompt, x_sample, cache_sb_k, cache_sb_v, state_ret, state_lru_h, state_lru_conv,
           state_ffn_conv, page_table, norm1, w_in, ret_gn, sb_bias, lru_conv_w, lru_conv_b, lru_w_a, lru_b_a,
           lru_w_x, lru_b_x, lru_lambda, w_br_ret, w_br_sb, w_br_lru, w_out, norm2, w_ffn_gate,
           w_ffn_up, ffn_conv_w, ffn_conv_b, w_ffn_down, norm_f, big_cache=True):
    f = lambda a: np.asarray(a, dtype=np.float32)
    x_prompt, x_sample = f(x_prompt), f(x_sample)
    w_in = f(w_in)
    shared = _consts()
    perm = np.concatenate([np.arange(h * 128, (h + 1) * 128).reshape(2, 64)[::-1].reshape(-1) for h in range(4)])
    w_in_ext = np.concatenate([w_in, w_in[:, :, 0:512][:, :, perm], w_in[:, :, 512:1024][:, :, perm]], axis=2)
    shared["w_in"] = np.stack([_blk(w_in_ext[l], 8, 512) for l in range(L)])
    wbr = np.stack([np.stack([f(w)[l].reshape(4, 128, 1024).transpose(1, 0, 2) for w in (w_br_ret, w_br_sb, w_br_lru)]) for l in range(L)])
    shared["w_br"] = np.ascontiguousarray(wbr)
    shared["w_out"] = np.stack([_blk(f(w_out)[l], 8, 512) for l in range(L)])
    shared["w_gate"] = np.stack([_blk(f(w_ffn_gate)[l], 8, 256) for l in range(L)])
    shared["w_up"] = np.stack([_blk(f(w_ffn_up)[l], 8, 256) for l in range(L)])
    shared["w_down"] = np.ascontiguousarray(f(w_ffn_down).reshape(L, NG, 2, 128, 1024).transpose(0, 1, 3, 2, 4))
    wl = np.concatenate([f(lru_w_a), f(lru_w_x)], axis=1)
    shared["w_lru"] = np.ascontiguousarray(wl.transpose(0, 2, 1, 3))
    shared["v_norm"] = np.concatenate([np.stack([f(norm1)[l], f(norm2)[l]]) for l in range(L)] + [f(norm_f)[None]], 0).reshape(2 * L + 1, 1, D)
    shared["v_gn"] = f(ret_gn).reshape(L, 1, 512)
    shared["v_sbb"] = np.ascontiguousarray(np.concatenate([f(sb_bias), np.zeros((L, 12), np.float32)], axis=1).reshape(L, 1, 16))
    vl = np.concatenate([f(lru_conv_w), f(lru_conv_b)[:, None], f(lru_b_a)[:, None], f(lru_b_x)[:, None], f(lru_lambda)[:, None]], axis=1)
    shared["v_lru"] = np.ascontiguousarray(vl.reshape(L, 8, 4, 128).transpose(0, 3, 2, 1))
    vf = np.concatenate([f(ffn_conv_w), f(ffn_conv_b)[:, None]], axis=1)
    shared["v_ffn"] = np.ascontiguousarray(vf.reshape(L, 4, 22, 128).transpose(0, 3, 2, 1))
    if not big_cache:
        shared["ck"] = np.zeros((128, 512), np.float32)
        shared["cv"] = np.zeros((128, 512), np.float32)
    else:
      shared["ck"] = f(cache_sb_k).reshape(L * 2560 * 128, 512)
      shared["cv"] = f(cache_sb_v).reshape(L * 2560 * 128, 512)
    shared["v_sbb16"] = np.ascontiguousarray(np.repeat(f(sb_bias), 4, axis=1).reshape(L, 16, 1))
    in_maps = []
    for c in range(NCORES):
        bs = slice(16 * c, 16 * c + 16)
        xin = np.zeros((NT, D), np.float32)
        xin[0:2048] = x_prompt[c]
        xin[2048:2112] = x_sample[bs].reshape(64, D)
        m = dict(shared)
        m["xin"] = xin
        m["pt"] = np.ascontiguousarray(np.asarray(page_table)[bs].astype(np.int32).reshape(1, 256))
        m["sret"] = np.ascontiguousarray(f(state_ret)[:, bs])
        m["slh"] = np.ascontiguousarray(f(state_lru_h)[:, bs])
        m["slc"] = np.ascontiguousarray(f(state_lru_conv)[:, bs].reshape(L, 48, 512))
        m["sfc"] = np.ascontiguousarray(f(state_ffn_conv)[:, bs].reshape(L, 32, DFF))
        in_maps.append(m)
    return in_maps


def _assemble(R):
    cat = lambda fn, ax: np.concatenate([fn(r) for r in R], axis=ax)
    y_prompt = np.stack([r["y"][0:2048] for r in R])
    y_sample = cat(lambda r: r["y"][2048:2112].reshape(16, 4, D), 0)
    sbk_p = np.stack([r["sbk"][:, 0:2048].reshape(L, 2048, 4, 128) for r in R], axis=1)
    sbv_p = np.stack([r["sbv"][:, 0:2048].reshape(L, 2048, 4, 128) for r in R], axis=1)
    sbk_s = cat(lambda r: r["sbk"][:, 2048:2112].reshape(L, 16, 4, 4, 128), 1)
    sbv_s = cat(lambda r: r["sbv"][:, 2048:2112].reshape(L, 16, 4, 4, 128), 1)
    ret_p = np.stack([r["rets"][:, 0] for r in R], axis=1)
    ret_s = cat(lambda r: r["rets"][:, 1:17], 1)
    lh_p = np.stack([r["lruh"][:, 0] for r in R], axis=1)
    lh_s = cat(lambda r: r["lruh"][:, 1:17], 1)
    lc_p = np.stack([r["lruc"][:, 0] for r in R], axis=1)
    lc_s = cat(lambda r: r["lruc"][:, 1:17], 1)
    fc_p = np.stack([r["ffnc"][:, 0] for r in R], axis=1)
    fc_s = cat(lambda r: r["ffnc"][:, 1:17], 1)
    outs = (y_prompt, y_sample, sbk_p, sbv_p, sbk_s, sbv_s, ret_p, ret_s, lh_p, lh_s, lc_p, lc_s, fc_p, fc_s)
    return tuple(np.ascontiguousarray(o, dtype=np.float32) for o in outs)


def kernel(**inputs):
    in_maps = _prep(**inputs)
    nc = build_nc()
    res = run_bass_kernel_spmd(nc, in_maps, core_ids=list(range(NCORES)))
    return _assemble(res.results)
```

```python
import numpy as np
from contextlib import ExitStack
import concourse.bass as bass
import concourse.mybir as mybir
from concourse.bass_utils import run_bass_kernel_spmd

F32 = mybir.dt.float32
BF16 = mybir.dt.bfloat16
I32 = mybir.dt.int32
AF = mybir.ActivationFunctionType
ALU = mybir.AluOpType
AX = mybir.AxisListType

NCORES = 8
L = 2
D = 1024
NT = 2176
NTT = 17
SS = 2048
BLKS = [(0, 512), (512, 512), (1024, 512), (1536, 512), (2048, 128)]
DFF = 2816
NG = 11
EPS = 1e-6
GAM = [1.0 - 2.0 ** (-5.0 - h) for h in range(4)]
import os as _os
SAME_ENGINE_SYNC = _os.environ.get("SES", "1") == "1"


class T:
    __slots__ = ("name", "w", "r", "excl")

    def __init__(self, name, excl=False):
        self.name = name
        self.w = None
        self.r = []
        self.excl = excl


class Sched:
    ENGS = ("sync", "scalar", "vector", "gpsimd", "tensor")

    def __init__(self, nc, stack, n_dma_sems=8):
        self.nc = nc
        self.recs = []
        self.sem = {e: stack.enter_context(nc.semaphore("s_" + e)) for e in self.ENGS}
        self.nds = n_dma_sems
        self.dsem = {e: [stack.enter_context(nc.semaphore("d_%s%d" % (e, i))) for i in range(n_dma_sems)]
                     for e in ("sync", "gpsimd")}
        self.ndma = {"sync": 0, "gpsimd": 0}
        self.dma_hist = {"sync": [], "gpsimd": []}
        self.last = {e: None for e in self.ENGS}

    def op(self, eng, fn, reads=(), writes=(), dma=False, extra=()):
        idx = len(self.recs)
        waits = set(extra)
        for t in reads:
            if t.w is not None:
                waits.add(t.w)
            if t.excl:
                waits.update(t.r)
        for t in writes:
            if t.w is not None:
                waits.add(t.w)
            waits.update(t.r)
        rec = dict(eng=eng, fn=fn, waits=waits, dma=dma, sig=False, dq=None)
        if dma:
            q = self.ndma[eng]
            self.ndma[eng] += 1
            rec["dq"] = q
            h = self.dma_hist[eng]
            if q >= self.nds:
                waits.add(h[q - self.nds])
            h.append(idx)
        self.recs.append(rec)
        for t in reads:
            t.r.append(idx)
        for t in writes:
            t.w = idx
            t.r = []
        if fn is not None:
            self.last[eng] = idx
        return idx

    def barrier(self):
        pend = set()
        for e in self.ENGS:
            if self.last[e] is not None:
                pend.add(self.last[e])
        for e in ("sync", "gpsimd"):
            pend.update(self.dma_hist[e][-self.nds:])
        for e in self.ENGS:
            self.op(e, None, extra=pend)

    def emit(self):
        nc = self.nc
        recs = self.recs
        for r in recs:
            keep = set()
            for w in r["waits"]:
                rw = recs[w]
                if rw["fn"] is None:
                    continue
                if rw["eng"] == r["eng"] and not rw["dma"]:
                    if r["eng"] in ("tensor", "sync"):
                        continue
                    if not SAME_ENGINE_SYNC:
                        continue
                keep.add(w)
            r["waits"] = keep
            for w in keep:
                recs[w]["sig"] = True
        cnt = {e: 0 for e in self.ENGS}
        for r in recs:
            if r["dma"]:
                q = r["dq"]
                r["sem"] = self.dsem[r["eng"]][q % self.nds]
                r["val"] = 16 * (q // self.nds + 1)
            elif r["sig"]:
                cnt[r["eng"]] += 1
                r["sem"] = self.sem[r["eng"]]
                r["val"] = cnt[r["eng"]]
        per = {e: [r for r in recs if r["eng"] == e] for e in self.ENGS}
        print('SEMCOUNTS', cnt, dict(self.ndma), {e: len(per[e]) for e in self.ENGS})

        def run(e, h):
            seen = {}
            for r in per[e]:
                for w in sorted(r["waits"]):
                    rw = recs[w]
                    key = id(rw["sem"])
                    if seen.get(key, 0) >= rw["val"]:
                        continue
                    seen[key] = rw["val"]
                    h.wait_ge(rw["sem"], rw["val"])
                if r["fn"] is None:
                    continue
                ins = r["fn"](h)
                if r["dma"] or r["sig"]:
                    ins.then_inc(r["sem"], 16 if r["dma"] else 1)
            if e in self.ndma:
                n = self.ndma[e]
                for s in range(min(n, self.nds)):
                    last_q = ((n - 1 - s) // self.nds) * self.nds + s
                    h.wait_ge(self.dsem[e][s], 16 * (last_q // self.nds + 1))

        with nc.Block() as block:
            @block.sync
            def _(h):
                run("sync", h)

            @block.scalar
            def _(h):
                run("scalar", h)

            @block.vector
            def _(h):
                run("vector", h)

            @block.gpsimd
            def _(h):
                run("gpsimd", h)

            @block.tensor
            def _(h):
                run("tensor", h)


def build_nc(n_layers=L, do_sample_sb=True, stop=None, big_cache=True):
    nc = bass.Bass("TRN2", target_bir_lowering=False)
    gstack = ExitStack()
    with gstack:
        S = Sched(nc, gstack)

        def din(name, shape, dt=F32):
            return nc.dram_tensor(name, list(shape), dt, kind="ExternalInput").ap()

        def dout(name, shape):
            return nc.dram_tensor(name, list(shape), F32, kind="ExternalOutput").ap()

        xin = din("xin", [NT, D])
        ck = din("ck", [L * 2560 * 128 if big_cache else 128, 512])
        cv = din("cv", [L * 2560 * 128 if big_cache else 128, 512])
        ptd = din("pt", [1, 256], I32)
        sret = din("sret", [L, 16, 4, 128, 128])
        slh = din("slh", [L, 16, 512])
        slc = din("slc", [L, 48, 512])
        sfc = din("sfc", [L, 32, DFF])
        w_in = din("w_in", [L, 16, 128, 8, 512])
        w_br = din("w_br", [L, 3, 128, 4, 1024])
        w_out = din("w_out", [L, 2, 128, 8, 512])
        w_gate = din("w_gate", [L, NG, 128, 8, 256])
        w_up = din("w_up", [L, NG, 128, 8, 256])
        w_down = din("w_down", [L, NG, 128, 2, 1024])
        w_lru = din("w_lru", [L, 128, 8, 128])
        v_norm = din("v_norm", [2 * L + 1, 1, D])
        v_gn = din("v_gn", [L, 1, 512])
        v_sbb = din("v_sbb", [L, 1, 16])
        v_sbb16 = din("v_sbb16", [L, 16, 1])
        v_lru = din("v_lru", [L, 128, 4, 8])
        v_ffn = din("v_ffn", [L, 128, 22, 4])
        c_cos = din("c_cos", [128, NT])
        c_sin = din("c_sin", [128, NT])
        c_dec = din("c_dec", [128, 8, 128])
        c_qdec = din("c_qdec", [128, 8, 128])
        c_kdec = din("c_kdec", [128, 8])
        c_mneg = din("c_mneg", [128, 128])
        c_bmask = din("c_bmask", [128, 16, 128])
        c_rmask = din("c_rmask", [128, 16])
        c_iota = din("c_iota", [128, 1])

        y = dout("y", [NT, D])
        sbk = dout("sbk", [L, NT, 512])
        sbv = dout("sbv", [L, NT, 512])
        rets = dout("rets", [L, 17, 4, 128, 128])
        lruh = dout("lruh", [L, 17, 512])
        lruc = dout("lruc", [L, 17, 3, 512])
        ffnc = dout("ffnc", [L, 17, 2, DFF])
        xA = [nc.dram_tensor("xA%d" % l, [NT, D], F32, kind="Internal").ap() for l in range(L)]
        xB = [nc.dram_tensor("xB%d" % l, [NT, D], F32, kind="Internal").ap() for l in range(L)]
        xAT = [[T("xA") for _ in range(NTT)] for l in range(L)]
        xBT = [[T("xB") for _ in range(NTT)] for l in range(L)]

        uid = [0]

        def sb(st, name, shape, dt):
            uid[0] += 1
            return st.enter_context(nc.sbuf_tensor("%s_%d" % (name, uid[0]), list(shape), dt))

        def OP(eng, name, *args, r=(), w=(), **kw):
            S.op(eng, lambda h: getattr(h, name)(*args, **kw), reads=r, writes=w)

        def DMA(eng, out, in_, r=(), w=(), **kw):
            S.op(eng, lambda h: h.dma_start(out=out, in_=in_, **kw), reads=r, writes=w, dma=True)

        def MM(out, lhsT, rhs, start, stop, r=(), w=()):
            S.op("tensor", lambda h: h.matmul(out, lhsT=lhsT, rhs=rhs, start=start, stop=stop), reads=r, writes=w)

        def TR(out, in_, ident, r=(), w=()):
            S.op("tensor", lambda h: h.transpose(out, in_, ident), reads=r, writes=w)

        def ACT(out, in_, func, r=(), w=(), **kw):
            S.op("scalar", lambda h: h.activation(out=out, in_=in_, func=func, **kw), reads=r, writes=w)

        pz = gstack.enter_context(nc.psum_tensor("pz", [128, 2560], F32))
        pbs = [gstack.enter_context(nc.psum_tensor("pb%d" % i, [128, 512], F32)) for i in range(3)]
        PB = [pz[:, k * 512:(k + 1) * 512] for k in range(5)] + [p[:, :] for p in pbs]
        PT = [T("pbank%d" % i, excl=True) for i in range(8)]
        pzT = T("pz")

        NRING = 3
        ring = [sb(gstack, "wr%d" % i, [128, 4096], BF16) for i in range(NRING)]
        ringT = [T("wr%d" % i) for i in range(NRING)]
        rstate = {"i": 0}

        def wload(src, kc, ncol):
            i = rstate["i"] % NRING
            rstate["i"] += 1
            view = ring[i][:, 0:kc * ncol].rearrange("p (k n) -> p k n", n=ncol)
            DMA("gpsimd", view, src, w=[ringT[i]])
            return view, ringT[i]

        FB = sb(gstack, "FB", [128, 8, NT], BF16)
        FBT = T("FB")
        xio = [sb(gstack, "xio%d" % i, [128, D], F32) for i in range(2)]
        xioT = [T("xio%d" % i) for i in range(2)]
        xnb = [sb(gstack, "xnb%d" % i, [128, D], BF16) for i in range(2)]
        xnbT = [T("xnb%d" % i) for i in range(2)]
        gB = sb(gstack, "gB", [128, D], F32)
        gBT = T("gB")
        identb = sb(gstack, "identb", [128, 128], BF16)
        identf = sb(gstack, "identf", [128, 128], F32)
        idT = T("ident")
        mneg = sb(gstack, "mneg", [128, 128], BF16)
        mnegf = sb(gstack, "mnegf", [128, 128], F32)
        ones = sb(gstack, "ones", [128, 1], F32)
        cT = T("consts")
        st4 = sb(gstack, "st4", [128, 16], F32)
        st4T = T("st4")

        OP("gpsimd", "memset", identf[:], 1.0, w=[idT])
        OP("gpsimd", "affine_select", out=identf[:], in_=identf[:], pattern=[[-1, 128]], compare_op=ALU.is_equal,
           fill=0.0, base=0, channel_multiplier=1, r=[idT], w=[idT])
        OP("vector", "tensor_copy", identb[:], identf[:], r=[idT], w=[idT])
        DMA("sync", mnegf[:], c_mneg, w=[cT])
        OP("vector", "tensor_copy", mneg[:], mnegf[:], r=[cT], w=[cT])
        OP("vector", "memset", ones[:], 1.0, w=[cT])

        def rmsnorm_stats(xt, xtT, ncols):
            junk = xnb[0]
            ACT(sqs[:, 0:ncols], xt, AF.Square, accum_out=st4[:, 0:1], r=[xtT], w=[sqsT, st4T])
            ACT(st4[:, 1:2], st4[:, 0:1], AF.Sqrt, bias=eps_t[:, 0:1], scale=1.0 / ncols, r=[st4T, cT], w=[st4T])
            OP("vector", "reciprocal", st4[:, 2:3], st4[:, 1:2], r=[st4T], w=[st4T])

        sqs = sb(gstack, "sqs", [128, D], BF16)
        sqsT = T("sqs")
        eps_t = sb(gstack, "eps_t", [128, 1], F32)
        OP("vector", "memset", eps_t[:], EPS, w=[cT])

        def norm_to_FB(src_tile_fn, l_norm_idx):
            DMA("sync", gB[:], v_norm[l_norm_idx].partition_broadcast(128), w=[gBT])
            for tt in range(NTT):
                xt, xtT = src_tile_fn(tt)
                rmsnorm_stats(xt, xtT, D)
                nb, nbT = xnb[tt % 2], xnbT[tt % 2]
                OP("vector", "scalar_tensor_tensor", out=nb[:], in0=xt, scalar=st4[:, 2:3], in1=gB[:],
                   op0=ALU.mult, op1=ALU.mult, r=[xtT, st4T, gBT], w=[nbT])
                bank = 5 + (tt % 2)
                pv = PB[bank].bitcast(BF16)
                for kc in range(8):
                    TR(pv[:, kc * 128:(kc + 1) * 128], nb[:, kc * 128:(kc + 1) * 128], identb[:], r=[nbT, idT], w=[PT[bank]])
                ACT(FB[:, :, tt * 128:(tt + 1) * 128], pv.rearrange("p (k n) -> p k n", n=128), AF.Copy,
                    r=[PT[bank]], w=[FBT])

        ptf = sb(gstack, "ptf", [128, 256], F32)
        idx16 = [sb(gstack, "idx16_%d" % i, [128, 16], I32) for i in range(2)]
        idx16T = [T("idx16") for i in range(2)]
        ptT = T("pt")
        with ExitStack() as tst:
            ptb = sb(tst, "ptb", [128, 256], I32)
            iof = sb(tst, "iof", [128, 1], F32)
            DMA("sync", ptb[:], ptd.partition_broadcast(128), w=[ptT])
            DMA("sync", iof[:], c_iota, w=[ptT])
            OP("vector", "tensor_copy", ptf[:], ptb[:], r=[ptT], w=[ptT])
            OP("vector", "scalar_tensor_tensor", out=ptf[:], in0=ptf[:], scalar=128.0, in1=iof[:].to_broadcast([128, 256]),
               op0=ALU.mult, op1=ALU.add, r=[ptT], w=[ptT])
            S.barrier()

        for l in range(n_layers):
            mst = ExitStack()
            with mst:
                xsrc = xin if l == 0 else xB[l - 1]
                xsrcT = None if l == 0 else xBT[l - 1]
                FA = sb(mst, "FA", [128, 8, NT], BF16)
                FAT = T("FA")
                OT = sb(mst, "OT", [128, 4, NT], BF16)
                OTT = T("OT")

                def src1(tt, xsrc=xsrc, xsrcT=xsrcT):
                    xt, xtT = xio[tt % 2], xioT[tt % 2]
                    DMA("sync", xt[:], xsrc[tt * 128:(tt + 1) * 128, :], r=([xsrcT[tt]] if xsrcT else []), w=[xtT])
                    return xt[:], xtT
                DMA("sync", gB[:], v_norm[2 * l].partition_broadcast(128), w=[gBT])
                for tt in range(NTT):
                    xt, xtT = src1(tt)
                    rmsnorm_stats(xt, xtT, D)
                    nb, nbT = xnb[tt % 2], xnbT[tt % 2]
                    OP("vector", "scalar_tensor_tensor", out=nb[:], in0=xt, scalar=st4[:, 2:3], in1=gB[:],
                       op0=ALU.mult, op1=ALU.mult, r=[xtT, st4T, gBT], w=[nbT])
                    bank = 5 + (tt % 2)
                    pv = PB[bank].bitcast(BF16)
                    for kc in range(8):
                        TR(pv[:, kc * 128:(kc + 1) * 128], nb[:, kc * 128:(kc + 1) * 128], identb[:], r=[nbT, idT], w=[PT[bank]])
                    ACT(FA[:, :, tt * 128:(tt + 1) * 128], pv.rearrange("p (k n) -> p k n", n=128), AF.Copy,
                        r=[PT[bank]], w=[FAT])

                def proj_fm(wv, wT, c0, blk, bank, src=FA, srcT=FAT, kcs=8):
                    b0, n = blk
                    for kc in range(kcs):
                        MM(PB[bank][:, 0:n], wv[:, kc, c0:c0 + 128], src[:, kc, b0:b0 + n], kc == 0, kc == kcs - 1,
                           r=[wT, srcT], w=[PT[bank]])

                def proj_tm(wv, wT, ncol, tt, bank, src=FA, srcT=FAT, kcs=8, c0=0):
                    for kc in range(kcs):
                        MM(PB[bank][:, 0:ncol], src[:, kc, tt * 128:(tt + 1) * 128], wv[:, kc, c0:c0 + ncol], kc == 0, kc == kcs - 1,
                           r=[wT, srcT], w=[PT[bank]])

                first_merge = [True]

                def merge(br):
                    wb, wbT = wload(w_br[l, br], 4, 1024)
                    for half in range(2):
                        wg, wgT = wload(w_in[l, 8 + 2 * br + half], 8, 512)
                        for jj in range(4):
                            j = half * 4 + jj
                            for bi, blk in enumerate(BLKS):
                                b0, n = blk
                                b1, b2 = (0, 1) if (bi % 2 == 0) else (2, 3)
                                proj_fm(wb, wbT, j * 128, blk, b1, src=OT, srcT=OTT, kcs=4)
                                proj_fm(wg, wgT, jj * 128, blk, b2)
                                gt, gtT = mtmp[bi % 2], mtmpT[bi % 2]
                                ACT(gt[:, 0:n], PB[b2][:, 0:n], AF.Sigmoid, r=[PT[b2]], w=[gtT])
                                if first_merge[0]:
                                    OP("vector", "tensor_tensor", FB[:, j, b0:b0 + n], gt[:, 0:n], PB[b1][:, 0:n], op=ALU.mult,
                                       r=[gtT, PT[b1]], w=[FBT])
                                else:
                                    OP("vector", "tensor_tensor", gt[:, 0:n], gt[:, 0:n], PB[b1][:, 0:n], op=ALU.mult,
                                       r=[gtT, PT[b1]], w=[gtT])
                                    OP("vector", "tensor_tensor", FB[:, j, b0:b0 + n], FB[:, j, b0:b0 + n], gt[:, 0:n], op=ALU.add,
                                       r=[gtT, FBT], w=[FBT])
                    first_merge[0] = False

                mtmp = [sb(mst, "mtmp%d" % i, [128, 512], F32) for i in range(2)]
                mtmpT = [T("mtmp%d" % i) for i in range(2)]

                if stop == "norm1":
                    S.barrier(); S.emit(); return nc
                bst = ExitStack()
                with bst:
                    EXT = sb(bst, "EXT", [128, 3 + 2048], F32); EXTT = T("EXT")
                    EXS = sb(bst, "EXS", [128, 16, 7], F32); EXST = T("EXS")
                    XC = sb(bst, "XC", [128, NT], F32); XCT = T("XC")
                    XCb = sb(bst, "XCb", [128, NT], BF16); XCbT = T("XCb")
                    RA = sb(bst, "RA", [128, NT], F32); RAT = T("RA")
                    IB = sb(bst, "IB", [128, NT], F32); IBT = T("IB")
                    A2 = sb(bst, "A2", [128, NT], F32); A2T = T("A2")
                    slc_tm = sb(bst, "slc_tm", [48, 512], F32); slcT = T("slc")
                    slh_tm = sb(bst, "slh_tm", [16, 512], F32); slhT = T("slh")
                    H0 = sb(bst, "H0", [128, 16], F32); H0T = T("H0")
                    HL = sb(bst, "HL", [128, 4, 17], F32); HLT = T("HL")
                    hl_tm = sb(bst, "hl_tm", [17, 512], F32); hltT = T("hl_tm")
                    lv = sb(bst, "lv", [128, 4, 8], F32); lvT = T("lv")
                    lc = sb(bst, "lc", [128, 4, 4], F32); lcT = T("lc")
                    lxtm = sb(bst, "lxtm", [128, 512], F32); lxtmT = T("lxtm")
                    wlr = sb(bst, "wlr", [128, 8, 128], BF16); wlrT = T("wlr")

                    DMA("sync", slc_tm[:], slc[l], w=[slcT])
                    DMA("sync", slh_tm[:], slh[l], w=[slhT])
                    DMA("sync", lv[:], v_lru[l], w=[lvT])
                    DMA("gpsimd", wlr[:], w_lru[l], w=[wlrT])
                    OP("vector", "memset", EXT[:, 0:3], 0.0, w=[EXTT])
                    OP("vector", "memset", XC[:, 2112:NT], 0.0, w=[XCT])
                    ACT(lc[:, :, 0], lv[:, :, 7], AF.Exp, scale=-1.0, r=[lvT], w=[lcT])
                    ACT(lc[:, :, 0], lc[:, :, 0], AF.Ln, bias=1.0, r=[lcT], w=[lcT])
                    OP("vector", "tensor_scalar", lc[:, :, 1], lc[:, :, 0], -8.0, None, op0=ALU.mult, r=[lcT], w=[lcT])
                    OP("vector", "tensor_scalar", lc[:, :, 2], lc[:, :, 0], -16.0, None, op0=ALU.mult, r=[lcT], w=[lcT])

                    wlx, wlxT = wload(w_in[l, 7], 8, 512)
                    for (tt, kind) in ((15, "p"), (16, "s")):
                        proj_tm(wlx, wlxT, 512, tt, 4)
                        OP("vector", "tensor_copy", lxtm[:], PB[4], r=[PT[4]], w=[lxtmT])
                        if kind == "p":
                            DMA("sync", lruc[l, 0], lxtm[125:128, :], r=[lxtmT])
                        else:
                            for i in range(3):
                                DMA("sync", lruc[l, 1:17, i, :], lxtm[1 + i:64:4, :], r=[lxtmT])
                    for n in range(4):
                        EXSv = EXS
                        TR(PB[4][:, 0:48], slc_tm[0:48, n * 128:(n + 1) * 128], identf[0:48, 0:48], r=[slcT, idT], w=[PT[4]])
                        OP("vector", "tensor_copy", EXS[:, :, 0:3], PB[4][:, 0:48].rearrange("p (b i) -> p b i", i=3), r=[PT[4]], w=[EXST])
                        TR(PB[4][:, 64:80], slh_tm[0:16, n * 128:(n + 1) * 128], identf[0:16, 0:16], r=[slhT, idT], w=[PT[4]])
                        OP("vector", "tensor_copy", H0[:], PB[4][:, 64:80], r=[PT[4]], w=[H0T])
                        for bi, blk in enumerate(BLKS):
                            b0, nn = blk
                            bank = bi % 4
                            proj_fm(wlx, wlxT, n * 128, blk, bank)
                            if bi < 4:
                                ACT(EXT[:, 3 + b0:3 + b0 + nn], PB[bank][:, 0:nn], AF.Copy, r=[PT[bank]], w=[EXTT])
                            else:
                                ACT(EXS[:, :, 3:7], PB[bank][:, 0:64].rearrange("p (b t) -> p b t", t=4), AF.Copy, r=[PT[bank]], w=[EXST])
                        XCs = XC[:, SS:SS + 64].rearrange("p (b t) -> p b t", t=4)
                        OP("vector", "tensor_scalar", XC[:, 0:2048], EXT[:, 0:2048], lv[:, n, 0:1], lv[:, n, 4:5], op0=ALU.mult, op1=ALU.add,
                           r=[EXTT, lvT], w=[XCT])
                        OP("vector", "tensor_scalar", XCs, EXS[:, :, 0:4], lv[:, n, 0:1], lv[:, n, 4:5], op0=ALU.mult, op1=ALU.add,
                           r=[EXST, lvT], w=[XCT])
                        for i in range(1, 4):
                            OP("vector", "scalar_tensor_tensor", out=XC[:, 0:2048], in0=EXT[:, i:i + 2048], scalar=lv[:, n, i:i + 1], in1=XC[:, 0:2048],
                               op0=ALU.mult, op1=ALU.add, r=[EXTT, lvT, XCT], w=[XCT])
                            OP("vector", "scalar_tensor_tensor", out=XCs, in0=EXS[:, :, i:i + 4], scalar=lv[:, n, i:i + 1], in1=XCs,
                               op0=ALU.mult, op1=ALU.add, r=[EXST, lvT, XCT], w=[XCT])
                        OP("vector", "tensor_copy", XCb[:], XC[:], r=[XCT], w=[XCbT])
                        for bi, blk in enumerate(BLKS):
                            b0, nn = blk
                            b1, b2 = (0, 1) if bi % 2 == 0 else (2, 3)
                            MM(PB[b1][:, 0:nn], wlr[:, n, :], XCb[:, b0:b0 + nn], True, True, r=[wlrT, XCbT], w=[PT[b1]])
                            MM(PB[b2][:, 0:nn], wlr[:, 4 + n, :], XCb[:, b0:b0 + nn], True, True, r=[wlrT, XCbT], w=[PT[b2]])
                            ACT(RA[:, b0:b0 + nn], PB[b1][:, 0:nn], AF.Sigmoid, bias=lv[:, n, 5:6], r=[PT[b1], lvT], w=[RAT])
                            ACT(IB[:, b0:b0 + nn], PB[b2][:, 0:nn], AF.Sigmoid, bias=lv[:, n, 6:7], r=[PT[b2], lvT], w=[IBT])
                        ACT(A2[:], RA[:], AF.Exp, scale=lc[:, n, 2:3], r=[RAT, lcT], w=[A2T])
                        ACT(RA[:], RA[:], AF.Exp, scale=lc[:, n, 1:2], r=[RAT, lcT], w=[RAT])
                        ACT(A2[:], A2[:], AF.Sqrt, scale=-1.0, bias=1.0, r=[A2T], w=[A2T])
                        OP("vector", "tensor_tensor", IB[:], IB[:], A2[:], op=ALU.mult, r=[IBT, A2T], w=[IBT])
                        OP("vector", "tensor_tensor", IB[:], IB[:], XC[:], op=ALU.mult, r=[IBT, XCT], w=[IBT])
                        RAs = RA[:, SS:SS + 64].rearrange("p (b t) -> p b t", t=4)
                        IBs = IB[:, SS:SS + 64].rearrange("p (b t) -> p b t", t=4)
                        OP("vector", "tensor_tensor", H0[:], H0[:], RAs[:, :, 0], op=ALU.mult, r=[H0T, RAT], w=[H0T])
                        OP("vector", "tensor_tensor", IBs[:, :, 0], IBs[:, :, 0], H0[:], op=ALU.add, r=[H0T, IBT], w=[IBT])
                        OP("vector", "memset", RAs[:, :, 0], 0.0, r=[H0T], w=[RAT])
                        OP("vector", "tensor_tensor_scan", out=XC[:, 0:2048], data0=RA[:, 0:2048], data1=IB[:, 0:2048], initial=0.0,
                           op0=ALU.mult, op1=ALU.add, r=[RAT, IBT], w=[XCT])
                        OP("vector", "tensor_tensor_scan", out=XC[:, SS:SS + 64], data0=RA[:, SS:SS + 64], data1=IB[:, SS:SS + 64], initial=0.0,
                           op0=ALU.mult, op1=ALU.add, r=[RAT, IBT], w=[XCT])
                        ACT(OT[:, n, :], XC[:], AF.Copy, r=[XCT], w=[OTT])
                        OP("vector", "tensor_copy", HL[:, n, 0:1], XC[:, 2047:2048], r=[XCT], w=[HLT])
                        OP("vector", "tensor_copy", HL[:, n, 1:17], XCs[:, :, 3], r=[XCT], w=[HLT])
                    for n in range(4):
                        TR(PB[4][0:17, n * 128:(n + 1) * 128], HL[:, n, :], identf[:], r=[HLT, idT], w=[PT[4]])
                    OP("vector", "tensor_copy", hl_tm[:], PB[4][0:17, :], r=[PT[4]], w=[hltT])
                    DMA("sync", lruh[l], hl_tm[:], r=[hltT])
                    if stop == "lru0":
                        S.barrier(); S.emit(); return nc
                    merge(2)
                    S.barrier()
                    if stop == "lru":
                        S.emit(); return nc

                bst = ExitStack()
                with bst:
                    cosT = sb(bst, "cosT", [128, NT], F32)
                    sinT = sb(bst, "sinT", [128, NT], F32)
                    tabT = T("tab")
                    DMA("sync", cosT[:], c_cos, w=[tabT])
                    DMA("sync", sinT[:], c_sin, w=[tabT])
                    dec = sb(bst, "dec", [128, 8, 128], F32)
                    qdec = sb(bst, "qdec", [128, 8, 128], F32)
                    kdec = sb(bst, "kdec", [128, 8], F32)
                    bmask = sb(bst, "bmask", [128, 16, 128], BF16)
                    rmask = sb(bst, "rmask", [128, 16], F32)
                    gnB = sb(bst, "gnB", [128, 512], F32)
                    DMA("sync", dec[:], c_dec, w=[tabT])
                    DMA("sync", qdec[:], c_qdec, w=[tabT])
                    DMA("sync", kdec[:], c_kdec, w=[tabT])
                    DMA("gpsimd", bmask[:], c_bmask, w=[tabT])
                    DMA("sync", rmask[:], c_rmask, w=[tabT])
                    DMA("sync", gnB[:], v_gn[l].partition_broadcast(128), w=[tabT])
                    qrT = sb(bst, "qrT", [128, NT], BF16); qrTT = T("qrT")
                    q2T = sb(bst, "q2T", [128, NT], BF16); q2TT = T("q2T")
                    krT = sb(bst, "krT", [128, NT], BF16); krTT = T("krT")
                    rt = [sb(bst, "rt%d" % i, [128, 512], F32) for i in range(2)]
                    rtT = [T("rt%d" % i) for i in range(2)]
                    Sf = sb(bst, "Sf", [128, 128], F32); SfT = T("Sf")
                    Sb2 = [sb(bst, "Sb%d" % i, [128, 128], BF16) for i in range(2)]; Sb2T = [T("Sb") for i in range(2)]
                    vb2 = [sb(bst, "vb%d" % i, [128, 128], BF16) for i in range(2)]; vb2T = [T("vb") for i in range(2)]
                    sg2 = [sb(bst, "sg%d" % i, [128, 128], F32) for i in range(2)]; sg2T = [T("sg") for i in range(2)]
                    scb2 = [sb(bst, "scb%d" % i, [128, 128], BF16) for i in range(2)]; scb2T = [T("scb") for i in range(2)]
                    kdb2 = [sb(bst, "kdb%d" % i, [128, 128], BF16) for i in range(2)]; kdb2T = [T("kdb") for i in range(2)]
                    on2 = [sb(bst, "on%d" % i, [128, 128], F32) for i in range(2)]; on2T = [T("on") for i in range(2)]
                    orow2 = [sb(bst, "orow%d" % i, [128, 128], BF16) for i in range(2)]; orow2T = [T("orow") for i in range(2)]
                    gs2 = [sb(bst, "gs%d" % i, [128, 8], F32) for i in range(2)]; gs2T = [T("gs") for i in range(2)]
                    TXsc = [T("Xsc") for i in range(2)]; TXv = [T("Xv") for i in range(2)]; TXo = [T("Xo") for i in range(2)]
                    TYd = [T("Yd") for i in range(2)]; TYk = [T("Yk") for i in range(2)]; TYo = [T("Yo") for i in range(2)]
                    Q2p = sb(bst, "Q2p", [128, 16, 128], BF16); Q2pT = T("Q2p")
                    kdp = Q2p; kdpT = Q2pT
                    SSf = sb(bst, "SSf", [128, 16, 128], F32); SSfT = T("SSf")
                    SSb = sb(bst, "SSb", [128, 16, 128], BF16); SSbT = T("SSb")

                    for h in range(4):
                        hs = slice(h * 128, (h + 1) * 128)
                        wq, wqT = wload(w_in[l, 0, :, :, hs], 8, 128)
                        wqs, wqsT = wload(w_in[l, 14, :, :, hs], 8, 128)
                        for (wa, waT, wsw, wswT, dst, dstT, scl) in ((wq, wqT, wqs, wqsT, qrT, qrTT, 1.0),):
                            for bi, blk in enumerate(BLKS):
                                b0, nn = blk
                                b1, b2 = (0, 1) if bi % 2 == 0 else (2, 3)
                                proj_fm(wa, waT, 0, blk, b1)
                                proj_fm(wsw, wswT, 0, blk, b2)
                                t1, t1T = rt[0], rtT[0]
                                t2, t2T = rt[1], rtT[1]
                                OP("vector", "tensor_tensor", t1[:, 0:nn], PB[b1][:, 0:nn], cosT[:, b0:b0 + nn], op=ALU.mult, r=[PT[b1], tabT], w=[t1T])
                                OP("vector", "tensor_tensor", t2[:, 0:nn], PB[b2][:, 0:nn], sinT[:, b0:b0 + nn], op=ALU.mult, r=[PT[b2], tabT], w=[t2T])
                                OP("vector", "tensor_tensor", dst[:, b0:b0 + nn], t1[:, 0:nn], t2[:, 0:nn], op=ALU.add, r=[t1T, t2T], w=[dstT])
                        wk, wkT = wload(w_in[l, 1, :, :, hs], 8, 128)
                        wks, wksT = wload(w_in[l, 15, :, :, hs], 8, 128)
                        for bi, blk in enumerate(BLKS):
                            b0, nn = blk
                            b1, b2 = (0, 1) if bi % 2 == 0 else (2, 3)
                            proj_fm(wk, wkT, 0, blk, b1)
                            proj_fm(wks, wksT, 0, blk, b2)
                            t1, t1T = rt[0], rtT[0]
                            t2, t2T = rt[1], rtT[1]
                            OP("vector", "scalar_tensor_tensor", out=t1[:, 0:nn], in0=PB[b1][:, 0:nn], scalar=128.0 ** -0.5, in1=cosT[:, b0:b0 + nn],
                               op0=ALU.mult, op1=ALU.mult, r=[PT[b1], tabT], w=[t1T])
                            OP("vector", "scalar_tensor_tensor", out=t2[:, 0:nn], in0=PB[b2][:, 0:nn], scalar=128.0 ** -0.5, in1=sinT[:, b0:b0 + nn],
                               op0=ALU.mult, op1=ALU.mult, r=[PT[b2], tabT], w=[t2T])
                            OP("vector", "tensor_tensor", krT[:, b0:b0 + nn], t1[:, 0:nn], t2[:, 0:nn], op=ALU.add, r=[t1T, t2T], w=[krTT])
                        OP("vector", "tensor_tensor", q2T[:, 0:2048].rearrange("p (c i) -> p c i", i=128),
                           qrT[:, 0:2048].rearrange("p (c i) -> p c i", i=128),
                           qdec[:, h, :].unsqueeze(1).to_broadcast([128, 16, 128]), op=ALU.mult, r=[qrTT, tabT], w=[q2TT])
                        OP("vector", "tensor_tensor", q2T[:, SS:NT], qrT[:, SS:NT], qdec[:, 4 + h, :], op=ALU.mult, r=[qrTT, tabT], w=[q2TT])
                        wv, wvT = wload(w_in[l, 2, :, :, hs], 8, 128)
                        wgt, wgtT = wload(w_in[l, 3, :, :, hs], 8, 128)
                        OP("vector", "memset", Sf[:], 0.0, w=[SfT])
                        OP("vector", "memset", Sb2[1][:], 0.0, w=[Sb2T[1]])
                        DMA("sync", SSf[:], sret[l, :, h].rearrange("b d e -> d b e"), w=[SSfT])
                        DMA("gpsimd", SSb[:], sret[l, :, h].rearrange("b d e -> d b e"), w=[SSbT])
                        pend_tr = []
                        for c in range(NTT):
                            p = c % 2
                            cs = slice(c * 128, (c + 1) * 128)
                            samp = (c == 16)
                            dsel = 4 + h if samp else h
                            Xsc, Xv, Xg, Xo = PB[p][:, 0:128], PB[2 + p][:, 0:128], PB[2 + p][:, 128:256], PB[4 + p][:, 0:128]
                            yv = PB[6].bitcast(BF16)
                            Yd = PB[7][:, 0:128]
                            TXsc = [PT[0], PT[1]]; TXv = [PT[2], PT[3]]; TXo = [PT[4], PT[5]]
                            TYk = [PT[6], PT[6]]; TYo = [PT[6], PT[6]]; TYd = [PT[7], PT[7]]
                            vb, vbT, sg, sgT, scb, scbT = vb2[p], vb2T[p], sg2[p], sg2T[p], scb2[p], scb2T[p]
                            kdb, kdbT, on, onT, orow, orowT, gs, gsT = kdb2[p], kdb2T[p], on2[p], on2T[p], orow2[p], orow2T[p], gs2[p], gs2T[p]
                            Sbp, SbpT = Sb2[1 - p], Sb2T[1 - p]
                            Sbn, SbnT = Sb2[p], Sb2T[p]
                            MM(Xsc, krT[:, cs], qrT[:, cs], True, True, r=[krTT, qrTT], w=[TXsc[p]])
                            OP("vector", "tensor_tensor", scb[:], Xsc, dec[:, dsel, :], op=ALU.mult, r=[TXsc[p], tabT], w=[scbT])
                            for kc in range(8):
                                MM(Xv, FA[:, kc, cs], wv[:, kc, :], kc == 0, kc == 7, r=[FAT, wvT], w=[TXv[p]])
                            for kc in range(8):
                                MM(Xg, FA[:, kc, cs], wgt[:, kc, :], kc == 0, kc == 7, r=[FAT, wgtT], w=[TXv[p]])
                            ACT(vb[:], Xv, AF.Copy, r=[TXv[p]], w=[vbT])
                            ACT(sg[:], Xg, AF.Silu, r=[TXv[p]], w=[sgT])
                            TR(yv[:, 0:128], krT[:, cs], identb[:], r=[krTT, idT], w=[TYk[p]])
                            OP("vector", "tensor_scalar", kdb[:], yv[:, 0:128], kdec[:, dsel:dsel + 1], None, op0=ALU.mult, r=[TYk[p], tabT], w=[kdbT])
                            if not samp:
                                MM(Yd, kdb[:], vb[:], True, True, r=[kdbT, vbT], w=[TYd[p]])
                                MM(Xo, scb[:], vb[:], True, False, r=[scbT, vbT], w=[TXo[p]])
                                MM(Xo, q2T[:, cs], Sbp[:], False, True, r=[q2TT, SbpT], w=[TXo[p]])
                                OP("vector", "scalar_tensor_tensor", out=Sbn[:], in0=Sf[:], scalar=GAM[h] ** 128, in1=Yd,
                                   op0=ALU.mult, op1=ALU.add, r=[SfT, TYd[p]], w=[SbnT])
                                OP("vector", "scalar_tensor_tensor", out=Sf[:], in0=Sf[:], scalar=GAM[h] ** 128, in1=Yd,
                                   op0=ALU.mult, op1=ALU.add, r=[SfT, TYd[p]], w=[SfT])
                                if c == 15:
                                    DMA("sync", rets[l, 0, h], Sf[:], r=[SfT])
                            else:
                                OP("vector", "tensor_tensor", Q2p[:], q2T[:, cs].unsqueeze(1).to_broadcast([128, 16, 128]), bmask[:],
                                   op=ALU.mult, r=[q2TT, tabT], w=[Q2pT])
                                MM(Xo, scb[:], vb[:], True, False, r=[scbT, vbT], w=[TXo[p]])
                                for b in range(16):
                                    MM(Xo, Q2p[:, b, :], SSb[:, b, :], False, b == 15, r=[Q2pT, SSbT], w=[TXo[p]])
                                OP("vector", "tensor_tensor", kdp[:], kdb[:].unsqueeze(1).to_broadcast([128, 16, 128]),
                                   rmask[:].unsqueeze(2).to_broadcast([128, 16, 128]), op=ALU.mult, r=[kdbT, tabT], w=[kdpT])
                                for g4 in range(4):
                                    bank = 7
                                    for bb_ in range(4):
                                        b = g4 * 4 + bb_
                                        MM(PB[bank][:, bb_ * 128:(bb_ + 1) * 128], kdp[:, b, :], vb[:], True, True, r=[kdpT, vbT], w=[PT[bank]])
                                    OP("vector", "scalar_tensor_tensor", out=SSf[:, g4 * 4:(g4 + 1) * 4, :], in0=SSf[:, g4 * 4:(g4 + 1) * 4, :],
                                       scalar=GAM[h] ** 4, in1=PB[bank].rearrange("p (b e) -> p b e", e=128),
                                       op0=ALU.mult, op1=ALU.add, r=[SSfT, PT[bank]], w=[SSfT])
                                DMA("sync", rets[l, 1:17, h].rearrange("b d e -> d b e"), SSf[:], r=[SSfT])
                            if len(pend_tr) > 0:
                                po, poT, pcs = pend_tr.pop(0)
                                TR(yv[:, 128:256], po[:], identb[:], r=[poT, idT], w=[PT[6]])
                                ACT(OT[:, h, pcs], yv[:, 128:256], AF.Copy, r=[PT[6]], w=[OTT])
                            ACT(on[:], Xo, AF.Square, accum_out=gs[:, 1:2], r=[TXo[p]], w=[onT, gsT])
                            OP("vector", "reduce_sum", gs[:, 0:1], Xo, axis=AX.X, r=[TXo[p], onT], w=[gsT])
                            OP("vector", "tensor_scalar", gs[:, 2:3], gs[:, 0:1], 1.0 / 128, None, op0=ALU.mult, r=[gsT], w=[gsT])
                            OP("vector", "tensor_tensor", gs[:, 3:4], gs[:, 2:3], gs[:, 2:3], op=ALU.mult, r=[gsT], w=[gsT])
                            OP("vector", "scalar_tensor_tensor", out=gs[:, 4:5], in0=gs[:, 1:2], scalar=1.0 / 128, in1=gs[:, 3:4],
                               op0=ALU.mult, op1=ALU.subtract, r=[gsT], w=[gsT])
                            ACT(gs[:, 5:6], gs[:, 4:5], AF.Sqrt, bias=eps_t[:, 0:1], r=[gsT, cT], w=[gsT])
                            OP("vector", "reciprocal", gs[:, 6:7], gs[:, 5:6], r=[gsT], w=[gsT])
                            OP("vector", "scalar_tensor_tensor", out=gs[:, 7:8], in0=gs[:, 2:3], scalar=-1.0, in1=gs[:, 6:7],
                               op0=ALU.mult, op1=ALU.mult, r=[gsT], w=[gsT])
                            ACT(on[:], Xo, AF.Identity, scale=gs[:, 6:7], bias=gs[:, 7:8], r=[TXo[p], gsT], w=[onT])
                            OP("vector", "tensor_tensor", on[:], on[:], gnB[:, hs], op=ALU.mult, r=[onT, tabT], w=[onT])
                            OP("vector", "tensor_tensor", orow[:], on[:], sg[:], op=ALU.mult, r=[onT, sgT], w=[orowT])
                            pend_tr.append((orow, orowT, cs))
                        while pend_tr:
                            po, poT, pcs = pend_tr.pop(0)
                            TR(yv[:, 128:256], po[:], identb[:], r=[poT, idT], w=[PT[6]])
                            ACT(OT[:, h, pcs], yv[:, 128:256], AF.Copy, r=[PT[6]], w=[OTT])
                    merge(0)
                    S.barrier()
                    if stop == "ret":
                        S.emit(); return nc

                bst = ExitStack()
                with bst:
                    B1 = sb(bst, "B1", [128, 2052], F32); B1T = T("B1")
                    B2 = sb(bst, "B2", [128, 2052], F32); B2T = T("B2")
                    Ab = sb(bst, "Ab", [128, 2048], BF16); AbT = T("Ab")
                    Apad = sb(bst, "Apad", [16, 16, 64], BF16); ApadT = T("Apad")
                    AT_ = sb(bst, "AT", [128, 17, 128], BF16); ATT = T("AT")
                    bb = sb(bst, "bb", [128, 32], F32); bbT = T("bb")
                    bb16 = sb(bst, "bb16", [16, 4], F32); bb16T = T("bb16")
                    sqS = sb(bst, "sqS", [128, 4, 128], BF16)
                    skS = sb(bst, "skS", [128, 4, 128], BF16)
                    svS = sb(bst, "svS", [128, 512], BF16)
                    sqST = T("sqS")
                    DMA("sync", bb[:, 0:16], v_sbb[l].partition_broadcast(128), w=[bbT])
                    DMA("sync", bb16[:, 0:1], v_sbb16[l], w=[bb16T])

                    def sb_core(h, qap, qT_, nk_past, keyT, keyTT, vfn, out_ap):
                        rows = 128
                        nkb = nk_past // 128
                        for kb in range(0, nk_past, 512):
                            n = min(512, nk_past - kb)
                            last_blk = (kb + n == nk_past)
                            MM(pz[0:rows, kb:kb + n], qap, keyT[:, kb:kb + n], True, not last_blk, r=[qT_, keyTT], w=PT[0:5])
                            if last_blk:
                                MM(pz[0:rows, nk_past - 128:nk_past], identb[:], mneg[:], False, True, r=[idT, cT], w=PT[0:5])
                        nk = nk_past
                        ACT(B1[0:rows, 0:nk], pz[0:rows, 0:nk], AF.Exp, bias=bb[0:rows, h:h + 1], r=PT[0:5] + [bbT], w=[B1T])
                        ACT(B1[0:rows, 0:nk], B1[0:rows, 0:nk], AF.Ln, bias=1.0, r=[B1T], w=[B1T])
                        OP("vector", "tensor_tensor_scan", out=B2[0:rows, 0:nk], data0=ones[0:rows, 0:1].to_broadcast([rows, nk]), data1=B1[0:rows, 0:nk], initial=0.0,
                           op0=ALU.mult, op1=ALU.add, r=[cT, B1T], w=[B2T])
                        OP("vector", "tensor_tensor", bb[0:rows, 16:17], bb[0:rows, h:h + 1], B2[0:rows, nk - 1:nk], op=ALU.subtract, r=[bbT, B2T], w=[bbT])
                        OP("vector", "tensor_tensor", B1[0:rows, 0:nk], B2[0:rows, 0:nk], B1[0:rows, 0:nk], op=ALU.subtract, r=[B1T, B2T], w=[B1T])
                        OP("vector", "tensor_tensor", B1[0:rows, 0:nk], B1[0:rows, 0:nk], pz[0:rows, 0:nk], op=ALU.add, r=[B1T] + PT[0:5], w=[B1T])
                        ACT(Ab[0:rows, 0:nk_past], B1[0:rows, 0:nk_past], AF.Exp, bias=bb[0:rows, 16:17], r=[B1T, bbT], w=[AbT])
                        for g in range(0, nkb, 8):
                            bank = 5 + ((g // 8) % 2)
                            pv = PB[bank].bitcast(BF16)
                            ng = min(8, nkb - g)
                            for k in range(ng):
                                TR(pv[:, k * 128:k * 128 + rows], Ab[0:rows, (g + k) * 128:(g + k + 1) * 128], identb[0:rows, 0:rows],
                                   r=[AbT, idT], w=[PT[bank]])
                            OP("vector", "tensor_copy", AT_[:, g:g + ng, 0:rows], pv[:, 0:ng * 128].rearrange("p (k n) -> p k n", n=128)[:, :, 0:rows],
                               r=[PT[bank]], w=[ATT])
                        for kt in range(nkb):
                            vap, vT = vfn(kt)
                            MM(PB[7][:, 0:rows], vap, AT_[:, kt, 0:rows], kt == 0, kt == nkb - 1, r=[vT, ATT], w=[PT[7]])
                        ACT(out_ap, PB[7][:, 0:rows], AF.Copy, r=[PT[7]], w=[OTT])

                    pst = ExitStack()
                    with pst:
                        sqT = sb(pst, "sqT", [128, NT], BF16); sqTT = T("sqT")
                        skT = sb(pst, "skT", [128, NT], BF16); skTT = T("skT")
                        svtm = sb(pst, "svtm", [128, NTT, 512], BF16); svtmT = T("svtm")
                        kvo = [sb(pst, "kvo%d" % i, [128, 512], F32) for i in range(2)]
                        kvoT = [T("kvo%d" % i) for i in range(2)]
                        wkf, wkfT = wload(w_in[l, 5], 8, 512)
                        wvf, wvfT = wload(w_in[l, 6], 8, 512)
                        for tt in range(NTT):
                            ts_ = slice(tt * 128, (tt + 1) * 128)
                            proj_tm(wkf, wkfT, 512, tt, 0)
                            OP("vector", "tensor_copy", kvo[0][:], PB[0], r=[PT[0]], w=[kvoT[0]])
                            DMA("sync", sbk[l, ts_, :], kvo[0][:], r=[kvoT[0]])
                            proj_tm(wvf, wvfT, 512, tt, 1)
                            OP("vector", "tensor_copy", kvo[1][:], PB[1], r=[PT[1]], w=[kvoT[1]])
                            OP("vector", "tensor_copy", svtm[:, tt, :], PB[1], r=[PT[1]], w=[svtmT])
                            DMA("sync", sbv[l, ts_, :], kvo[1][:], r=[kvoT[1]])
                        OP("vector", "tensor_copy", svS[:], svtm[:, 16, :], r=[svtmT], w=[sqST])
                        for h in range(4):
                            hs = slice(h * 128, (h + 1) * 128)
                            wq, wqT = wload(w_in[l, 4, :, :, hs], 8, 128)
                            wk, wkT = wload(w_in[l, 5, :, :, hs], 8, 128)
                            for bi, blk in enumerate(BLKS):
                                b0, nn = blk
                                b1, b2 = (0, 1) if bi % 2 == 0 else (2, 3)
                                proj_fm(wq, wqT, 0, blk, b1)
                                proj_fm(wk, wkT, 0, blk, b2)
                                ACT(sqT[:, b0:b0 + nn], PB[b1][:, 0:nn], AF.Copy, scale=128.0 ** -0.5, r=[PT[b1]], w=[sqTT])
                                OP("vector", "tensor_copy", skT[:, b0:b0 + nn], PB[b2][:, 0:nn], r=[PT[b2]], w=[skTT])
                            OP("vector", "tensor_copy", sqS[:, h, :], sqT[:, SS:NT], r=[sqTT], w=[sqST])
                            OP("vector", "tensor_copy", skS[:, h, :], skT[:, SS:NT], r=[skTT], w=[sqST])
                            for qb in range(16):
                                qc = slice(qb * 128, (qb + 1) * 128)
                                sb_core(h, sqT[:, qc], sqTT, (qb + 1) * 128, skT, skTT,
                                        lambda kt, h=h: (svtm[:, kt, h * 128:(h + 1) * 128], svtmT), OT[:, h, qc])
                        OP("vector", "memset", OT[:, :, SS:NT], 0.0, w=[OTT])
                        S.barrier()
                    if do_sample_sb:
                        sst = ExitStack()
                        with sst:
                            Kb = sb(sst, "Kb", [128, 16, 512], BF16); KbT = [T("Kb%d" % i) for i in range(16)]
                            Vb = sb(sst, "Vb", [128, 16, 512], BF16); VbT = [T("Vb%d" % i) for i in range(16)]
                            KT = [sb(sst, "KT%d" % i, [128, 2048], BF16) for i in range(2)]
                            KTT = [T("KT%d" % i) for i in range(2)]
                            qpad = [sb(sst, "qpad%d" % i, [128, 96], BF16) for i in range(2)]
                            qpadT = [T("qpad%d" % i) for i in range(2)]
                            E16 = sb(sst, "E16", [4, 16], BF16); E16T = T("E16")
                            OP("vector", "memset", Apad[:], 0.0, w=[ApadT])
                            for i in range(2):
                                OP("vector", "memset", qpad[i][:], 0.0, w=[qpadT[i]])
                            for hh_ in range(4):
                                OP("vector", "tensor_copy", E16[:, hh_ * 4:(hh_ + 1) * 4], identb[0:4, 0:4], r=[idT], w=[E16T])
                            for b in range(16):
                                ix, ixT = idx16[b % 2], idx16T[b % 2]
                                OP("vector", "tensor_scalar", ix[:], ptf[:, b * 16:(b + 1) * 16], float(l * 2560 * 128), None, op0=ALU.add,
                                   r=[ptT], w=[ixT])
                                for pg in range(16):
                                    S.op("gpsimd", (lambda pg, ix: lambda hh: hh.indirect_dma_start(
                                        out=Kb[:, pg, :], out_offset=None, in_=ck,
                                        in_offset=bass.IndirectOffsetOnAxis(ap=ix[:, pg:pg + 1], axis=0)))(pg, ix),
                                        reads=[ixT], writes=[KbT[pg]], dma=True)
                                for pg in range(16):
                                    S.op("gpsimd", (lambda pg, ix: lambda hh: hh.indirect_dma_start(
                                        out=Vb[:, pg, :], out_offset=None, in_=cv,
                                        in_offset=bass.IndirectOffsetOnAxis(ap=ix[:, pg:pg + 1], axis=0)))(pg, ix),
                                        reads=[ixT], writes=[VbT[pg]], dma=True)
                                qp, qpT = qpad[b % 2], qpadT[b % 2]
                                OP("vector", "tensor_copy", qp[:, 0:80].rearrange("p (h x) -> p h x", x=20)[:, :, 0:4], sqS[:, :, b * 4:b * 4 + 4],
                                   r=[sqST], w=[qpT])
                                for h in range(4):
                                    kt_, ktT = KT[h % 2], KTT[h % 2]
                                    for g in range(0, 16, 8):
                                        bank = 5 + ((g // 8) % 2)
                                        pv = PB[bank].bitcast(BF16)
                                        for k in range(8):
                                            TR(pv[:, k * 128:(k + 1) * 128], Kb[:, g + k, h * 128:(h + 1) * 128], identb[:], r=[KbT[g + k], idT], w=[PT[bank]])
                                        OP("vector", "tensor_copy", kt_[:, g * 128:(g + 8) * 128], pv, r=[PT[bank]], w=[ktT])
                                    for kb in range(0, 2048, 512):
                                        MM(pz[0:16, kb:kb + 512], qp[:, h * 16:(h + 1) * 16], kt_[:, kb:kb + 512], h == 0, h == 3, r=[qpT, ktT], w=PT[0:5])
                                    MM(pz[0:16, 2048:2052], qp[:, h * 16:(h + 1) * 16], skS[:, h, b * 4:b * 4 + 4], h == 0, False, r=[qpT, sqST], w=PT[0:5])
                                MM(pz[0:16, 2048:2052], E16[:], mneg[0:4, 0:4], False, True, r=[E16T, cT], w=PT[0:5])
                                nk = 2052
                                ACT(B1[0:16, 0:nk], pz[0:16, 0:nk], AF.Exp, bias=bb16[:, 0:1], r=PT[0:5] + [bb16T], w=[B1T])
                                ACT(B1[0:16, 0:nk], B1[0:16, 0:nk], AF.Ln, bias=1.0, r=[B1T], w=[B1T])
                                OP("vector", "tensor_tensor_scan", out=B2[0:16, 0:nk], data0=ones[0:16, 0:1].to_broadcast([16, nk]), data1=B1[0:16, 0:nk], initial=0.0,
                                   op0=ALU.mult, op1=ALU.add, r=[cT, B1T], w=[B2T])
                                OP("vector", "tensor_tensor", bb16[:, 1:2], bb16[:, 0:1], B2[0:16, nk - 1:nk], op=ALU.subtract, r=[bb16T, B2T], w=[bb16T])
                                OP("vector", "tensor_tensor", B1[0:16, 0:nk], B2[0:16, 0:nk], B1[0:16, 0:nk], op=ALU.subtract, r=[B1T, B2T], w=[B1T])
                                OP("vector", "tensor_tensor", B1[0:16, 0:nk], B1[0:16, 0:nk], pz[0:16, 0:nk], op=ALU.add, r=[B1T] + PT[0:5], w=[B1T])
                                ACT(Ab[0:16, 0:2048], B1[0:16, 0:2048], AF.Exp, bias=bb16[:, 1:2], r=[B1T, bb16T], w=[AbT])
                                ACT(Apad[0:16, b, b * 4:b * 4 + 4], B1[0:16, 2048:2052], AF.Exp, bias=bb16[:, 1:2], r=[B1T, bb16T], w=[ApadT])
                                for g in range(0, 16, 8):
                                    bank = 5 + ((g // 8) % 2)
                                    pv = PB[bank].bitcast(BF16)
                                    for k in range(8):
                                        TR(pv[:, k * 128:k * 128 + 16], Ab[0:16, (g + k) * 128:(g + k + 1) * 128], identb[0:16, 0:16], r=[AbT, idT], w=[PT[bank]])
                                    OP("vector", "tensor_copy", AT_[:, g:g + 8, 0:16], pv.rearrange("p (k n) -> p k n", n=128)[:, :, 0:16], r=[PT[bank]], w=[ATT])
                                pv = PB[5].bitcast(BF16)
                                TR(pv[0:64, 0:16], Apad[0:16, b, :], identb[0:16, 0:16], r=[ApadT, idT], w=[PT[5]])
                                OP("vector", "tensor_copy", AT_[0:64, 16, 0:16], pv[0:64, 0:16], r=[PT[5]], w=[ATT])
                                for h in range(4):
                                    for kt in range(16):
                                        MM(PB[7][:, h * 4:(h + 1) * 4], Vb[:, kt, h * 128:(h + 1) * 128], AT_[:, kt, h * 4:(h + 1) * 4], kt == 0, False,
                                           r=[VbT[kt], ATT], w=[PT[7]])
                                    MM(PB[7][:, h * 4:(h + 1) * 4], svS[0:64, h * 128:(h + 1) * 128], AT_[0:64, 16, h * 4:(h + 1) * 4], False, True,
                                       r=[sqST, ATT], w=[PT[7]])
                                ACT(OT[:, :, SS + b * 4:SS + b * 4 + 4], PB[7][:, 0:16].rearrange("p (h t) -> p h t", t=4), AF.Copy, r=[PT[7]], w=[OTT])
                            S.barrier()
                    merge(1)
                    S.barrier()
                    if stop == "sb":
                        S.emit(); return nc

                for ch in range(2):
                    wo, woT = wload(w_out[l, ch], 8, 512)
                    for tt in range(NTT):
                        ts_ = slice(tt * 128, (tt + 1) * 128)
                        bank = tt % 4
                        xt, xtT = xio[tt % 2], xioT[tt % 2]
                        DMA("sync", xt[:, 0:512], xsrc[ts_, ch * 512:(ch + 1) * 512], r=([xsrcT[tt]] if xsrcT else []), w=[xtT])
                        proj_tm(wo, woT, 512, tt, bank, src=FB, srcT=FBT)
                        OP("vector", "tensor_tensor", xt[:, 0:512], xt[:, 0:512], PB[bank], op=ALU.add, r=[xtT, PT[bank]], w=[xtT])
                        DMA("sync", xA[l][ts_, ch * 512:(ch + 1) * 512], xt[:, 0:512], r=[xtT], w=[xAT[l][tt]])
                S.barrier()
                if stop == "wout":
                    S.emit(); return nc

            fst = ExitStack()
            with fst:
                X = sb(fst, "X", [128, NTT, D], F32)
                XT = [T("X%d" % i) for i in range(NTT)]
                GE2 = [sb(fst, "GE%d" % i, [128, 2 + 2048], F32) for i in range(2)]; GE2T = [T("GE") for i in range(2)]
                GS2 = [sb(fst, "GS%d" % i, [128, 16, 6], F32) for i in range(2)]; GS2T = [T("GS") for i in range(2)]
                GC2 = [sb(fst, "GC%d" % i, [128, NT], F32) for i in range(2)]; GC2T = [T("GC") for i in range(2)]
                HT = sb(fst, "HT", [128, 2, NT], BF16); HTT = T("HT")
                sfc_tm = sb(fst, "sfc_tm", [32, DFF], F32); sfcT = T("sfc")
                fv = sb(fst, "fv", [128, 22, 4], F32); fvT = T("fv")
                gtm = sb(fst, "gtm", [128, 256], F32); gtmT = T("gtm")
                DMA("sync", sfc_tm[:], sfc[l], w=[sfcT])
                DMA("sync", fv[:], v_ffn[l], w=[fvT])
                for i_ in range(2):
                    OP("vector", "memset", GE2[i_][:, 0:2], 0.0, w=[GE2T[i_]])
                    OP("vector", "memset", GC2[i_][:, 2112:NT], 0.0, w=[GC2T[i_]])
                DMA("sync", gB[:], v_norm[2 * l + 1].partition_broadcast(128), w=[gBT])
                for tt in range(NTT):
                    DMA("sync", X[:, tt, :], xA[l][tt * 128:(tt + 1) * 128, :], r=[xAT[l][tt]], w=[XT[tt]])
                    rmsnorm_stats(X[:, tt, :], XT[tt], D)
                    nb, nbT = xnb[tt % 2], xnbT[tt % 2]
                    OP("vector", "scalar_tensor_tensor", out=nb[:], in0=X[:, tt, :], scalar=st4[:, 2:3], in1=gB[:],
                       op0=ALU.mult, op1=ALU.mult, r=[XT[tt], st4T, gBT], w=[nbT])
                    bank = 5 + (tt % 2)
                    pv = PB[bank].bitcast(BF16)
                    for kc in range(8):
                        TR(pv[:, kc * 128:(kc + 1) * 128], nb[:, kc * 128:(kc + 1) * 128], identb[:], r=[nbT, idT], w=[PT[bank]])
                    ACT(FB[:, :, tt * 128:(tt + 1) * 128], pv.rearrange("p (k n) -> p k n", n=128), AF.Copy, r=[PT[bank]], w=[FBT])
                for gi in range(NG):
                    wg, wgT = wload(w_gate[l, gi], 8, 256)
                    wu, wuT = wload(w_up[l, gi], 8, 256)
                    wd, wdT = wload(w_down[l, gi], 2, 1024)
                    for (tt, kind) in ((15, "p"), (16, "s")):
                        proj_tm(wg, wgT, 256, tt, 4, src=FB, srcT=FBT)
                        OP("vector", "tensor_copy", gtm[:], PB[4][:, 0:256], r=[PT[4]], w=[gtmT])
                        if kind == "p":
                            DMA("sync", ffnc[l, 0, :, gi * 256:(gi + 1) * 256], gtm[126:128, :], r=[gtmT])
                        else:
                            for i in range(2):
                                DMA("sync", ffnc[l, 1:17, i, gi * 256:(gi + 1) * 256], gtm[2 + i:64:4, :], r=[gtmT])
                    for c2 in range(2):
                        gc = gi * 2 + c2
                        GE, GET, GS, GST, GC, GCT = GE2[c2], GE2T[c2], GS2[c2], GS2T[c2], GC2[c2], GC2T[c2]
                        TR(PB[4][:, 256:288], sfc_tm[0:32, gc * 128:(gc + 1) * 128], identf[0:32, 0:32], r=[sfcT, idT], w=[PT[4]])
                        OP("vector", "tensor_copy", GS[:, :, 0:2], PB[4][:, 256:288].rearrange("p (b i) -> p b i", i=2), r=[PT[4]], w=[GST])
                        for bi, blk in enumerate(BLKS):
                            b0, nn = blk
                            bank = bi % 2
                            proj_fm(wg, wgT, c2 * 128, blk, bank, src=FB, srcT=FBT)
                            if bi < 4:
                                ACT(GE[:, 2 + b0:2 + b0 + nn], PB[bank][:, 0:nn], AF.Copy, r=[PT[bank]], w=[GET])
                            else:
                                ACT(GS[:, :, 2:6], PB[bank][:, 0:64].rearrange("p (b t) -> p b t", t=4), AF.Copy, r=[PT[bank]], w=[GST])
                    for c2 in range(2):
                        gc = gi * 2 + c2
                        GE, GET, GS, GST, GC, GCT = GE2[c2], GE2T[c2], GS2[c2], GS2T[c2], GC2[c2], GC2T[c2]
                        GCs = GC[:, SS:SS + 64].rearrange("p (b t) -> p b t", t=4)
                        OP("vector", "tensor_scalar", GC[:, 0:2048], GE[:, 0:2048], fv[:, gc, 0:1], fv[:, gc, 3:4], op0=ALU.mult, op1=ALU.add,
                           r=[GET, fvT], w=[GCT])
                        OP("vector", "tensor_scalar", GCs, GS[:, :, 0:4], fv[:, gc, 0:1], fv[:, gc, 3:4], op0=ALU.mult, op1=ALU.add,
                           r=[GST, fvT], w=[GCT])
                        for i in range(1, 3):
                            OP("vector", "scalar_tensor_tensor", out=GC[:, 0:2048], in0=GE[:, i:i + 2048], scalar=fv[:, gc, i:i + 1], in1=GC[:, 0:2048],
                               op0=ALU.mult, op1=ALU.add, r=[GET, fvT, GCT], w=[GCT])
                            OP("vector", "scalar_tensor_tensor", out=GCs, in0=GS[:, :, i:i + 4], scalar=fv[:, gc, i:i + 1], in1=GCs,
                               op0=ALU.mult, op1=ALU.add, r=[GST, fvT, GCT], w=[GCT])
                        ACT(GC[:], GC[:], AF.Gelu_apprx_tanh, r=[GCT], w=[GCT])
                    for c2 in range(2):
                        GC, GCT = GC2[c2], GC2T[c2]
                        for bi, blk in enumerate(BLKS):
                            b0, nn = blk
                            bank = 2 + (c2 * 5 + bi) % 4
                            proj_fm(wu, wuT, c2 * 128, blk, bank, src=FB, srcT=FBT)
                            OP("vector", "tensor_tensor", HT[:, c2, b0:b0 + nn], GC[:, b0:b0 + nn], PB[bank][:, 0:nn], op=ALU.mult,
                               r=[GCT, PT[bank]], w=[HTT])
                    for tt in range(NTT):
                        for ch in range(2):
                            bank = 6 + (tt * 2 + ch) % 2
                            for c2 in range(2):
                                MM(PB[bank], HT[:, c2, tt * 128:(tt + 1) * 128], wd[:, c2, ch * 512:(ch + 1) * 512], c2 == 0, c2 == 1,
                                   r=[HTT, wdT], w=[PT[bank]])
                            OP("vector", "tensor_tensor", X[:, tt, ch * 512:(ch + 1) * 512], X[:, tt, ch * 512:(ch + 1) * 512], PB[bank], op=ALU.add,
                               r=[XT[tt], PT[bank]], w=[XT[tt]])
                if l < L - 1:
                    for tt in range(NTT):
                        DMA("sync", xB[l][tt * 128:(tt + 1) * 128, :], X[:, tt, :], r=[XT[tt]], w=[xBT[l][tt]])
                else:
                    DMA("sync", gB[:], v_norm[2 * L].partition_broadcast(128), w=[gBT])
                    for tt in range(NTT):
                        rmsnorm_stats(X[:, tt, :], XT[tt], D)
                        xt, xtT = xio[tt % 2], xioT[tt % 2]
                        OP("vector", "scalar_tensor_tensor", out=xt[:], in0=X[:, tt, :], scalar=st4[:, 2:3], in1=gB[:],
                           op0=ALU.mult, op1=ALU.mult, r=[XT[tt], st4T, gBT], w=[xtT])
                        DMA("sync", y[tt * 128:(tt + 1) * 128, :], xt[:], r=[xtT])
                S.barrier()
        S.emit()
    return nc


def _consts():
    pos = np.concatenate([np.arange(2048), np.tile(2048 + np.arange(4), 16), np.zeros(64)]).astype(np.float32)
    half = 64
    inv = (np.float32(10000.0) ** (-np.arange(half, dtype=np.float32) / np.float32(half))).astype(np.float32)
    ang = (pos[None, :] * inv[:, None]).astype(np.float32)
    cos = np.cos(ang).astype(np.float32)
    sin = np.sin(ang).astype(np.float32)
    c_cos = np.concatenate([cos, cos], 0)
    c_sin = np.concatenate([-sin, sin], 0)
    lg = np.log(np.array(GAM, dtype=np.float64))
    idx = np.arange(128)
    dec = np.zeros((128, 8, 128), np.float64)
    qdec = np.zeros((128, 8, 128), np.float64)
    kdec = np.zeros((128, 8), np.float64)
    for h in range(4):
        diff = idx[None, :] - idx[:, None]
        dec[:, h, :] = np.where(diff >= 0, np.exp(np.maximum(diff, 0) * lg[h]), 0.0)
        qdec[:, h, :] = np.exp((idx + 1.0) * lg[h])[None, :]
        kdec[:, h] = np.exp((127.0 - idx) * lg[h])
        for j in range(64):
            for i in range(64):
                if j // 4 == i // 4 and i >= j:
                    dec[j, 4 + h, i] = np.exp((i - j) * lg[h])
        t = idx % 4
        qd = np.exp((t + 1.0) * lg[h]); qd[64:] = 0.0
        qdec[:, 4 + h, :] = qd[None, :]
        kd = np.exp((3.0 - t) * lg[h]); kd[64:] = 0.0
        kdec[:, 4 + h] = kd
    mneg = np.where(idx[None, :] < idx[:, None], 0.0, -30000.0)
    bmask = np.zeros((128, 16, 128), np.float32)
    rmask = np.zeros((128, 16), np.float32)
    for b in range(16):
        bmask[:, b, b * 4:b * 4 + 4] = 1.0
        rmask[b * 4:b * 4 + 4, b] = 1.0
    return dict(c_cos=c_cos, c_sin=c_sin, c_dec=dec.astype(np.float32), c_qdec=qdec.astype(np.float32),
                c_kdec=kdec.astype(np.float32), c_mneg=mneg.astype(np.float32), c_bmask=bmask, c_rmask=rmask,
                c_iota=np.arange(128, dtype=np.float32).reshape(128, 1))


def _blk(w, kc, ncol):
    K, N = w.shape
    return np.ascontiguousarray(w.reshape(kc, 128, N // ncol, ncol).transpose(2, 1, 0, 3))


def _prep(x_prompt, x_sample, cache_sb_k, cache_sb_v, state_ret, state_lru_h, state_lru_conv,
           state_ffn_conv, page_table, norm1, w_in, ret_gn, sb_bias, lru_conv_w, lru_conv_b, lru_w_a, lru_b_a,
           lru_w_x, lru_b_x, lru_lambda, w_br_ret, w_br_sb, w_br_lru, w_out, norm2, w_ffn_gate,
           w_ffn_up, ffn_conv_w, ffn_conv_b, w_ffn_down, norm_f, big_cache=True):
    f = lambda a: np.asarray(a, dtype=np.float32)
    x_prompt, x_sample = f(x_prompt), f(x_sample)
    w_in = f(w_in)
    shared = _consts()
    perm = np.concatenate([np.arange(h * 128, (h + 1) * 128).reshape(2, 64)[::-1].reshape(-1) for h in range(4)])
    w_in_ext = np.concatenate([w_in, w_in[:, :, 0:512][:, :, perm], w_in[:, :, 512:1024][:, :, perm]], axis=2)
    shared["w_in"] = np.stack([_blk(w_in_ext[l], 8, 512) for l in range(L)])
    wbr = np.stack([np.stack([f(w)[l].reshape(4, 128, 1024).transpose(1, 0, 2) for w in (w_br_ret, w_br_sb, w_br_lru)]) for l in range(L)])
    shared["w_br"] = np.ascontiguousarray(wbr)
    shared["w_out"] = np.stack([_blk(f(w_out)[l], 8, 512) for l in range(L)])
    shared["w_gate"] = np.stack([_blk(f(w_ffn_gate)[l], 8, 256) for l in range(L)])
    shared["w_up"] = np.stack([_blk(f(w_ffn_up)[l], 8, 256) for l in range(L)])
    shared["w_down"] = np.ascontiguousarray(f(w_ffn_down).reshape(L, NG, 2, 128, 1024).transpose(0, 1, 3, 2, 4))
    wl = np.concatenate([f(lru_w_a), f(lru_w_x)], axis=1)
    shared["w_lru"] = np.ascontiguousarray(wl.transpose(0, 2, 1, 3))
    shared["v_norm"] = np.concatenate([np.stack([f(norm1)[l], f(norm2)[l]]) for l in range(L)] + [f(norm_f)[None]], 0).reshape(2 * L + 1, 1, D)
    shared["v_gn"] = f(ret_gn).reshape(L, 1, 512)
    shared["v_sbb"] = np.ascontiguousarray(np.concatenate([f(sb_bias), np.zeros((L, 12), np.float32)], axis=1).reshape(L, 1, 16))
    vl = np.concatenate([f(lru_conv_w), f(lru_conv_b)[:, None], f(lru_b_a)[:, None], f(lru_b_x)[:, None], f(lru_lambda)[:, None]], axis=1)
    shared["v_lru"] = np.ascontiguousarray(vl.reshape(L, 8, 4, 128).transpose(0, 3, 2, 1))
    vf = np.concatenate([f(ffn_conv_w), f(ffn_conv_b)[:, None]], axis=1)
    shared["v_ffn"] = np.ascontiguousarray(vf.reshape(L, 4, 22, 128).transpose(0, 3, 2, 1))
    if not big_cache:
        shared["ck"] = np.zeros((128, 512), np.float32)
        shared["cv"] = np.zeros((128, 512), np.float32)
    else:
      shared["ck"] = f(cache_sb_k).reshape(L * 2560 * 128, 512)
      shared["cv"] = f(cache_sb_v).reshape(L * 2560 * 128, 512)
    shared["v_sbb16"] = np.ascontiguousarray(np.repeat(f(sb_bias), 4, axis=1).reshape(L, 16, 1))
    in_maps = []
    for c in range(NCORES):
        bs = slice(16 * c, 16 * c + 16)
        xin = np.zeros((NT, D), np.float32)
        xin[0:2048] = x_prompt[c]
        xin[2048:2112] = x_sample[bs].reshape(64, D)
        m = dict(shared)
        m["xin"] = xin
        m["pt"] = np.ascontiguousarray(np.asarray(page_table)[bs].astype(np.int32).reshape(1, 256))
        m["sret"] = np.ascontiguousarray(f(state_ret)[:, bs])
        m["slh"] = np.ascontiguousarray(f(state_lru_h)[:, bs])
        m["slc"] = np.ascontiguousarray(f(state_lru_conv)[:, bs].reshape(L, 48, 512))
        m["sfc"] = np.ascontiguousarray(f(state_ffn_conv)[:, bs].reshape(L, 32, DFF))
        in_maps.append(m)
    return in_maps


def _assemble(R):
    cat = lambda fn, ax: np.concatenate([fn(r) for r in R], axis=ax)
    y_prompt = np.stack([r["y"][0:2048] for r in R])
    y_sample = cat(lambda r: r["y"][2048:2112].reshape(16, 4, D), 0)
    sbk_p = np.stack([r["sbk"][:, 0:2048].reshape(L, 2048, 4, 128) for r in R], axis=1)
    sbv_p = np.stack([r["sbv"][:, 0:2048].reshape(L, 2048, 4, 128) for r in R], axis=1)
    sbk_s = cat(lambda r: r["sbk"][:, 2048:2112].reshape(L, 16, 4, 4, 128), 1)
    sbv_s = cat(lambda r: r["sbv"][:, 2048:2112].reshape(L, 16, 4, 4, 128), 1)
    ret_p = np.stack([r["rets"][:, 0] for r in R], axis=1)
    ret_s = cat(lambda r: r["rets"][:, 1:17], 1)
    lh_p = np.stack([r["lruh"][:, 0] for r in R], axis=1)
    lh_s = cat(lambda r: r["lruh"][:, 1:17], 1)
    lc_p = np.stack([r["lruc"][:, 0] for r in R], axis=1)
    lc_s = cat(lambda r: r["lruc"][:, 1:17], 1)
    fc_p = np.stack([r["ffnc"][:, 0] for r in R], axis=1)
    fc_s = cat(lambda r: r["ffnc"][:, 1:17], 1)
    outs = (y_prompt, y_sample, sbk_p, sbv_p, sbk_s, sbv_s, ret_p, ret_s, lh_p, lh_s, lc_p, lc_s, fc_p, fc_s)
    return tuple(np.ascontiguousarray(o, dtype=np.float32) for o in outs)


def kernel(**inputs):
    in_maps = _prep(**inputs)
    nc = build_nc()
    res = run_bass_kernel_spmd(nc, in_maps, core_ids=list(range(NCORES)))
    return _assemble(res.results)
```
